# Optimizing a Trainium2 kernel written in Bass

```python
import math
import jax, jax.numpy as jnp
from jax import lax
import numpy as np


D_MODEL = 1024
BATCH = 16
SEQ = 2048
DEPTH = 4
DEC_BATCH = 32
DEC_SEQ = 2048
PAST_LEN = 128

N_EVEN = (DEPTH + 1) // 2
N_ODD = DEPTH // 2
EPS = 1e-5

D_SSD = D_MODEL
SSD_HEAD_DIM = 64
SSD_HEADS = D_SSD // SSD_HEAD_DIM
SSD_GROUPS = 2
SSD_STATE = 64
SSD_CHUNK = 128
D_CONV = 5
CONV_CH = D_SSD + 2 * SSD_GROUPS * SSD_STATE

D_ATT = D_MODEL
DA_HEAD_DIM = 64
DA_HEADS = D_ATT // (2 * DA_HEAD_DIM)
Q_BLOCK = 128
N_BUCKETS = 32
MAX_DISTANCE = 128

IN_AB = D_SSD + CONV_CH + 2 * SSD_HEADS + 4 * D_ATT
D_AB = D_SSD + D_ATT

D_S5 = D_MODEL
S5_GROUP = 16
S5_GROUPS = D_S5 // S5_GROUP
S5_STATE = 64
IN_C = 2 * D_S5

kernel_name = 'hybrid_ssd_diffattn_s5_encoder'


def rmsnorm(x, w):
    xf = x.astype(jnp.float32)
    y = xf * lax.rsqrt(jnp.mean(xf * xf, axis=-1, keepdims=True) + EPS)
    return (y * w.astype(jnp.float32)).astype(x.dtype)


def rev(t):
    return jnp.flip(t, axis=1)


def depthwise_conv(x, w, b):
    c = x.shape[-1]
    y = lax.conv_general_dilated(x, w[:, None, :], window_strides=(1,),
                                 padding=[(D_CONV // 2, D_CONV // 2)],
                                 dimension_numbers=('NWC', 'WIO', 'NWC'),
                                 feature_group_count=c)
    return y + b


def ssd_scan(x, dt, A, Bm, Cm):
    b, L, H, P = x.shape
    G, N = Bm.shape[2], Bm.shape[3]
    R = H // G
    nc = L // SSD_CHUNK
    xdt = (x * dt[..., None]).reshape(b, nc, SSD_CHUNK, G, R, P)
    a = (dt * A).reshape(b, nc, SSD_CHUNK, G, R)
    Bc = Bm.reshape(b, nc, SSD_CHUNK, G, N)
    Cc = Cm.reshape(b, nc, SSD_CHUNK, G, N)
    a_cs = jnp.cumsum(a, axis=2)
    seg = a_cs[:, :, :, None] - a_cs[:, :, None]
    lower = jnp.tril(jnp.ones((SSD_CHUNK, SSD_CHUNK), dtype=bool))
    Lm = jnp.exp(jnp.where(lower[:, :, None, None], seg, -jnp.inf))
    cb = jnp.einsum('bcign,bcjgn->bcijg', Cc, Bc)
    y_diag = jnp.einsum('bcijg,bcijgr,bcjgrp->bcigrp', cb, Lm, xdt)
    decay_to_end = jnp.exp(a_cs[:, :, -1:] - a_cs)
    chunk_states = jnp.einsum('bcjgn,bcjgr,bcjgrp->bcgrpn', Bc, decay_to_end, xdt)
    chunk_decay = jnp.exp(a_cs[:, :, -1])

    def step(h, inp):
        s, d = inp
        return h * d[..., None, None] + s, h

    h0 = jnp.zeros_like(chunk_states[:, 0])
    _, h_prev = lax.scan(step, h0, (jnp.moveaxis(chunk_states, 1, 0), jnp.moveaxis(chunk_decay, 1, 0)))
    h_prev = jnp.moveaxis(h_prev, 0, 1)
    y_off = jnp.einsum('bcign,bcigr,bcgrpn->bcigrp', Cc, jnp.exp(a_cs), h_prev)
    return (y_diag + y_off).reshape(b, L, H, P)


def ssd_mixer(z, xbc, dt_raw, conv_w, conv_b, dt_bias, A_log, D_skip, norm_w):
    b, L, _ = z.shape
    GN = SSD_GROUPS * SSD_STATE
    xbc = jax.nn.silu(depthwise_conv(xbc, conv_w, conv_b))
    xs = xbc[..., :D_SSD].reshape(b, L, SSD_HEADS, SSD_HEAD_DIM)
    Bm = xbc[..., D_SSD:D_SSD + GN].reshape(b, L, SSD_GROUPS, SSD_STATE)
    Cm = xbc[..., D_SSD + GN:].reshape(b, L, SSD_GROUPS, SSD_STATE)
    dt_raw = dt_raw.reshape(b, L, 2, SSD_HEADS)
    y = xs * D_skip[:, None]
    for d in range(2):
        dt = jax.nn.softplus(dt_raw[:, :, d] + dt_bias[d])
        A = -jnp.exp(A_log[d])
        if d == 0:
            y = y + ssd_scan(xs, dt, A, Bm, Cm)
        else:
            y = y + rev(ssd_scan(rev(xs), rev(dt), A, rev(Bm), rev(Cm)))
    y = y.reshape(b, L, D_SSD)
    return rmsnorm(y * jax.nn.silu(z), norm_w)


def t5_bucket(rel):
    half = N_BUCKETS // 2
    max_exact = half // 2
    n = jnp.abs(rel)
    large = max_exact + (jnp.log(jnp.maximum(n, 1).astype(jnp.float32) / max_exact)
                         / math.log(MAX_DISTANCE / max_exact) * (half - max_exact)).astype(jnp.int32)
    large = jnp.minimum(large, half - 1)
    return jnp.where(rel > 0, half, 0) + jnp.where(n < max_exact, n, large)


def diff_attention(q, k, v, g, lam_qk, subln_w, rel_bias, lambda_init):
    b, L, _ = q.shape
    nb = L // Q_BLOCK
    scale = DA_HEAD_DIM ** -0.5
    qb_all = (q * scale).reshape(b, nb, Q_BLOCK, DA_HEADS, 2, DA_HEAD_DIM).transpose(1, 0, 2, 3, 4, 5)
    k = k.reshape(b, L, DA_HEADS, 2, DA_HEAD_DIM)
    v = v.reshape(b, L, DA_HEADS, 2 * DA_HEAD_DIM)
    lq = lam_qk.astype(jnp.float32)
    lam = jnp.exp(jnp.sum(lq[0] * lq[1])) - jnp.exp(jnp.sum(lq[2] * lq[3])) + lambda_init
    key_pos = jnp.arange(L)

    def block(args):
        i, qb = args
        s = jnp.einsum('bqhcd,bkhcd->bhcqk', qb, k).astype(jnp.float32)
        rel = key_pos[None, :] - (i * Q_BLOCK + jnp.arange(Q_BLOCK))[:, None]
        bias = rel_bias[t5_bucket(rel)].astype(jnp.float32)
        s = s + jnp.transpose(bias, (2, 0, 1))[None, :, None]
        p = jax.nn.softmax(s, axis=-1)
        w = (p[:, :, 0] - lam * p[:, :, 1]).astype(v.dtype)
        return jnp.einsum('bhqk,bkhe->bqhe', w, v)

    o = lax.map(block, (jnp.arange(nb), qb_all))
    o = o.transpose(1, 0, 2, 3, 4).reshape(b, L, DA_HEADS, 2 * DA_HEAD_DIM)
    o = rmsnorm(o, subln_w) * (1.0 - lambda_init)
    return o.reshape(b, L, D_ATT) * jax.nn.silu(g)


def complex_affine_combine(e1, e2):
    ar1, ai1, br1, bi1 = e1
    ar2, ai2, br2, bi2 = e2
    ar = ar1 * ar2 - ai1 * ai2
    ai = ar1 * ai2 + ai1 * ar2
    br = ar2 * br1 - ai2 * bi1 + br2
    bi = ar2 * bi1 + ai2 * br1 + bi2
    return ar, ai, br, bi


def s5_mixer(u, lam_re, lam_im, log_dt, B_re, B_im, C_re, C_im, D_skip, w_glu, b_glu):
    b, L, _ = u.shape
    ug = u.reshape(b, L, S5_GROUPS, S5_GROUP)
    y = u * D_skip
    for d in range(2):
        lr, li = lam_re[d], lam_im[d]
        step = jnp.exp(log_dt[d])[:, None]
        mag = jnp.exp(lr * step)
        ab_re = mag * jnp.cos(li * step)
        ab_im = mag * jnp.sin(li * step)
        den = lr * lr + li * li
        nr = ab_re - 1.0
        cr = (nr * lr + ab_im * li) / den
        ci = (ab_im * lr - nr * li) / den
        bb_re = cr[..., None] * B_re - ci[..., None] * B_im
        bb_im = cr[..., None] * B_im + ci[..., None] * B_re
        bu_re = jnp.einsum('blgc,gpc->lbgp', ug, bb_re)
        bu_im = jnp.einsum('blgc,gpc->lbgp', ug, bb_im)
        a_re = jnp.broadcast_to(ab_re, (L, 1) + ab_re.shape)
        a_im = jnp.broadcast_to(ab_im, (L, 1) + ab_im.shape)
        _, _, x_re, x_im = lax.associative_scan(complex_affine_combine, (a_re, a_im, bu_re, bu_im),
                                                reverse=(d == 1), axis=0)
        yd = jnp.einsum('lbgp,gcp->blgc', x_re, C_re[d]) - jnp.einsum('lbgp,gcp->blgc', x_im, C_im[d])
        y = y + yd.reshape(b, L, D_S5)
    h = jax.nn.gelu(y)
    return h * jax.nn.sigmoid(h @ w_glu + b_glu)


def setup_inputs(seed: int = 0) -> dict:
    key = jax.random.key(seed)
    ks = jax.random.split(key, 28)
    f32 = jnp.float32

    def nrm(k, shape, scale):
        return jax.random.normal(k, shape, f32) * scale

    def unif(k, shape, lo, hi):
        return jax.random.uniform(k, shape, f32, minval=lo, maxval=hi)

    x_prompt = nrm(ks[0], (BATCH, SEQ, D_MODEL), 1.0)
    x_sample = nrm(ks[1], (DEC_BATCH, DEC_SEQ, D_MODEL), 1.0)
    norm_w = 1.0 + nrm(ks[2], (DEPTH, D_MODEL), 0.02)
    final_norm_w = 1.0 + nrm(ks[3], (D_MODEL,), 0.02)
    rel_bias = nrm(ks[4], (N_BUCKETS, DA_HEADS), 0.5)
    w_in_ab = nrm(ks[5], (N_EVEN, D_MODEL, IN_AB), D_MODEL ** -0.5)
    conv_w = nrm(ks[6], (N_EVEN, D_CONV, CONV_CH), D_CONV ** -0.5)
    conv_b = nrm(ks[7], (N_EVEN, CONV_CH), 0.02)
    dt0 = jnp.exp(unif(ks[8], (N_EVEN, 2, SSD_HEADS), math.log(1e-3), math.log(1e-1)))
    ssd_dt_bias = dt0 + jnp.log(-jnp.expm1(-dt0))
    ssd_A_log = jnp.log(unif(ks[9], (N_EVEN, 2, SSD_HEADS), 1.0, 16.0))
    ssd_D = 1.0 + nrm(ks[10], (N_EVEN, SSD_HEADS), 0.02)
    ssd_norm_w = 1.0 + nrm(ks[11], (N_EVEN, D_SSD), 0.02)
    diff_lambda = nrm(ks[12], (N_EVEN, 4, DA_HEAD_DIM), 0.1)
    diff_subln_w = 1.0 + nrm(ks[13], (N_EVEN, 2 * DA_HEAD_DIM), 0.02)
    w_out_ab = nrm(ks[14], (N_EVEN, D_AB, D_MODEL), D_AB ** -0.5)
    w_in_c = nrm(ks[15], (N_ODD, D_MODEL, IN_C), D_MODEL ** -0.5)
    n_idx = jnp.arange(S5_STATE, dtype=f32)
    s5_lambda_re = -0.5 + nrm(ks[16], (N_ODD, 2, S5_GROUPS, S5_STATE), 0.01)
    s5_lambda_im = math.pi * n_idx + nrm(ks[17], (N_ODD, 2, S5_GROUPS, S5_STATE), 0.01)
    s5_log_dt = unif(ks[18], (N_ODD, 2, S5_GROUPS), math.log(1e-3), math.log(1e-1))
    s5_B_re = nrm(ks[19], (N_ODD, S5_GROUPS, S5_STATE, S5_GROUP), (2 * S5_GROUP) ** -0.5)
    s5_B_im = nrm(ks[20], (N_ODD, S5_GROUPS, S5_STATE, S5_GROUP), (2 * S5_GROUP) ** -0.5)
    s5_C_re = nrm(ks[21], (N_ODD, 2, S5_GROUPS, S5_GROUP, S5_STATE), S5_STATE ** -0.5)
    s5_C_im = nrm(ks[22], (N_ODD, 2, S5_GROUPS, S5_GROUP, S5_STATE), S5_STATE ** -0.5)
    s5_D = 1.0 + nrm(ks[23], (N_ODD, D_S5), 0.1)
    w_glu = nrm(ks[24], (N_ODD, D_S5, D_S5), D_S5 ** -0.5)
    b_glu = nrm(ks[25], (N_ODD, D_S5), 0.02)
    w_out_c = nrm(ks[26], (N_ODD, D_S5, D_MODEL), D_S5 ** -0.5)
    return {'x_prompt': x_prompt, 'x_sample': x_sample, 'norm_w': norm_w, 'final_norm_w': final_norm_w,
            'rel_bias': rel_bias, 'w_in_ab': w_in_ab, 'conv_w': conv_w, 'conv_b': conv_b,
            'ssd_dt_bias': ssd_dt_bias, 'ssd_A_log': ssd_A_log, 'ssd_D': ssd_D, 'ssd_norm_w': ssd_norm_w,
            'diff_lambda': diff_lambda, 'diff_subln_w': diff_subln_w, 'w_out_ab': w_out_ab,
            'w_in_c': w_in_c, 's5_lambda_re': s5_lambda_re, 's5_lambda_im': s5_lambda_im,
            's5_log_dt': s5_log_dt, 's5_B_re': s5_B_re, 's5_B_im': s5_B_im, 's5_C_re': s5_C_re,
            's5_C_im': s5_C_im, 's5_D': s5_D, 'w_glu': w_glu, 'b_glu': b_glu, 'w_out_c': w_out_c}


def reference(x_prompt, x_sample, norm_w, final_norm_w, rel_bias, w_in_ab, conv_w, conv_b,
              ssd_dt_bias, ssd_A_log, ssd_D, ssd_norm_w, diff_lambda, diff_subln_w, w_out_ab,
              w_in_c, s5_lambda_re, s5_lambda_im, s5_log_dt, s5_B_re, s5_B_im, s5_C_re, s5_C_im,
              s5_D, w_glu, b_glu, w_out_c):
    split_ab = [D_SSD, D_SSD + CONV_CH, D_SSD + CONV_CH + 2 * SSD_HEADS]
    split_ab = split_ab + [split_ab[-1] + D_ATT * i for i in range(1, 4)]

    def run(x):
        for l in range(DEPTH):
            h = rmsnorm(x, norm_w[l])
            if l % 2 == 0:
                e = l // 2
                proj = h @ w_in_ab[e]
                z, xbc, dt_raw, q, k, v, g = jnp.split(proj, split_ab, axis=-1)
                y_ssd = ssd_mixer(z, xbc, dt_raw, conv_w[e], conv_b[e], ssd_dt_bias[e], ssd_A_log[e],
                                  ssd_D[e], ssd_norm_w[e])
                lambda_init = 0.8 - 0.6 * math.exp(-0.3 * l)
                y_att = diff_attention(q, k, v, g, diff_lambda[e], diff_subln_w[e], rel_bias, lambda_init)
                x = x + jnp.concatenate([y_ssd, y_att], axis=-1) @ w_out_ab[e]
            else:
                o = l // 2
                proj = h @ w_in_c[o]
                u, zc = proj[..., :D_S5], proj[..., D_S5:]
                y = s5_mixer(u, s5_lambda_re[o], s5_lambda_im[o], s5_log_dt[o], s5_B_re[o], s5_B_im[o],
                             s5_C_re[o], s5_C_im[o], s5_D[o], w_glu[o], b_glu[o])
                x = x + (y * jax.nn.silu(zc)) @ w_out_c[o]
        return rmsnorm(x, final_norm_w)

    y_prompt = run(x_prompt)
    y_sample = run(x_sample)
    return (y_prompt, y_sample)
```

```python
import contextlib
import math
import numpy as np
import concourse.bass as bass
import concourse.mybir as mybir
from concourse.bass_utils import run_bass_kernel_spmd

F32 = mybir.dt.float32
BF16 = mybir.dt.bfloat16
AF = mybir.ActivationFunctionType
ALU = mybir.AluOpType
AX = mybir.AxisListType

L = 2048
D = 1024
EPS = 1e-5
IN_AB = 6432
N_CORES = 8


class _Rec:
    def __init__(self):
        self.call = None

    def __getattr__(self, name):
        def f(*a, **k):
            self.call = (name, a, k)
            return self
        return f


def _record(fn):
    if fn is None:
        return None
    r = _Rec()
    fn(r)
    assert r.call is not None
    return r.call


class Prog:
    ENGS = ("pe", "act", "dve", "pool", "sp")

    def __init__(self, nc, same_engine_sync=True):
        self.nc = nc
        self.same = same_engine_sync
        self.streams = {e: [] for e in self.ENGS}
        self.cnt = {e: 0 for e in self.ENGS}
        self.dcnt = {e: 0 for e in self.ENGS}
        self.seen = {e: {} for e in self.ENGS}
        self.lastw = {}
        self.readers = {}
        self.n_ops = 0

    def _deps(self, eng, reads, writes):
        ev = []
        for k in reads:
            w = self.lastw.get(k)
            if w is not None:
                ev.append(w)
        for k in writes:
            w = self.lastw.get(k)
            if w is not None:
                ev.append(w)
            ev.extend(self.readers.get(k, ()))
        waits = {}
        seen = self.seen[eng]
        own = "c_" + eng
        for (s, v) in ev:
            if s == own and (eng == "pe" or not self.same):
                continue
            if seen.get(s, 0) >= v:
                continue
            if waits.get(s, 0) < v:
                waits[s] = v
        for s, v in waits.items():
            seen[s] = v
        return list(waits.items())

    def _commit(self, tok, reads, writes):
        for k in reads:
            self.readers.setdefault(k, []).append(tok)
        for k in writes:
            self.lastw[k] = tok
            self.readers[k] = []

    def op(self, eng, fn, reads=(), writes=()):
        waits = self._deps(eng, reads, writes)
        self.cnt[eng] += 1
        tok = ("c_" + eng, self.cnt[eng])
        self.streams[eng].append((waits, _record(fn), tok[0], 1))
        self._commit(tok, reads, writes)
        self.n_ops += 1

    NDS = 16

    def dma(self, eng, fn, reads=(), writes=()):
        waits = self._deps(eng, reads, writes)
        n = self.dcnt[eng]
        self.dcnt[eng] += 1
        r = n % self.NDS
        k = n // self.NDS + 1
        sname = f"d_{eng}_{r}"
        if k > 1 and self.seen[eng].get(sname, 0) < 16 * (k - 1):
            waits = [w for w in waits if w[0] != sname] + [(sname, 16 * (k - 1))]
            self.seen[eng][sname] = 16 * (k - 1)
        tok = (sname, 16 * k)
        self.streams[eng].append((waits, _record(fn), sname, 16))
        self._commit(tok, reads, writes)
        self.n_ops += 1

    def all_tokens(self):
        fin = []
        for e in self.ENGS:
            if self.cnt[e]:
                fin.append(("c_" + e, self.cnt[e]))
            n = self.dcnt[e]
            for r in range(min(n, self.NDS)):
                k = (n - 1 - r) // self.NDS + 1
                fin.append((f"d_{e}_{r}", 16 * k))
        return fin

    def barrier(self):
        fin = self.all_tokens()
        for e in self.ENGS:
            waits = []
            for (s, v) in fin:
                if self.seen[e].get(s, 0) >= v:
                    continue
                if s == "c_" + e and e == "pe":
                    continue
                waits.append((s, v))
                self.seen[e][s] = v
            if waits:
                self.streams[e].append((waits, None, None, 0))

    def emit(self, final_wait_eng="sp"):
        nc = self.nc
        names = ["c_" + e for e in self.ENGS]
        for e in self.ENGS:
            names += [f"d_{e}_{r}" for r in range(min(self.dcnt[e], self.NDS))]
        fin = self.all_tokens()
        with contextlib.ExitStack() as st:
            sems = {n: st.enter_context(nc.semaphore(n)) for n in names}
            block = st.enter_context(nc.Block())

            def mk(ename):
                def body(engine):
                    for (waits, fn, sname, inc) in self.streams[ename]:
                        for (s, v) in waits:
                            engine.wait_ge(sems[s], v)
                        if fn is not None:
                            name, a, k = fn
                            ins = getattr(engine, name)(*a, **k)
                            ins.then_inc(sems[sname], inc)
                    if ename == final_wait_eng:
                        for (s, v) in fin:
                            engine.wait_ge(sems[s], v)
                return body

            block.tensor(mk("pe"))
            block.scalar(mk("act"))
            block.vector(mk("dve"))
            block.gpsimd(mk("pool"))
            block.sync(mk("sp"))


class BuilderBase:
    def __init__(self, nseq, layers, final_norm=True):
        self.nseq = nseq
        self.layers = layers
        self.final_norm = final_norm
        self.nc = bass.Bass("TRN2", target_bir_lowering=False)
        self.P = Prog(self.nc)
        self.dram = {}
        self.uid = 0
        import os
        self.stop = os.environ.get("K_STOP", "")

    def din(self, name, shape, dt=F32):
        t = self.nc.dram_tensor(name, list(shape), dt, kind="ExternalInput").ap()
        self.dram[name] = t
        return t

    def sb(self, st, name, shape, dt):
        self.uid += 1
        return st.enter_context(self.nc.sbuf_tensor(f"{name}_u{self.uid}", list(shape), dt))

    def build(self):
        nc, P = self.nc, self.P
        ns = self.nseq
        x_all = self.din("x_all", [ns, L, D])
        self.norm_w = self.din("norm_w", [4, D])
        self.final_norm_w = self.din("final_norm_w", [D])
        ident_d = self.din("ident", [128, 128])
        y_all = nc.dram_tensor("y_all", [ns, L, D], F32, kind="ExternalOutput").ap()
        self.declare_weights()
        with contextlib.ExitStack() as st0:
            self._st_small = st0
            self.identb = self.sb(st0, "identb", [128, 128], BF16)
            self.nwT = self.sb(st0, "nwT", [128, 4, 8], F32)
            self.ss = self.sb(st0, "ss", [128, 16], F32)
            self.rstd = self.sb(st0, "rstd", [128, 16], F32)
            self.epsc = self.sb(st0, "epsc", [128, 1], F32)
            self.A16 = self.sb(st0, "A16", [128, 2, 2, 64], F32)
            self.ps = [st0.enter_context(nc.psum_tensor(f"ps{i}", [128, 512], F32)) for i in range(8)]
            P.op("dve", lambda e: e.memset(self.epsc[:], EPS), writes=["epsc"])
            P.dma("pool", lambda e: e.dma_start(out=self.identb[:], in_=ident_d), writes=["identb"])
            P.dma("sp", lambda e: e.dma_start(out=self.nwT[:], in_=self.norm_w.rearrange("l (k p) -> p l k", p=128),
                                              allow_slow_non_contiguous=True), writes=["nw"])
            self.prologue()
            st = st0
            self.x_sb = self.sb(st, "x_sb", [128, 16, D], F32)
            self.hT = self.sb(st, "hT", [128, 8, L], BF16)
            x_sb = self.x_sb
            for s in range(ns):
                xv = x_all[s].rearrange("(c j) d -> c j d", j=16)
                for q in range(4):
                    P.dma("sp", lambda e, q=q, xv=xv: e.dma_start(out=x_sb[:, 4 * q:4 * q + 4, :], in_=xv[:, 4 * q:4 * q + 4, :]),
                          writes=[("x", j) for j in range(4 * q, 4 * q + 4)])
                for (kind, idx, lnum) in self.layers:
                    self.rms_to_hT(lnum)
                    if self.stop in ("pro", "p1", "p2", "p3", "p4", "p5"):
                        continue
                    if kind == "s5":
                        self.s5_layer(idx)
                    elif kind == "ab":
                        self.ab_layer(idx, lnum)
                yv = y_all[s].rearrange("(c j) d -> c j d", j=16)
                with contextlib.ExitStack() as stf:
                    obs = [self.sb(stf, f"ob{i}", [128, D], F32) for i in range(2)]
                    self.junk = self.sb(stf, "junk", [128, D], BF16)
                    self.fnw = self.sb(stf, "fnw", [128, D], F32)
                    P.dma("sp", lambda e: e.dma_start(out=self.fnw[:], in_=self.final_norm_w.partition_broadcast(128)), writes=["nw"])
                    for j in range(16):
                        ob = obs[j % 2]
                        okey = f"ob{j % 2}"
                        if self.final_norm:
                            self.rms_stats(j)
                            P.op("dve", lambda e, j=j, ob=ob: e.scalar_tensor_tensor(
                                out=ob[:], in0=x_sb[:, j, :], scalar=self.rstd[:, j:j + 1], in1=self.fnw[:],
                                op0=ALU.mult, op1=ALU.mult), reads=[("x", j), "rstd", "nw"], writes=[okey])
                        else:
                            P.op("dve", lambda e, j=j, ob=ob: e.tensor_copy(out=ob[:], in_=x_sb[:, j, :]),
                                 reads=[("x", j)], writes=[okey])
                        P.dma("sp", lambda e, j=j, ob=ob, yv=yv: e.dma_start(out=yv[:, j, :], in_=ob[:]),
                              reads=[okey], writes=[("yout", s, j)])
                    P.barrier()
            P.emit()
        return nc

    def declare_weights(self):
        kinds = set(k for (k, _, _) in self.layers)
        self.kinds = kinds
        if "s5" in kinds:
            self.s5_declare()
        if "ab" in kinds:
            self.ab_declare()

    def prologue(self):
        if "s5" in self.kinds:
            for o in sorted(set(i for (k, i, _) in self.layers if k == "s5")):
                self.s5_prologue(o)
        if "ab" in self.kinds:
            self.ab_prologue()

    def rms_stats(self, j):
        P = self.P
        P.op("act", lambda e: e.activation(out=self.junk[:], in_=self.x_sb[:, j, :], func=AF.Square,
                                           accum_out=self.ss[:, j:j + 1]),
             reads=[("x", j)], writes=["junk", "ss"])
        P.op("act", lambda e: e.activation(out=self.rstd[:, j:j + 1], in_=self.ss[:, j:j + 1], func=AF.Sqrt,
                                           bias=self.epsc[:, 0:1], scale=1.0 / D),
             reads=["ss", "epsc"], writes=["rstd"])
        P.op("dve", lambda e: e.reciprocal(out=self.rstd[:, j:j + 1], in_=self.rstd[:, j:j + 1]),
             reads=["rstd"], writes=["rstd"])

    def rms_to_hT(self, lnum):
        P = self.P
        with contextlib.ExitStack() as st:
            hbs = [self.sb(st, f"hb{i}", [128, D], BF16) for i in range(2)]
            self.junk = self.sb(st, "junk", [128, D], BF16)
            for j in range(16):
                self.rms_stats(j)
                hb = hbs[j % 2]
                hk = f"hb{j % 2}"
                P.op("dve", lambda e, j=j, hb=hb: e.tensor_scalar(out=hb[:], in0=self.x_sb[:, j, :], scalar1=self.rstd[:, j:j + 1],
                                                                  scalar2=None, op0=ALU.mult), reads=[("x", j), "rstd"], writes=[hk])
                pt = self.ps[j % 2]
                pk = ("ps", j % 2)
                ptv = pt[:].bitcast(BF16)
                for kt in range(8):
                    P.op("pe", lambda e, kt=kt, hb=hb, ptv=ptv: e.transpose(ptv[:, kt * 128:(kt + 1) * 128],
                                                                             hb[:, kt * 128:(kt + 1) * 128], self.identb[:]),
                         reads=[hk, "identb"], writes=[pk])
                P.op("dve", lambda e, j=j, ptv=ptv: e.tensor_tensor(out=self.hT[:, :, j::16], in0=ptv.rearrange("p (k c) -> p k c", k=8),
                                                                    in1=self.nwT[:, lnum, :].unsqueeze(2).to_broadcast([128, 8, 128]), op=ALU.mult),
                     reads=[pk, "nw"], writes=["hT"])
            P.barrier()


def _consts():
    mf, mb, idm = _s5_masks()
    return {"ident": np.eye(128, dtype=np.float32), "identf": np.eye(128, dtype=np.float32), "att_bidx": _att_bidx(), "ssd_masks": _ssd_masks(),
            "s5_ktab": _s5_ktab(), "s5_mf": mf, "s5_mb": mb, "s5_idm": idm}


_CACHE = {}


def kernel(**inputs):
    xp = np.asarray(inputs["x_prompt"], dtype=np.float32)
    xs = np.asarray(inputs["x_sample"], dtype=np.float32)
    layers = [("ab", 0, 0), ("s5", 0, 1), ("ab", 1, 2), ("s5", 1, 3)]
    b = Builder(6, layers)
    nc = b.build()
    consts = _consts()
    in_maps = []
    for i in range(N_CORES):
        xa = np.concatenate([xp[2 * i:2 * i + 2], xs[4 * i:4 * i + 4]], axis=0)
        m = {"x_all": np.ascontiguousarray(xa)}
        for k in b.dram:
            if k == "x_all":
                continue
            if k in consts:
                m[k] = consts[k]
            else:
                m[k] = np.ascontiguousarray(np.asarray(inputs[k], dtype=np.float32))
        in_maps.append(m)
    res = run_bass_kernel_spmd(nc, in_maps, core_ids=list(range(N_CORES)))
    yp = np.empty_like(xp)
    ys = np.empty_like(xs)
    for i in range(N_CORES):
        y = res.results[i]["y_all"]
        yp[2 * i:2 * i + 2] = y[0:2]
        ys[4 * i:4 * i + 4] = y[2:6]
    return (yp, ys)


TWO_PI = 2.0 * math.pi


def _s5_ktab():
    kt = np.zeros((128, 5, 16), np.float32)
    idx = np.arange(16, dtype=np.float32)
    kt[:64, 0] = -idx
    kt[:64, 1] = 15 - idx
    kt[:64, 2] = idx
    kt[:64, 3] = idx + 1
    kt[64:, 0] = idx
    kt[64:, 1] = idx
    kt[64:, 2] = -idx
    kt[64:, 3] = 16 - idx
    kt[:, 4, 0] = 16
    kt[:, 4, 1] = 1
    return kt


def _s5_masks():
    j = (np.arange(256) // 16)[:, None]
    i = (np.arange(256) // 16)[None, :]
    mf = (i >= j).astype(np.float32).reshape(2, 128, 256)
    mb = (j >= i).astype(np.float32).reshape(2, 128, 256)
    idm = np.eye(256, dtype=np.float32).reshape(2, 128, 256)
    return mf, mb, idm


class S5Mixin:
    GB = 2

    def s5_declare(self):
        nc = self.nc
        for nm, shp in [("w_in_c", [2, 1024, 2048]), ("s5_lambda_re", [2, 2, 64, 64]), ("s5_lambda_im", [2, 2, 64, 64]),
                        ("s5_log_dt", [2, 2, 64]), ("s5_B_re", [2, 64, 64, 16]), ("s5_B_im", [2, 64, 64, 16]),
                        ("s5_C_re", [2, 2, 64, 16, 64]), ("s5_C_im", [2, 2, 64, 16, 64]), ("s5_D", [2, 1024]),
                        ("w_glu", [2, 1024, 1024]), ("b_glu", [2, 1024]), ("w_out_c", [2, 1024, 1024])]:
            setattr(self, nm, self.din(nm, shp))
        self.s5_ktab = self.din("s5_ktab", [128, 5, 16])
        self.s5_mf = self.din("s5_mf", [2, 128, 256])
        self.s5_mb = self.din("s5_mb", [2, 128, 256])
        self.s5_idm = self.din("s5_idm", [2, 128, 256])
        self.identf_d = self.din("identf", [128, 128])
        import os
        kd = "ExternalOutput" if os.environ.get("K_DEBUG") else "Internal"
        self.LS = nc.dram_tensor("s5_LS", [2, 64, 128, 512], BF16, kind=kd).ap()
        self.SWS = nc.dram_tensor("s5_SWS", [2, 64, 128, 512], BF16, kind=kd).ap()
        self.WXS = nc.dram_tensor("s5_WXS", [2, 64, 128, 1024], BF16, kind=kd).ap()

    def s5_prologue(self, o):
        nc, P = self.nc, self.P
        I32 = mybir.dt.int32
        GB = self.GB
        with contextlib.ExitStack() as st:
            sb = lambda n, s, d=F32: self.sb(st, f"s5p_{n}", s, d)
            identf = sb("identf", [128, 128]); ktab = sb("ktab", [128, 5, 16])
            mf = sb("mf", [128, 2, 256]); mb = sb("mb", [128, 2, 256]); idm = sb("idm", [128, 2, 256])
            P.dma("sp", lambda e: e.dma_start(out=identf[:], in_=self.identf_d), writes=["s5p_identf"])
            P.dma("sp", lambda e: e.dma_start(out=ktab[:], in_=self.s5_ktab), writes=["s5p_ktab"])
            P.dma("sp", lambda e: e.dma_start(out=mf[:], in_=self.s5_mf.rearrange("k p f -> p k f")), writes=["s5p_mf"])
            P.dma("sp", lambda e: e.dma_start(out=mb[:], in_=self.s5_mb.rearrange("k p f -> p k f")), writes=["s5p_mb"])
            P.dma("sp", lambda e: e.dma_start(out=idm[:], in_=self.s5_idm.rearrange("k p f -> p k f")), writes=["s5p_idm"])
            raw = sb("raw", [64, 2, 2, 64])
            P.dma("sp", lambda e: e.dma_start(out=raw[:, 0], in_=self.s5_lambda_re[o].rearrange("d g n -> g d n")), writes=["s5p_raw"])
            P.dma("sp", lambda e: e.dma_start(out=raw[:, 1], in_=self.s5_lambda_im[o].rearrange("d g n -> g d n")), writes=["s5p_raw"])
            LR = sb("LR", [128, 64]); LI = sb("LI", [128, 64]); STEP = sb("STEP", [128, 64])
            for ri in range(2):
                P.op("pe", lambda e, ri=ri: e.transpose(self.ps[0][:, ri * 64:(ri + 1) * 64], raw[:, ri, :, :], identf[0:64, 0:64]),
                     reads=["s5p_raw", "s5p_identf"], writes=[("ps", 0)])
            for d in range(2):
                hs = slice(d * 64, (d + 1) * 64)
                P.dma("sp", lambda e, hs=hs, d=d: e.dma_start(out=STEP[hs, :], in_=self.s5_log_dt[o, d].partition_broadcast(64)),
                      writes=["s5p_STEP"])
            P.op("dve", lambda e: e.tensor_copy(out=LR[:], in_=self.ps[0][:, 0:64]), reads=[("ps", 0)], writes=["s5p_LR"])
            P.op("dve", lambda e: e.tensor_copy(out=LI[:], in_=self.ps[0][:, 64:128]), reads=[("ps", 0)], writes=["s5p_LI"])
            P.op("act", lambda e: e.activation(out=STEP[:], in_=STEP[:], func=AF.Exp), reads=["s5p_STEP"], writes=["s5p_STEP"])
            LSt = sb("LSt", [128, 64]); TH = sb("TH", [128, 64])
            P.op("dve", lambda e: e.tensor_mul(out=LSt[:], in0=LR[:], in1=STEP[:]), reads=["s5p_LR", "s5p_STEP"], writes=["s5p_LSt"])
            P.op("dve", lambda e: e.tensor_mul(out=TH[:], in0=LI[:], in1=STEP[:]), reads=["s5p_LI", "s5p_STEP"], writes=["s5p_TH"])
            if self.stop == "p1":
                P.barrier(); return
            ER = [sb(f"ER{t}", [128, 64, 16]) for t in range(5)]
            EI = [sb(f"EI{t}", [128, 64, 16]) for t in range(5)]
            arg = sb("arg", [128, 64, 16]); ti = sb("ti", [128, 64, 16], I32); tf = sb("tf", [128, 64, 16]); tg = sb("tg", [128, 64, 16])
            mag = sb("mag", [128, 64, 16]); sn = sb("sn", [128, 64, 16])
            shp = [128, 64, 16]

            def reduce_turns(key):
                P.op("dve", lambda e: e.tensor_copy(out=ti[:], in_=arg[:]), reads=[key], writes=["s5p_ti"])
                P.op("dve", lambda e: e.tensor_copy(out=tf[:], in_=ti[:]), reads=["s5p_ti"], writes=["s5p_tf"])
                P.op("dve", lambda e: e.tensor_sub(out=arg[:], in0=arg[:], in1=tf[:]), reads=[key, "s5p_tf"], writes=[key])
                P.op("dve", lambda e: e.tensor_scalar(out=tg[:], in0=arg[:], scalar1=0.5, scalar2=None, op0=ALU.is_gt), reads=[key], writes=["s5p_tg"])
                P.op("dve", lambda e: e.tensor_sub(out=arg[:], in0=arg[:], in1=tg[:]), reads=[key, "s5p_tg"], writes=[key])
                P.op("dve", lambda e: e.tensor_scalar(out=tg[:], in0=arg[:], scalar1=-0.5, scalar2=None, op0=ALU.is_lt), reads=[key], writes=["s5p_tg"])
                P.op("dve", lambda e: e.tensor_add(out=arg[:], in0=arg[:], in1=tg[:]), reads=[key, "s5p_tg"], writes=[key])

            for t in range(5):
                kb = ktab[:, t, :].unsqueeze(1).to_broadcast(shp)
                thb = TH[:].unsqueeze(2).to_broadcast(shp)
                lsb = LSt[:].unsqueeze(2).to_broadcast(shp)
                P.op("dve", lambda e, kb=kb, lsb=lsb: e.tensor_tensor(out=mag[:], in0=lsb, in1=kb, op=ALU.mult),
                     reads=["s5p_LSt", "s5p_ktab"], writes=["s5p_mag"])
                P.op("act", lambda e: e.activation(out=mag[:], in_=mag[:], func=AF.Exp), reads=["s5p_mag"], writes=["s5p_mag"])
                for which in range(2):
                    P.op("dve", lambda e, kb=kb, thb=thb: e.tensor_tensor(out=arg[:], in0=thb, in1=kb, op=ALU.mult),
                         reads=["s5p_TH", "s5p_ktab"], writes=["s5p_arg"])
                    P.op("dve", lambda e, which=which: e.tensor_scalar(out=arg[:], in0=arg[:], scalar1=1.0 / TWO_PI,
                                                                        scalar2=0.25 * which, op0=ALU.mult, op1=ALU.add),
                         reads=["s5p_arg"], writes=["s5p_arg"])
                    reduce_turns("s5p_arg")
                    P.op("act", lambda e: e.activation(out=sn[:], in_=arg[:], func=AF.Sin, scale=TWO_PI), reads=["s5p_arg"], writes=["s5p_sn"])
                    dst = EI[t] if which == 0 else ER[t]
                    P.op("dve", lambda e, dst=dst: e.tensor_mul(out=dst[:], in0=mag[:], in1=sn[:]),
                         reads=["s5p_mag", "s5p_sn"], writes=[f"s5p_E{t}{which}"])
            if self.stop == "p2":
                P.barrier(); return
            ekeys = lambda t: [f"s5p_E{t}0", f"s5p_E{t}1"]
            P.op("dve", lambda e: e.tensor_copy(out=self.A16[:, o, 0, :], in_=ER[4][:, :, 0]), reads=ekeys(4), writes=["A16"])
            P.op("dve", lambda e: e.tensor_copy(out=self.A16[:, o, 1, :], in_=EI[4][:, :, 0]), reads=ekeys(4), writes=["A16"])
            nr = sb("nr", [128, 64]); den = sb("den", [128, 64]); t1 = sb("t1", [128, 64]); t2 = sb("t2", [128, 64])
            cr = sb("cr", [128, 64]); ci = sb("ci", [128, 64])
            a1r = ER[4][:, :, 1]; a1i = EI[4][:, :, 1]
            K = ["s5p_nr", "s5p_den", "s5p_t1", "s5p_t2", "s5p_cr", "s5p_ci", "s5p_LR", "s5p_LI"] + ekeys(4)
            ops = [
                lambda e: e.tensor_scalar(out=nr[:], in0=a1r, scalar1=-1.0, scalar2=None, op0=ALU.add),
                lambda e: e.tensor_mul(out=den[:], in0=LR[:], in1=LR[:]),
                lambda e: e.tensor_mul(out=t1[:], in0=LI[:], in1=LI[:]),
                lambda e: e.tensor_add(out=den[:], in0=den[:], in1=t1[:]),
                lambda e: e.reciprocal(out=den[:], in_=den[:]),
                lambda e: e.tensor_mul(out=t1[:], in0=nr[:], in1=LR[:]),
                lambda e: e.tensor_mul(out=t2[:], in0=a1i, in1=LI[:]),
                lambda e: e.tensor_add(out=t1[:], in0=t1[:], in1=t2[:]),
                lambda e: e.tensor_mul(out=cr[:], in0=t1[:], in1=den[:]),
                lambda e: e.tensor_mul(out=t1[:], in0=a1i, in1=LR[:]),
                lambda e: e.tensor_mul(out=t2[:], in0=nr[:], in1=LI[:]),
                lambda e: e.tensor_sub(out=t1[:], in0=t1[:], in1=t2[:]),
                lambda e: e.tensor_mul(out=ci[:], in0=t1[:], in1=den[:]),
            ]
            for f in ops:
                P.op("dve", f, reads=K, writes=K[:6])
            CAr = [sb(f"CAr{t}", shp) for t in range(2)]
            CAi = [sb(f"CAi{t}", shp) for t in range(2)]
            crb = cr[:].unsqueeze(2).to_broadcast(shp)
            cib = ci[:].unsqueeze(2).to_broadcast(shp)
            for t in range(2):
                kk = ["s5p_cr", "s5p_ci", "s5p_arg", "s5p_mag"] + ekeys(t)
                P.op("dve", lambda e, t=t: e.tensor_tensor(out=arg[:], in0=ER[t][:], in1=crb, op=ALU.mult), reads=kk, writes=["s5p_arg"])
                P.op("dve", lambda e, t=t: e.tensor_tensor(out=mag[:], in0=EI[t][:], in1=cib, op=ALU.mult), reads=kk, writes=["s5p_mag"])
                P.op("dve", lambda e, t=t: e.tensor_sub(out=CAr[t][:], in0=arg[:], in1=mag[:]), reads=kk, writes=[f"s5p_CAr{t}"])
                P.op("dve", lambda e, t=t: e.tensor_tensor(out=arg[:], in0=EI[t][:], in1=crb, op=ALU.mult), reads=kk, writes=["s5p_arg"])
                P.op("dve", lambda e, t=t: e.tensor_tensor(out=mag[:], in0=ER[t][:], in1=cib, op=ALU.mult), reads=kk, writes=["s5p_mag"])
                P.op("dve", lambda e, t=t: e.tensor_add(out=CAi[t][:], in0=arg[:], in1=mag[:]), reads=kk, writes=[f"s5p_CAi{t}"])
            if self.stop == "p3":
                P.barrier(); return
            BR = sb("BR", [128, 64, 16]); BI = sb("BI", [128, 64, 16])
            for d in range(2):
                hs = slice(d * 64, (d + 1) * 64)
                P.dma("sp", lambda e, hs=hs: e.dma_start(out=BR[hs], in_=self.s5_B_re[o].rearrange("g n c -> n g c")), writes=["s5p_BR"])
                P.dma("sp", lambda e, hs=hs: e.dma_start(out=BI[hs], in_=self.s5_B_im[o].rearrange("g n c -> n g c")), writes=["s5p_BI"])
            CR = sb("CR", [128, 64, 16]); CI = sb("CI", [128, 64, 16])
            craw = sb("craw", [128, 8, 2, 64])
            for ri, (src, dstc) in enumerate([(self.s5_C_re, CR), (self.s5_C_im, CI)]):
                for d in range(2):
                    P.dma("sp", lambda e, src=src, d=d: e.dma_start(out=craw[:, :, d, :], in_=src[o, d].rearrange("(gt g) c n -> (g c) gt n", g=8)),
                          writes=["s5p_craw"])
                for gt in range(8):
                    P.op("pe", lambda e, gt=gt: e.transpose(self.ps[1 + (gt // 4)][:, (gt % 4) * 128:(gt % 4 + 1) * 128],
                                                             craw[:, gt, :, :], identf[:]),
                         reads=["s5p_craw", "s5p_identf"], writes=[("ps", 1 + gt // 4)])
                for hh in range(2):
                    P.op("dve", lambda e, hh=hh, dstc=dstc: e.tensor_copy(
                        out=dstc[:, hh * 32:(hh + 1) * 32, :], in_=self.ps[1 + hh][:, :].rearrange("p (g c) -> p g c", c=16)),
                        reads=[("ps", 1 + hh)], writes=[f"s5p_C{ri}"])
            ckeys = ["s5p_C0", "s5p_C1"]
            DG = sb("DG", [128, 64])
            for jl in range(8):
                P.dma("sp", lambda e, jl=jl: e.dma_start(out=DG[jl * 16:(jl + 1) * 16, :], in_=self.s5_D[o].rearrange("(g c) -> c g", c=16),
                                                        allow_slow_non_contiguous=True), writes=["s5p_DG"])
            if self.stop == "p4":
                P.barrier(); return
            bshape = [128, GB, 16, 16]
            prod = {nm: sb(nm, bshape) for nm in ["Pr", "Pi", "Sr", "Si", "Qr", "QiN", "Wr", "Wi"]}
            tmpa = [sb(f"tmpa{i}", bshape) for i in range(2)]
            tmpb = [sb(f"tmpb{i}", bshape) for i in range(2)]
            Lout = [sb(f"Lout{i}", [128, GB, 2, 256], BF16) for i in range(2)]
            SWout = [sb(f"SWout{i}", [128, GB, 512], BF16) for i in range(2)]
            WXout = [sb(f"WXout{i}", [128, GB, 2, 2, 256], BF16) for i in range(2)]
            QFB = {nm: sb("FB" + nm, [128, 2, GB, 256]) for nm in ["Qr", "QiN"]}
            for i in range(2):
                P.op("dve", lambda e, i=i: e.memset(WXout[i][:], 0.0), writes=[f"s5p_WXout{i}"])
            for nm in ["Qr", "QiN"]:
                P.op("dve", lambda e, nm=nm: e.memset(QFB[nm][:], 0.0), writes=["s5p_FB" + nm])
            l1 = [sb(f"l1_{i}", [128, 256]) for i in range(2)]
            l2 = [sb(f"l2_{i}", [128, 256]) for i in range(2)]
            nb = 64 // GB
            for b in range(nb):
                g0 = b * GB
                gs = slice(g0, g0 + GB)
                pb = b % 2

                def cprod(eng, slot, outr, outi, Ar, Ai, Br, Bi, a_over_ch, keysA, keysB, neg_i=False):
                    Ab = lambda X: X[:, gs, :].unsqueeze(3).to_broadcast(bshape)
                    Bb = lambda X: X[:, gs, :].unsqueeze(2).to_broadcast(bshape)
                    ta, tb = tmpa[slot], tmpb[slot]
                    rk = keysA + keysB
                    ka, kb_ = f"s5p_tmpa{slot}", f"s5p_tmpb{slot}"
                    P.op(eng, lambda e: e.tensor_tensor(out=ta[:], in0=Ab(Ar), in1=Bb(Br), op=ALU.mult), reads=rk, writes=[ka])
                    P.op(eng, lambda e: e.tensor_tensor(out=tb[:], in0=Ab(Ai), in1=Bb(Bi), op=ALU.mult), reads=rk, writes=[kb_])
                    P.op(eng, lambda e: e.tensor_sub(out=prod[outr][:], in0=ta[:], in1=tb[:]), reads=[ka, kb_], writes=["s5p_" + outr])
                    P.op(eng, lambda e: e.tensor_tensor(out=ta[:], in0=Ab(Ar), in1=Bb(Bi), op=ALU.mult), reads=rk, writes=[ka])
                    P.op(eng, lambda e: e.tensor_tensor(out=tb[:], in0=Ab(Ai), in1=Bb(Br), op=ALU.mult), reads=rk, writes=[kb_])
                    if neg_i:
                        P.op("dve", lambda e: e.scalar_tensor_tensor(out=prod[outi][:], in0=ta[:], scalar=-1.0, in1=tb[:], op0=ALU.mult, op1=ALU.subtract),
                             reads=[ka, kb_], writes=["s5p_" + outi])
                    else:
                        P.op(eng, lambda e: e.tensor_add(out=prod[outi][:], in0=ta[:], in1=tb[:]), reads=[ka, kb_], writes=["s5p_" + outi])

                bk = ["s5p_BR", "s5p_BI"]
                cprod("dve", 0, "Pr", "Pi", CAr[0], CAi[0], BR, BI, True, ["s5p_CAr0", "s5p_CAi0"], bk)
                cprod("dve", 1, "Sr", "Si", CAr[1], CAi[1], BR, BI, True, ["s5p_CAr1", "s5p_CAi1"], bk)
                cprod("dve", 0, "Qr", "QiN", ER[2], EI[2], CR, CI, True, ekeys(2), ckeys, neg_i=True)
                cprod("dve", 1, "Wr", "Wi", ER[3], EI[3], CR, CI, True, ekeys(3), ckeys, neg_i=True)
                for nm in ["Qr", "QiN"]:
                    for d in range(2):
                        hs = slice(d * 64, (d + 1) * 64)
                        P.op("dve", lambda e, nm=nm, d=d, hs=hs: e.tensor_copy(out=QFB[nm][hs, d, :, :],
                                                                                 in_=prod[nm][hs].rearrange("p g i c -> p g (i c)")),
                             reads=["s5p_" + nm], writes=["s5p_FB" + nm])
                import os
                sub = int(os.environ.get("K_SUB", "99"))
                if sub == 0:
                    P.barrier(); return
                for gl in range(GB):
                    g = g0 + gl
                    for kh in range(2):
                        if sub == 1 and (gl, kh) == (0, 1):
                            P.barrier(); return
                        bank = 3 + (gl * 2 + kh) % 4
                        pt = self.ps[bank]
                        for d in range(2):
                            outp = pt[:, d * 256:(d + 1) * 256]
                            P.op("pe", lambda e, d=d, outp=outp, gl=gl, kh=kh: e.matmul(
                                outp, prod["Pr"][:, gl, kh * 8:(kh + 1) * 8, :], QFB["Qr"][:, d, gl, :], start=True, stop=False),
                                reads=["s5p_Pr", "s5p_FBQr"], writes=[("ps", bank)])
                            P.op("pe", lambda e, d=d, outp=outp, gl=gl, kh=kh: e.matmul(
                                outp, prod["Pi"][:, gl, kh * 8:(kh + 1) * 8, :], QFB["QiN"][:, d, gl, :], start=False, stop=True),
                                reads=["s5p_Pi", "s5p_FBQiN"], writes=[("ps", bank)])
                        sl = (gl * 2 + kh) % 2
                        P.op("dve", lambda e, pt=pt, kh=kh, sl=sl: e.tensor_tensor(out=l1[sl][:], in0=pt[:, 0:256], in1=mf[:, kh, :], op=ALU.mult),
                             reads=[("ps", bank), "s5p_mf"], writes=[f"s5p_l1_{sl}"])
                        P.op("dve", lambda e, pt=pt, kh=kh, sl=sl: e.tensor_tensor(out=l2[sl][:], in0=pt[:, 256:512], in1=mb[:, kh, :], op=ALU.mult),
                             reads=[("ps", bank), "s5p_mb"], writes=[f"s5p_l2_{sl}"])
                        P.op("dve", lambda e, sl=sl: e.tensor_add(out=l1[sl][:], in0=l1[sl][:], in1=l2[sl][:]),
                             reads=[f"s5p_l1_{sl}", f"s5p_l2_{sl}"], writes=[f"s5p_l1_{sl}"])
                        P.op("dve", lambda e, sl=sl, kh=kh, g=g, gl=gl: e.scalar_tensor_tensor(
                            out=Lout[pb][:, gl, kh, :], in0=idm[:, kh, :], scalar=DG[:, g:g + 1], in1=l1[sl][:], op0=ALU.mult, op1=ALU.add),
                            reads=[f"s5p_l1_{sl}", "s5p_idm", "s5p_DG"], writes=[f"s5p_Lout{pb}"])
                    if sub == 2:
                        P.barrier(); return
                    for kh in range(2):
                        for ri, nm in enumerate(["Sr", "Si"]):
                            q = kh * 2 + ri
                            P.op("pe", lambda e, q=q, nm=nm, gl=gl, kh=kh: e.transpose(
                                self.ps[7][:, q * 128:(q + 1) * 128], prod[nm][:, gl, kh * 8:(kh + 1) * 8, :], identf[:]),
                                reads=["s5p_" + nm, "s5p_identf"], writes=[("ps", 7)])
                    P.op("act", lambda e, gl=gl: e.copy(out=SWout[pb][:, gl, :], in_=self.ps[7][:]), reads=[("ps", 7)], writes=[f"s5p_SWout{pb}"])
                    for d in range(2):
                        hs = slice(d * 64, (d + 1) * 64)
                        for ri, nm in enumerate(["Wr", "Wi"]):
                            P.op("act", lambda e, gl=gl, d=d, hs=hs, ri=ri, nm=nm: e.copy(
                                out=WXout[pb][hs, gl, d, ri, :], in_=prod[nm][hs, gl].rearrange("p i c -> p (i c)")),
                                reads=["s5p_" + nm], writes=[f"s5p_WXout{pb}"])
                if self.stop == "p5":
                    P.barrier(); return
                P.dma("sp", lambda e, gs=gs, pb=pb: e.dma_start(out=self.LS[o, gs].rearrange("g p f -> p g f"),
                                                                in_=Lout[pb][:].rearrange("p g k f -> p g (k f)")),
                      reads=[f"s5p_Lout{pb}"], writes=[("LS", o)])
                P.dma("sp", lambda e, gs=gs, pb=pb: e.dma_start(out=self.SWS[o, gs].rearrange("g p f -> p g f"), in_=SWout[pb][:]),
                      reads=[f"s5p_SWout{pb}"], writes=[("SWS", o)])
                P.dma("sp", lambda e, gs=gs, pb=pb: e.dma_start(out=self.WXS[o, gs].rearrange("g p f -> p g f"),
                                                                in_=WXout[pb][:].rearrange("p g d k f -> p g (d k f)")),
                      reads=[f"s5p_WXout{pb}"], writes=[("WXS", o)])
            P.barrier()

    def s5_layer(self, o):
        nc, P = self.nc, self.P
        GB = self.GB
        x_sb, hT, ps = self.x_sb, self.hT, self.ps
        idb = self.identb

        def load_w(st, name, src_ap, cols):
            t = self.sb(st, name, [128, 8, cols], BF16)
            v = src_ap.rearrange("(kt p) f -> p kt f", p=128)
            for q in range(4):
                P.dma("pool", lambda e, q=q: e.dma_start(out=t[:, 2 * q:2 * q + 2, :], in_=v[:, 2 * q:2 * q + 2, :]), writes=[name])
            return t

        with contextlib.ExitStack() as stL:
            Z = self.sb(stL, "s5_Z", [128, 16, D], BF16)
            zk = lambda j: ("s5_Z", j)
            with contextlib.ExitStack() as st:
                Wu = load_w(st, "s5_Wu", self.w_in_c[o][:, 0:1024], 1024)
                for j in range(16):
                    for half in range(2):
                        bank = (j * 2 + half) % 4
                        for kt in range(8):
                            P.op("pe", lambda e, j=j, half=half, kt=kt, bank=bank: e.matmul(
                                ps[bank][:], hT[:, kt, j::16], Wu[:, kt, half * 512:(half + 1) * 512], start=(kt == 0), stop=(kt == 7)),
                                reads=["hT", "s5_Wu"], writes=[("ps", bank)])
                        eng = "act" if (j + half) % 2 == 0 else "dve"
                        self.evac(eng, Z[:, j, half * 512:(half + 1) * 512], ps[bank][:], [("ps", bank)], [zk(j)])
                P.barrier()
            if self.stop == "A":
                return
            with contextlib.ExitStack() as st:
                SRI = self.sb(st, "s5_SRI", [128, 2, 64, 128], BF16)
                UG = [self.sb(st, f"s5_UG{i}", [128, 2, 128], BF16) for i in range(2)]
                XS = [self.sb(st, f"s5_XS{i}", [128, 2, 64], F32) for i in range(2)]
                identf = self.sb(st, "s5_identf", [128, 128], F32)
                P.dma("sp", lambda e: e.dma_start(out=identf[:], in_=self.identf_d), writes=["s5_identf"])

                STG = [self.sb(st, "s5_STG0", [128, 8, 16, 16], BF16)] * 2

                def make_ug(g, slot):
                    bank = slot
                    ft, g8 = g // 8, g % 8
                    stg = STG[0]
                    sk_ = "s5_STG0"
                    if g8 == 0:
                        P.op("dve", lambda e, stg=stg, ft=ft: e.tensor_copy(
                            out=stg[:], in_=Z[:, :, ft * 128:(ft + 1) * 128].rearrange("p j (g c) -> p g j c", c=16)),
                            reads=[zk(j) for j in range(16)], writes=[sk_])
                    for kh in range(2):
                        P.op("pe", lambda e, kh=kh, bank=bank, stg=stg, g8=g8: e.transpose(
                            ps[bank][:].bitcast(BF16)[:, kh * 128:(kh + 1) * 128], stg[:, g8, kh * 8:(kh + 1) * 8, :], idb[:]),
                            reads=[sk_, "identb"], writes=[("ps", bank)])
                    self.evac("act", UG[slot][:], ps[bank][:].bitcast(BF16)[:, 0:256].rearrange("p (k c) -> p k c", k=2), [("ps", bank)], [f"s5_UG{slot}"])

                stB = contextlib.ExitStack()
                SWt = [self.sb(stB, f"s5_SWt{i}", [128, GB, 512], BF16) for i in range(2)]
                for g in range(64):
                    gl, b = g % GB, g // GB
                    pb = b % 2
                    if gl == 0:
                        P.dma("sp", lambda e, b=b, pb=pb: e.dma_start(out=SWt[pb][:], in_=self.SWS[o, b * GB:(b + 1) * GB].rearrange("g p f -> p g f")),
                              reads=[("SWS", o)], writes=[f"s5_SWt{pb}"])
                    slot = g % 2
                    make_ug(g, slot)
                    bank = 2 + g % 2
                    for d in range(2):
                        for ri in range(2):
                            for kh in range(2):
                                rhs = UG[slot][:, kh, :] if d == 0 else UG[slot][:, kh, ::-1]
                                q = kh * 2 + ri
                                P.op("pe", lambda e, d=d, ri=ri, kh=kh, rhs=rhs, q=q, gl=gl, pb=pb, bank=bank: e.matmul(
                                    ps[bank][:, (d * 2 + ri) * 128:(d * 2 + ri + 1) * 128], SWt[pb][:, gl, q * 128:(q + 1) * 128], rhs,
                                    start=(kh == 0), stop=(kh == 1)),
                                    reads=[f"s5_SWt{pb}", f"s5_UG{slot}"], writes=[("ps", bank)])
                    for d in range(2):
                        hs = slice(d * 64, (d + 1) * 64)
                        self.evac("dve" if d == 0 else "act", SRI[hs, :, g, :],
                                  ps[bank][hs, d * 256:(d + 1) * 256].rearrange("p (r c) -> p r c", r=2), [("ps", bank)], [("s5_S", g, d)])
                P.barrier()
                stB.close()
                if self.stop == "B":
                    return
                P.op("dve", lambda e: e.memset(XS[0][:], 0.0), writes=["s5_XS0"])
                AAt = self.sb(st, "s5_AAt", [128, 2, 64], F32)
                ABt = self.sb(st, "s5_ABt", [128, 2, 64], F32)
                U = self.sb(st, "s5_U", [128, 2, 64], F32)
                V = self.sb(st, "s5_V", [128, 2, 64], F32)
                P.op("dve", lambda e: e.tensor_copy(out=AAt[:, 0, :], in_=self.A16[:, o, 0, :]), reads=["A16"], writes=["s5_AAt"])
                P.op("dve", lambda e: e.tensor_copy(out=AAt[:, 1, :], in_=self.A16[:, o, 0, :]), reads=["A16"], writes=["s5_AAt"])
                P.op("dve", lambda e: e.tensor_scalar(out=ABt[:, 0, :], in0=self.A16[:, o, 1, :], scalar1=-1.0, scalar2=None, op0=ALU.mult), reads=["A16"], writes=["s5_ABt"])
                P.op("dve", lambda e: e.tensor_copy(out=ABt[:, 1, :], in_=self.A16[:, o, 1, :]), reads=["A16"], writes=["s5_ABt"])
                for c in range(128):
                    cur, nxt = XS[c % 2], XS[(c + 1) % 2]
                    ck, nk = f"s5_XS{c % 2}", f"s5_XS{(c + 1) % 2}"
                    sk = ("s5_Sc", c)
                    P.op("dve", lambda e, cur=cur: e.tensor_tensor(out=U[:], in0=cur[:], in1=AAt[:], op=ALU.mult), reads=[ck, "s5_AAt"], writes=["s5_U"])
                    P.op("dve", lambda e, cur=cur: e.tensor_tensor(out=V[:], in0=cur[:, ::-1, :], in1=ABt[:], op=ALU.mult), reads=[ck, "s5_ABt"], writes=["s5_V"])
                    P.op("dve", lambda e: e.tensor_add(out=U[:], in0=U[:], in1=V[:]), reads=["s5_U", "s5_V"], writes=["s5_U"])
                    P.op("dve", lambda e, c=c, nxt=nxt: e.tensor_tensor(out=nxt[:], in0=U[:], in1=SRI[:, :, :, c], op=ALU.add), reads=["s5_U", sk], writes=[nk])
                    P.op("act", lambda e, c=c, cur=cur: e.copy(out=SRI[:, :, :, c], in_=cur[:]), reads=[ck], writes=[sk])
                P.barrier()
                if self.stop == "C":
                    return
                Lt = [self.sb(st, f"s5_Lt{i}", [128, GB, 512], BF16) for i in range(2)]
                WXt = [self.sb(st, f"s5_WXt{i}", [128, GB, 1024], BF16) for i in range(2)]
                YS = [self.sb(st, f"s5_YS{i}", [128, 2, 128], F32) for i in range(2)]
                YF = self.sb(st, "s5_YF", [128, 16, 32], F32)
                G1 = self.sb(st, "s5_G1", [128, 16, 32], F32)
                for g in range(64):
                    gl, b = g % GB, g // GB
                    pb = b % 2
                    if gl == 0:
                        P.dma("sp", lambda e, b=b, pb=pb: e.dma_start(out=Lt[pb][:], in_=self.LS[o, b * GB:(b + 1) * GB].rearrange("g p f -> p g f")),
                              reads=[("LS", o)], writes=[f"s5_Lt{pb}"])
                        P.dma("sp", lambda e, b=b, pb=pb: e.dma_start(out=WXt[pb][:], in_=self.WXS[o, b * GB:(b + 1) * GB].rearrange("g p f -> p g f")),
                              reads=[("WXS", o)], writes=[f"s5_WXt{pb}"])
                    slot = g % 2
                    make_ug(g, slot)
                    bank = 2 + g % 2
                    for mh in range(2):
                        outp = ps[bank][:, mh * 128:(mh + 1) * 128]
                        ms = slice(mh * 128, (mh + 1) * 128)
                        mms = []
                        for kh in range(2):
                            mms.append((Lt[pb][:, gl, kh * 256 + mh * 128:kh * 256 + (mh + 1) * 128], UG[slot][:, kh, :], [f"s5_Lt{pb}", f"s5_UG{slot}"]))
                        for ri in range(2):
                            c0 = ri * 256 + mh * 128
                            mms.append((WXt[pb][:, gl, c0:c0 + 128], SRI[:, ri, g, :], [f"s5_WXt{pb}", ("s5_S", g, 0), ("s5_S", g, 1)]))
                        for ri in range(2):
                            c0 = 512 + ri * 256 + mh * 128
                            mms.append((WXt[pb][:, gl, c0:c0 + 128], SRI[:, ri, g, ::-1], [f"s5_WXt{pb}", ("s5_S", g, 0), ("s5_S", g, 1)]))
                        for n_, (lt, rh, rk) in enumerate(mms):
                            P.op("pe", lambda e, lt=lt, rh=rh, outp=outp, n_=n_: e.matmul(outp, lt, rh, start=(n_ == 0), stop=(n_ == len(mms) - 1)),
                                 reads=rk, writes=[("ps", bank)])
                    ys = YS[g % 2]
                    self.evac("act", ys[:], ps[bank][:, 0:256].rearrange("p (m c) -> p m c", m=2), [("ps", bank)], [f"s5_YS{g % 2}"])
                    tb = 4 + g % 2
                    for mh in range(2):
                        P.op("pe", lambda e, mh=mh, ys=ys, tb=tb: e.transpose(ps[tb][:, mh * 128:(mh + 1) * 128], ys[:, mh, :], identf[:]),
                             reads=[f"s5_YS{g % 2}", "s5_identf"], writes=[("ps", tb)])
                    g4 = g % 2
                    self.evac("dve", YF[:, :, g4 * 16:(g4 + 1) * 16], ps[tb][:, 0:256].rearrange("p (i c) -> p i c", c=16), [("ps", tb)], ["s5_YF"])
                    if g4 == 1:
                        f0 = (g // 2) * 32
                        P.op("dve", lambda e: e.tensor_mul(out=G1[:], in0=YF[:], in1=YF[:]), reads=["s5_YF"], writes=["s5_G1"])
                        P.op("dve", lambda e: e.tensor_scalar(out=G1[:], in0=G1[:], scalar1=0.044715, scalar2=1.0, op0=ALU.mult, op1=ALU.add),
                             reads=["s5_G1"], writes=["s5_G1"])
                        P.op("dve", lambda e: e.tensor_mul(out=G1[:], in0=G1[:], in1=YF[:]), reads=["s5_G1", "s5_YF"], writes=["s5_G1"])
                        P.op("act", lambda e: e.activation(out=G1[:], in_=G1[:], func=AF.Sigmoid, scale=1.5957691216057308), reads=["s5_G1"], writes=["s5_G1"])
                        P.op("dve", lambda e, f0=f0: e.tensor_mul(out=Z[:, :, f0:f0 + 32], in0=G1[:], in1=YF[:]),
                             reads=["s5_G1", "s5_YF"], writes=[zk(j) for j in range(16)])
                P.barrier()
            if self.stop == "D":
                return
            with contextlib.ExitStack() as st:
                Wg = load_w(st, "s5_Wg", self.w_glu[o], 1024)
                Wz = load_w(st, "s5_Wz", self.w_in_c[o][:, 1024:2048], 1024)
                bg = self.sb(st, "s5_bg", [128, D], F32)
                P.dma("sp", lambda e: e.dma_start(out=bg[:], in_=self.b_glu[o].partition_broadcast(128)), writes=["s5_bg"])
                hgT = [self.sb(st, f"s5_hgT{i}", [128, 8, 128], BF16) for i in range(2)]
                gl_t = [self.sb(st, f"s5_gl{i}", [128, 512], F32) for i in range(2)]
                sz_t = [self.sb(st, f"s5_sz{i}", [128, 512], F32) for i in range(2)]
                for j in range(16):
                    hk = f"s5_hgT{j % 2}"
                    self.transpose8(Z[:, j, :], hgT[j % 2], [zk(j)], hk, bank=j % 2)
                    for half in range(2):
                        hsl = slice(half * 512, (half + 1) * 512)
                        s2 = (j * 2 + half) % 2
                        bg_ = 2 + s2
                        bz_ = 4 + s2
                        for kt in range(8):
                            P.op("pe", lambda e, kt=kt, j=j, hsl=hsl, bg_=bg_: e.matmul(ps[bg_][:], hgT[j % 2][:, kt, :], Wg[:, kt, hsl], start=(kt == 0), stop=(kt == 7)),
                                 reads=[hk, "s5_Wg"], writes=[("ps", bg_)])
                        for kt in range(8):
                            P.op("pe", lambda e, kt=kt, j=j, hsl=hsl, bz_=bz_: e.matmul(ps[bz_][:], hT[:, kt, j::16], Wz[:, kt, hsl], start=(kt == 0), stop=(kt == 7)),
                                 reads=["hT", "s5_Wz"], writes=[("ps", bz_)])
                        glt, szt = gl_t[s2], sz_t[s2]
                        P.op("dve", lambda e, glt=glt, bg_=bg_, hsl=hsl: e.tensor_tensor(out=glt[:], in0=ps[bg_][:], in1=bg[:, hsl], op=ALU.add),
                             reads=[("ps", bg_), "s5_bg"], writes=[f"s5_gl{s2}"])
                        P.op("act", lambda e, glt=glt: e.activation(out=glt[:], in_=glt[:], func=AF.Sigmoid), reads=[f"s5_gl{s2}"], writes=[f"s5_gl{s2}"])
                        P.op("act", lambda e, szt=szt, bz_=bz_: e.activation(out=szt[:], in_=ps[bz_][:], func=AF.Silu), reads=[("ps", bz_)], writes=[f"s5_sz{s2}"])
                        P.op("dve", lambda e, glt=glt, szt=szt: e.tensor_mul(out=glt[:], in0=glt[:], in1=szt[:]), reads=[f"s5_gl{s2}", f"s5_sz{s2}"], writes=[f"s5_gl{s2}"])
                        P.op("dve", lambda e, glt=glt, j=j, hsl=hsl: e.tensor_mul(out=Z[:, j, hsl], in0=Z[:, j, hsl], in1=glt[:]),
                             reads=[f"s5_gl{s2}", zk(j), hk], writes=[zk(j)])
                P.barrier()
            if self.stop == "E":
                return
            with contextlib.ExitStack() as st:
                Wo = load_w(st, "s5_Wo", self.w_out_c[o], 1024)
                mT = [self.sb(st, f"s5_mT{i}", [128, 8, 128], BF16) for i in range(2)]
                for j in range(16):
                    mk = f"s5_mT{j % 2}"
                    self.transpose8(Z[:, j, :], mT[j % 2], [zk(j)], mk, bank=j % 2)
                    for half in range(2):
                        hsl = slice(half * 512, (half + 1) * 512)
                        bank = 2 + (j * 2 + half) % 4
                        for kt in range(8):
                            P.op("pe", lambda e, kt=kt, j=j, hsl=hsl, bank=bank: e.matmul(ps[bank][:], mT[j % 2][:, kt, :], Wo[:, kt, hsl], start=(kt == 0), stop=(kt == 7)),
                                 reads=[mk, "s5_Wo"], writes=[("ps", bank)])
                        P.op("dve", lambda e, j=j, hsl=hsl, bank=bank: e.tensor_tensor(out=x_sb[:, j, hsl], in0=x_sb[:, j, hsl], in1=ps[bank][:], op=ALU.add),
                             reads=[("ps", bank), ("x", j)], writes=[("x", j)])
                P.barrier()

    def evac(self, eng, out, in_, reads, writes):
        if eng == "act":
            self.P.op("act", lambda e: e.copy(out=out, in_=in_), reads=reads, writes=writes)
        else:
            self.P.op(eng, lambda e: e.tensor_copy(out=out, in_=in_), reads=reads, writes=writes)

    def transpose8(self, src, dst, rkeys, wkey, bank):
        P = self.P
        pv = self.ps[bank][:].bitcast(BF16)
        for kt in range(8):
            P.op("pe", lambda e, kt=kt: e.transpose(pv[:, kt * 128:(kt + 1) * 128], src[:, kt * 128:(kt + 1) * 128], self.identb[:]),
                 reads=list(rkeys) + ["identb"], writes=[("ps", bank)])
        self.evac("act", dst[:], pv.rearrange("p (k c) -> p k c", k=8), [("ps", bank)], [wkey])


def _att_bidx():
    import jax
    import jax.numpy as jnp
    k = np.arange(128)[:, None, None]
    d = np.arange(-1, 2)[None, :, None]
    q = np.arange(128)[None, None, :]
    rel = (k - q - 128 * d).astype(np.int32)
    half, max_exact = 16, 8
    try:
        cpu = jax.devices("cpu")[0]
        with jax.default_device(cpu):
            r = jnp.asarray(rel)
            n = jnp.abs(r)
            large = max_exact + (jnp.log(jnp.maximum(n, 1).astype(jnp.float32) / max_exact)
                                 / math.log(128 / max_exact) * (half - max_exact)).astype(jnp.int32)
            large = jnp.minimum(large, half - 1)
            out = jnp.where(r > 0, half, 0) + jnp.where(n < max_exact, n, large)
            return np.asarray(out).astype(np.float32)
    except Exception:
        n = np.abs(rel)
        large = max_exact + (np.log(np.maximum(n, 1).astype(np.float32) / np.float32(max_exact))
                             / np.float32(math.log(128 / max_exact)) * np.float32(half - max_exact)).astype(np.int32)
        large = np.minimum(large, half - 1)
        return (np.where(rel > 0, half, 0) + np.where(n < max_exact, n, large)).astype(np.float32)


class ABMixin:
    def ab_declare(self):
        for nm, shp in [("rel_bias", [32, 8]), ("w_in_ab", [2, 1024, IN_AB]), ("conv_w", [2, 5, 1280]), ("conv_b", [2, 1280]),
                        ("ssd_dt_bias", [2, 2, 16]), ("ssd_A_log", [2, 2, 16]), ("ssd_D", [2, 16]), ("ssd_norm_w", [2, 1024]),
                        ("diff_lambda", [2, 4, 64]), ("diff_subln_w", [2, 128]), ("w_out_ab", [2, 2048, 1024])]:
            setattr(self, nm, self.din(nm, shp))
        self.att_bidx = self.din("att_bidx", [128, 3, 128])
        self.ssd_declare()

    def ab_prologue(self):
        nc, P = self.nc, self.P
        st0 = self._st_small
        self.ssd_prologue()
        self.RB = self.sb(st0, "RB", [128, 256], F32)
        self.NBS = nc.dram_tensor("att_NBS", [128, 2 * 8 * 384], BF16, kind="Internal").ap()
        self.NEGLAM = self.sb(st0, "NEGLAM", [128, 2], F32)
        self.SLW = self.sb(st0, "SLW", [128, 2, 128], F32)
        P.dma("sp", lambda e: e.dma_start(out=self.RB[:], in_=self.rel_bias.rearrange("b h -> (b h)").partition_broadcast(128)), writes=["RB"])
        with contextlib.ExitStack() as st:
            sb = lambda n, s, d=F32: self.sb(st, f"abp_{n}", s, d)
            bidx = sb("bidx", [128, 384]); msk = sb("msk", [128, 384]); acc = sb("acc", [128, 8, 384]); t32 = sb("t32", [128, 8, 384])
            self.NB = sb("NBp", [128, 2, 8, 384], BF16)
            P.dma("sp", lambda e: e.dma_start(out=bidx[:], in_=self.att_bidx.rearrange("k d q -> k (d q)")), writes=["abp_bidx"])
            P.op("dve", lambda e: e.memset(acc[:], 0.0), writes=["abp_acc"])
            for b in range(32):
                P.op("dve", lambda e, b=b: e.tensor_scalar(out=msk[:], in0=bidx[:], scalar1=float(b), scalar2=None, op0=ALU.is_equal),
                     reads=["abp_bidx"], writes=["abp_msk"])
                for h in range(8):
                    P.op("dve", lambda e, b=b, h=h: e.scalar_tensor_tensor(out=acc[:, h, :], in0=msk[:], scalar=self.RB[:, b * 8 + h:b * 8 + h + 1],
                                                                            in1=acc[:, h, :], op0=ALU.mult, op1=ALU.add),
                         reads=["abp_msk", "RB", "abp_acc"], writes=["abp_acc"])
            P.op("dve", lambda e: e.tensor_scalar(out=acc[:], in0=acc[:], scalar1=8.0, scalar2=None, op0=ALU.mult), reads=["abp_acc"], writes=["abp_acc"])
            P.op("dve", lambda e: e.tensor_copy(out=self.NB[:, 0], in_=acc[:]), reads=["abp_acc"], writes=["NB"])
            P.op("dve", lambda e: e.tensor_copy(out=t32[:], in_=self.NB[:, 0]), reads=["NB"], writes=["abp_t32"])
            P.op("dve", lambda e: e.tensor_sub(out=t32[:], in0=acc[:], in1=t32[:]), reads=["abp_acc", "abp_t32"], writes=["abp_t32"])
            P.op("dve", lambda e: e.tensor_copy(out=self.NB[:, 1], in_=t32[:]), reads=["abp_t32"], writes=["NB"])
            P.dma("sp", lambda e: e.dma_start(out=self.NBS, in_=self.NB[:].rearrange("p a h f -> p (a h f)")), reads=["NB"], writes=["NBS"])
            dl = sb("dl", [128, 2, 4, 64]); pj = sb("pj", [128, 64]); pr = sb("pr", [128, 4])
            P.dma("sp", lambda e: e.dma_start(out=dl[:].rearrange("p e a d -> p (e a d)"),
                                              in_=self.diff_lambda.rearrange("e a d -> (e a d)").partition_broadcast(128)), writes=["abp_dl"])
            sw = sb("sw", [128, 2, 128])
            P.dma("sp", lambda e: e.dma_start(out=sw[:].rearrange("p e d -> p (e d)"),
                                              in_=self.diff_subln_w.rearrange("e d -> (e d)").partition_broadcast(128)), writes=["abp_sw"])
            for e_ in range(2):
                lam_init = 0.8 - 0.6 * math.exp(-0.3 * (2 * e_))
                for a in range(2):
                    P.op("dve", lambda e, e_=e_, a=a: e.scalar_tensor_tensor(out=pj[:], in0=dl[:, e_, 2 * a, :], scalar=1.0, in1=dl[:, e_, 2 * a + 1, :],
                                                                              op0=ALU.mult, op1=ALU.mult, accum_out=pr[:, e_ * 2 + a:e_ * 2 + a + 1]),
                         reads=["abp_dl"], writes=["abp_pj", "abp_pr"])
                P.op("act", lambda e, e_=e_: e.activation(out=pr[:, e_ * 2:e_ * 2 + 2], in_=pr[:, e_ * 2:e_ * 2 + 2], func=AF.Exp), reads=["abp_pr"], writes=["abp_pr"])
                P.op("dve", lambda e, e_=e_: e.tensor_sub(out=self.NEGLAM[:, e_:e_ + 1], in0=pr[:, e_ * 2 + 1:e_ * 2 + 2], in1=pr[:, e_ * 2:e_ * 2 + 1]),
                     reads=["abp_pr"], writes=["NEGLAM"])
                P.op("dve", lambda e, e_=e_, lam_init=lam_init: e.tensor_scalar(out=self.NEGLAM[:, e_:e_ + 1], in0=self.NEGLAM[:, e_:e_ + 1],
                                                                                  scalar1=-lam_init, scalar2=None, op0=ALU.add),
                     reads=["NEGLAM"], writes=["NEGLAM"])
                P.op("dve", lambda e, e_=e_, lam_init=lam_init: e.tensor_scalar(out=self.SLW[:, e_, :], in0=sw[:, e_, :], scalar1=1.0 - lam_init,
                                                                                  scalar2=None, op0=ALU.mult),
                     reads=["abp_sw"], writes=["SLW"])
            P.barrier()

    def ab_layer(self, e_, lnum):
        P = self.P
        with contextlib.ExitStack() as stL:
            yT = self.sb(stL, "ab_yT", [128, 8, L], BF16)
            if "nossd" not in self.stop:
                self.ssd_part(e_, yT)
                self.out_proj_half(e_, yT, 0, rstd=True)
            if "noatt" not in self.stop:
                self.att_part(e_, yT)
                self.out_proj_half(e_, yT, 1, rstd=False)

    def out_proj_half(self, e_, yT, half_idx, rstd):
        P = self.P
        ps, x_sb = self.ps, self.x_sb
        with contextlib.ExitStack() as st:
            Wo = self.sb(st, "ab_Wo", [128, 8, 1024], BF16)
            v = self.w_out_ab[e_][half_idx * 1024:(half_idx + 1) * 1024, :].rearrange("(kt p) f -> p kt f", p=128)
            for q in range(4):
                P.dma("pool", lambda e, q=q: e.dma_start(out=Wo[:, 2 * q:2 * q + 2, :], in_=v[:, 2 * q:2 * q + 2, :]), writes=["ab_Wo"])
            if rstd:
                for kt in range(8):
                    P.op("dve", lambda e, kt=kt: e.tensor_scalar(out=Wo[:, kt, :], in0=Wo[:, kt, :], scalar1=self.snw[:, e_, kt:kt + 1],
                                                                                       scalar2=None, op0=ALU.mult), reads=["ab_Wo", "snw"], writes=["ab_Wo"])
            for j in range(16):
                for hf in range(2):
                    hsl = slice(hf * 512, (hf + 1) * 512)
                    bank = (j * 2 + hf) % 4
                    for kt in range(8):
                        P.op("pe", lambda e, kt=kt, j=j, hsl=hsl, bank=bank: e.matmul(ps[bank][:], yT[:, kt, j::16], Wo[:, kt, hsl], start=(kt == 0), stop=(kt == 7)),
                             reads=["ab_yT", "ab_Wo"], writes=[("ps", bank)])
                    if rstd:
                        P.op("dve", lambda e, j=j, hsl=hsl, bank=bank: e.scalar_tensor_tensor(
                            out=x_sb[:, j, hsl], in0=ps[bank][:], scalar=self.rs_ssd[:, j:j + 1], in1=x_sb[:, j, hsl], op0=ALU.mult, op1=ALU.add),
                            reads=[("ps", bank), ("x", j), "rs_ssd"], writes=[("x", j)])
                    else:
                        P.op("dve", lambda e, j=j, hsl=hsl, bank=bank: e.tensor_tensor(out=x_sb[:, j, hsl], in0=x_sb[:, j, hsl], in1=ps[bank][:], op=ALU.add),
                             reads=[("ps", bank), ("x", j)], writes=[("x", j)])
            P.barrier()

    def att_part(self, e_, yT):
        P = self.P
        ps, hT, idb = self.ps, self.hT, self.identb
        c_q = 1024 + 1280 + 32
        with contextlib.ExitStack() as st:
            sb = lambda n, s, d=BF16: self.sb(st, f"at_{n}", s, d)
            W = [sb(f"W{i}", [128, 4, 8, 128]) for i in range(2)]
            qTc = [sb(f"qT{c}", [128, L]) for c in range(2)]
            kT = sb("kT", [128, L])
            Vaug = sb("Vaug", [128, 16, 130])
            SG = sb("SG", [128, 16, 128])
            PT = [sb(f"PT{i}", [128, 512]) for i in range(4)]
            accs = sb("accs", [128, 8, 129], F32)
            rr = sb("rr", [128, 8], F32)
            o4 = sb("o4", [128, 4, 128], F32)
            t4 = sb("t4", [128, 4, 128], F32)
            ssq = sb("ssq", [128, 4], F32)
            y4 = sb("y4", [128, 4, 128], BF16)
            NBt = sb("NB", [128, 2, 8, 384])
            P.dma("sp", lambda e: e.dma_start(out=NBt[:].rearrange("p a h f -> p (a h f)"), in_=self.NBS), reads=["NBS"], writes=["at_NB"])
            P.op("dve", lambda e: e.memset(qTc[0][:], 0.0), writes=["at_qT0"])
            P.op("dve", lambda e: e.memset(qTc[1][:], 0.0), writes=["at_qT1"])
            P.op("dve", lambda e: e.memset(Vaug[:], 1.0), writes=["at_Vaug"])

            def load_w(h):
                wt = W[h % 2]
                for s_ in range(4):
                    c0 = c_q + s_ * 1024 + h * 128
                    P.dma("pool", lambda e, s_=s_, c0=c0, wt=wt: e.dma_start(
                        out=wt[:, s_], in_=self.w_in_ab[e_][:, c0:c0 + 128].rearrange("(kt p) f -> p kt f", p=128)), writes=[f"at_W{h % 2}"])

            load_w(0)
            for h in range(8):
                wt = W[h % 2]
                wk = f"at_W{h % 2}"
                if h + 1 < 8:
                    load_w(h + 1)
                for s_ in range(2):
                    for qc in range(4):
                        for kt in range(8):
                            P.op("pe", lambda e, s_=s_, qc=qc, kt=kt, wt=wt: e.matmul(ps[7][:], wt[:, s_, kt, :], hT[:, kt, qc * 512:(qc + 1) * 512],
                                                                                     start=(kt == 0), stop=(kt == 7)),
                                 reads=[wk, "hT"], writes=[("ps", 7)])
                        csl = slice(qc * 512, (qc + 1) * 512)
                        if s_ == 0:
                            self.evac("dve", qTc[0][0:64, csl], ps[7][0:64, :], [("ps", 7)], ["at_qT0"])
                            self.evac("dve", qTc[1][64:128, csl], ps[7][64:128, :], [("ps", 7)], ["at_qT1"])
                        else:
                            self.evac("dve", kT[:, csl], ps[7][:], [("ps", 7)], ["at_kT"])
                for s_ in (2, 3):
                    for kq in range(4):
                        for kl in range(4):
                            kb = kq * 4 + kl
                            for kt in range(8):
                                P.op("pe", lambda e, s_=s_, kb=kb, kl=kl, kt=kt, wt=wt: e.matmul(
                                    ps[7][:, kl * 128:(kl + 1) * 128], hT[:, kt, kb * 128:(kb + 1) * 128], wt[:, s_, kt, :], start=(kt == 0), stop=(kt == 7)),
                                    reads=[wk, "hT"], writes=[("ps", 7)])
                        src = ps[7][:].rearrange("p (k f) -> p k f", k=4)
                        if s_ == 2:
                            self.evac("dve", Vaug[:, kq * 4:(kq + 1) * 4, 0:128], src, [("ps", 7)], ["at_Vaug"])
                        else:
                            P.op("act", lambda e, kq=kq, src=src: e.activation(out=SG[:, kq * 4:(kq + 1) * 4, :], in_=src, func=AF.Silu),
                                 reads=[("ps", 7)], writes=["at_SG"])
                iters = [(qc, kb, comp) for qc in range(4) for kb in range(16) for comp in range(2)]
                n_it = len(iters)
                PRE = 3

                def emit_S(it):
                    qc, kb, comp = iters[it]
                    qsl = slice(qc * 512, (qc + 1) * 512)
                    sbank = it % 4
                    near = [ql for ql in range(4) if abs(qc * 4 + ql - kb) <= 1]
                    P.op("pe", lambda e: e.matmul(ps[sbank][:], kT[:, kb * 128:(kb + 1) * 128], qTc[comp][:, qsl], start=True, stop=(len(near) == 0)),
                         reads=["at_kT", f"at_qT{comp}"], writes=[("ps", sbank)])
                    for ni, ql in enumerate(near):
                        d = qc * 4 + ql - kb
                        for hl in range(2):
                            last = (ni == len(near) - 1) and hl == 1
                            P.op("pe", lambda e, ql=ql, d=d, hl=hl, last=last: e.matmul(
                                ps[sbank][:, ql * 128:(ql + 1) * 128], idb[:], NBt[:, hl, h, (d + 1) * 128:(d + 2) * 128], start=False, stop=last),
                                reads=["identb", "at_NB"], writes=[("ps", sbank)])

                def emit_exp(it):
                    qc, kb, comp = iters[it]
                    sbank = it % 4
                    pt = PT[it % 4]
                    ptk = f"at_PT{it % 4}"
                    segs = []
                    for ql in range(4):
                        d = qc * 4 + ql - kb
                        ty = 0 if abs(d) <= 1 else (1 if d <= -2 else 2)
                        if segs and segs[-1][0] == ty:
                            segs[-1][2] = ql + 1
                        else:
                            segs.append([ty, ql, ql + 1])
                    for (ty, a_, b_) in segs:
                        csl = slice(a_ * 128, b_ * 128)
                        if ty == 0:
                            P.op("act", lambda e, csl=csl: e.activation(out=pt[:, csl], in_=ps[sbank][:, csl], func=AF.Exp, scale=0.125),
                                 reads=[("ps", sbank)], writes=[ptk])
                        else:
                            col = (31 if ty == 1 else 15) * 8 + h
                            P.op("act", lambda e, csl=csl, col=col: e.activation(
                                out=pt[:, csl], in_=ps[sbank][:, csl], func=AF.Exp, bias=self.RB[:, col:col + 1], scale=0.125),
                                reads=[("ps", sbank), "RB"], writes=[ptk])

                def emit_PV(it):
                    qc, kb, comp = iters[it]
                    pt = PT[it % 4]
                    ptk = f"at_PT{it % 4}"
                    for ql in range(4):
                        a_ = comp * 4 + ql
                        abank = 4 + a_ // 3
                        off = (a_ % 3) * 129
                        P.op("pe", lambda e, ql=ql, abank=abank, off=off: e.matmul(
                            ps[abank][:, off:off + 129], pt[:, ql * 128:(ql + 1) * 128], Vaug[:, kb, 0:129], start=(kb == 0 and off == 0), stop=(kb == 15)),
                            reads=[ptk, "at_Vaug"], writes=[("ps", abank)])

                def stage_A(qc):
                    for bk, (a0, a1) in enumerate([(0, 3), (3, 6), (6, 8)]):
                        P.op("dve", lambda e, bk=bk, a0=a0, a1=a1: e.tensor_copy(
                            out=accs[:, a0:a1, :], in_=ps[4 + bk][:, 0:(a1 - a0) * 129].rearrange("p (a f) -> p a f", f=129)),
                            reads=[("ps", 4 + bk)], writes=["at_accs"])
                    P.op("dve", lambda e: e.reciprocal(out=rr[:], in_=accs[:, :, 128]), reads=["at_accs"], writes=["at_rr"])
                    P.op("dve", lambda e: e.tensor_scalar(out=rr[:, 4:8], in0=rr[:, 4:8], scalar1=self.NEGLAM[:, e_:e_ + 1], scalar2=None, op0=ALU.mult),
                         reads=["at_rr", "NEGLAM"], writes=["at_rr"])
                    P.op("dve", lambda e: e.tensor_tensor(out=o4[:], in0=accs[:, 0:4, 0:128], in1=rr[:, 0:4].unsqueeze(2).to_broadcast([128, 4, 128]), op=ALU.mult),
                         reads=["at_accs", "at_rr"], writes=["at_o4"])
                    P.op("dve", lambda e: e.tensor_tensor(out=t4[:], in0=accs[:, 4:8, 0:128], in1=rr[:, 4:8].unsqueeze(2).to_broadcast([128, 4, 128]), op=ALU.mult),
                         reads=["at_accs", "at_rr"], writes=["at_t4"])
                    P.op("dve", lambda e: e.tensor_add(out=o4[:], in0=o4[:], in1=t4[:]), reads=["at_o4", "at_t4"], writes=["at_o4"])
                    P.op("dve", lambda e: e.tensor_mul(out=t4[:], in0=o4[:], in1=o4[:]), reads=["at_o4", "at_t4"], writes=["at_t4"])
                    P.op("dve", lambda e: e.tensor_reduce(out=ssq[:], in_=t4[:], axis=AX.X, op=ALU.add), reads=["at_t4"], writes=["at_ssq"])

                def stage_B(qc):
                    P.op("act", lambda e: e.activation(out=ssq[:], in_=ssq[:], func=AF.Ln, bias=self.epsc[:, 0:1], scale=1.0 / 128), reads=["at_ssq", "epsc"], writes=["at_ssq"])
                    P.op("act", lambda e: e.activation(out=ssq[:], in_=ssq[:], func=AF.Exp, scale=-0.5), reads=["at_ssq"], writes=["at_ssq"])

                def stage_C(qc):
                    P.op("dve", lambda e: e.tensor_tensor(out=o4[:], in0=o4[:], in1=ssq[:].unsqueeze(2).to_broadcast([128, 4, 128]), op=ALU.mult),
                         reads=["at_o4", "at_ssq"], writes=["at_o4"])
                    P.op("dve", lambda e: e.tensor_tensor(out=o4[:], in0=o4[:], in1=self.SLW[:, e_, :].unsqueeze(1).to_broadcast([128, 4, 128]), op=ALU.mult),
                         reads=["at_o4", "SLW"], writes=["at_o4"])
                    P.op("dve", lambda e: e.tensor_tensor(out=y4[:], in0=o4[:], in1=SG[:, qc * 4:(qc + 1) * 4, :], op=ALU.mult),
                         reads=["at_o4", "at_SG"], writes=["at_y4"])
                    pv = ps[7][:].bitcast(BF16)
                    for ql in range(4):
                        P.op("pe", lambda e, ql=ql, pv=pv: e.transpose(pv[:, ql * 128:(ql + 1) * 128], y4[:, ql, :], idb[:]), reads=["at_y4", "identb"], writes=[("ps", 7)])
                    self.evac("dve", yT[:, h, qc * 512:(qc + 1) * 512], pv[:, 0:512], [("ps", 7)], ["ab_yT"])

                deferred = {}
                for i0 in range(min(PRE, n_it)):
                    emit_S(i0)
                for it in range(n_it):
                    qc, kb, comp = iters[it]
                    if it + PRE < n_it:
                        emit_S(it + PRE)
                    emit_exp(it)
                    emit_PV(it)
                    for fn_ in deferred.pop(it, []):
                        fn_()
                    if kb == 15 and comp == 1:
                        stage_A(qc)
                        if qc < 3:
                            deferred.setdefault(it + 5, []).append(lambda qc=qc: stage_B(qc))
                            deferred.setdefault(it + 10, []).append(lambda qc=qc: stage_C(qc))
                        else:
                            stage_B(qc)
                            stage_C(qc)
            P.barrier()

def _ssd_masks():
    k = np.arange(128)[:, None]
    i = np.arange(128)[None, :]
    m = np.zeros((6, 128, 128), np.float32)
    m[0] = (k <= i)
    m[1] = (k > i)
    m[2] = (k >= i)
    m[3] = (k < i)
    m[4] = (i >= k)
    m[5] = (k >= i)
    return m


class SSDMixin:
    def ssd_declare(self):
        self.ssd_masks_d = self.din("ssd_masks", [6, 128, 128])

    def ssd_prologue(self):
        P = self.P
        st0 = self._st_small
        self.MK = self.sb(st0, "MK", [128, 6, 128], F32)
        self.MKb = self.sb(st0, "MKb", [128, 4, 128], BF16)
        self.onesf = self.sb(st0, "onesf", [128, 128], F32)
        self.onesb = self.sb(st0, "onesb", [128, 1], BF16)
        self.convw = self.sb(st0, "convw", [128, 2, 10, 5], F32)
        self.convb = self.sb(st0, "convb", [128, 2, 10], F32)
        self.dtb = self.sb(st0, "dtb", [128, 2, 32], F32)
        self.Aneg = self.sb(st0, "Aneg", [128, 2, 32], F32)
        self.Dsk = self.sb(st0, "Dsk", [128, 2, 16], F32)
        self.snw = self.sb(st0, "snw", [128, 2, 8], F32)
        self.rs_ssd = self.sb(st0, "rs_ssd", [128, 16], F32)
        P.dma("sp", lambda e: e.dma_start(out=self.MK[:], in_=self.ssd_masks_d.rearrange("m k i -> k m i")), writes=["MK"])
        P.dma("pool", lambda e: e.dma_start(out=self.MKb[:], in_=self.ssd_masks_d[0:4].rearrange("m k i -> k m i")), writes=["MKb"])
        P.op("dve", lambda e: e.memset(self.onesf[:], 1.0), writes=["onesf"])
        P.op("dve", lambda e: e.memset(self.onesb[:], 1.0), writes=["onesb"])
        for e_ in range(2):
            for k in range(5):
                P.dma("sp", lambda e, e_=e_, k=k: e.dma_start(out=self.convw[:, e_, :, k], in_=self.conv_w[e_, k].rearrange("(f p) -> p f", p=128),
                                                              allow_slow_non_contiguous=True), writes=["convw"])
            P.dma("sp", lambda e, e_=e_: e.dma_start(out=self.convb[:, e_], in_=self.conv_b[e_].rearrange("(f p) -> p f", p=128),
                                                     allow_slow_non_contiguous=True), writes=["convb"])
            P.dma("sp", lambda e, e_=e_: e.dma_start(out=self.snw[:, e_], in_=self.ssd_norm_w[e_].rearrange("(f p) -> p f", p=128),
                                                     allow_slow_non_contiguous=True), writes=["snw"])
        P.dma("sp", lambda e: e.dma_start(out=self.dtb[:].rearrange("p e h -> p (e h)"),
                                          in_=self.ssd_dt_bias.rearrange("e d h -> (e d h)").partition_broadcast(128)), writes=["dtb"])
        P.dma("sp", lambda e: e.dma_start(out=self.Aneg[:].rearrange("p e h -> p (e h)"),
                                          in_=self.ssd_A_log.rearrange("e d h -> (e d h)").partition_broadcast(128)), writes=["Aneg"])
        P.dma("sp", lambda e: e.dma_start(out=self.Dsk[:].rearrange("p e h -> p (e h)"),
                                          in_=self.ssd_D.rearrange("e h -> (e h)").partition_broadcast(128)), writes=["Dsk"])
        P.op("act", lambda e: e.activation(out=self.Aneg[:], in_=self.Aneg[:], func=AF.Exp), reads=["Aneg"], writes=["Aneg"])
        P.op("dve", lambda e: e.tensor_scalar(out=self.Aneg[:], in0=self.Aneg[:], scalar1=-1.0, scalar2=None, op0=ALU.mult), reads=["Aneg"], writes=["Aneg"])

    def conv_ft(self, e_, ft, Wt, wk, XC, dst_fn, post_fn=None):
        P = self.P
        ps, hT = self.ps, self.hT
        for qc in range(4):
            bank = 5 + qc % 2
            for kt in range(8):
                P.op("pe", lambda e, qc=qc, kt=kt, bank=bank: e.matmul(ps[bank][:], Wt[:, kt, :], hT[:, kt, qc * 512:(qc + 1) * 512], start=(kt == 0), stop=(kt == 7)),
                     reads=[wk, "hT"], writes=[("ps", bank)])
            self.evac("act" if qc % 2 == 0 else "dve", XC[:, 2 + qc * 512:2 + (qc + 1) * 512], ps[bank][:], [("ps", bank)], [("sd_XC", qc)])
        acc = self.sd_acc
        cw = self.convw[:, e_, ft, :]
        for qc in range(4):
            rk = [("sd_XC", q) for q in range(max(0, qc - 1), min(4, qc + 2))] + ["sd_XCh"]
            o = qc * 512
            P.op("dve", lambda e, o=o: e.tensor_scalar(out=acc[:], in0=XC[:, o:o + 512], scalar1=cw[:, 0:1], scalar2=self.convb[:, e_, ft:ft + 1], op0=ALU.mult, op1=ALU.add),
                 reads=rk + ["convw", "convb"], writes=["sd_acc"])
            for k in range(1, 5):
                P.op("dve", lambda e, k=k, o=o: e.scalar_tensor_tensor(out=acc[:], in0=XC[:, o + k:o + k + 512], scalar=cw[:, k:k + 1], in1=acc[:], op0=ALU.mult, op1=ALU.add),
                     reads=rk + ["convw", "sd_acc"], writes=["sd_acc"])
            dst, dkey = dst_fn(qc)
            P.op("act", lambda e, dst=dst: e.activation(out=dst, in_=acc[:], func=AF.Silu), reads=["sd_acc"], writes=[dkey])
            if post_fn is not None:
                post_fn(qc)

    NH = 4

    def ssd_part(self, e_, yT):
        P = self.P
        ps, hT, idb = self.ps, self.hT, self.identb
        c_x, c_dt = 1024, 2304
        MK, MKb = self.MK, self.MKb
        NH = self.NH
        NW = NH * 64
        NF = NW // 128
        NP = 16 // NH
        with contextlib.ExitStack() as st:
            sb = lambda n, s, d=BF16: self.sb(st, f"sd_{n}", s, d)
            CTf = sb("CTf", [128, L]); BTm = sb("BTm", [128, L])
            Wdt = sb("Wdt", [128, 8, 32])
            DT = sb("DT", [128, 16, 2 * NH], F32); ECS = sb("ECS", [128, 16, 2 * NH], F32); DDT = sb("DDT", [128, 16, 2 * NH], F32)
            CDX = sb("CDX", [128, 16, 2 * NH], F32); AT = sb("AT", [128, 16, 2 * NH], F32)
            AH = sb("AH", [128, 16, 2 * NH]); AL = sb("AL", [128, 16, 2 * NH]); t16 = sb("t16", [128, 16, 2 * NH], F32)
            XS = sb("XS", [128, 16, NW])
            Wz = sb("Wz", [128, 8, NW])
            Hst = sb("Hst", [128, NW], F32)
            Hbf_l = [sb(f"Hbf{i}", [128, NW]) for i in range(2)]; HPt_l = [sb(f"HPt{i}", [128, NW]) for i in range(2)]
            RH_l = [sb(f"RH{i}", [128, NH, 128]) for i in range(2)]; RL_l = [sb(f"RL{i}", [128, NH, 128]) for i in range(2)]
            Et_l = [sb(f"E{i}", [128, NH, 128]) for i in range(2)]; CBM_l = [sb(f"CBM{i}", [128, 128]) for i in range(2)]
            XDT_l = [sb(f"XDT{i}", [128, NH, 64]) for i in range(2)]; XDD_l = [sb(f"XDD{i}", [128, NH, 64]) for i in range(2)]
            Yacc_l = [sb(f"Yacc{i}", [128, NH, 64], F32) for i in range(2)]; tY_l = [sb(f"tY{i}", [128, NH, 64], F32) for i in range(2)]
            SZ4 = sb("SZ4", [128, 4, NW]); GT_l = [sb(f"GT{i}", [128, NW]) for i in range(2)]
            Btk_l = [sb(f"Btk{i}", [128, 128]) for i in range(2)]
            XC = sb("XC", [128, L + 4], F32)
            self.sd_acc = sb("acc", [128, 512], F32)
            Wt = [sb(f"Wt{i}", [128, 8, 128]) for i in range(2)]
            xtf = [sb(f"xtf{i}", [128, 512]) for i in range(2)]
            P.dma("pool", lambda e: e.dma_start(out=Wdt[:], in_=self.w_in_ab[e_][:, c_dt:c_dt + 32].rearrange("(kt p) f -> p kt f", p=128)), writes=["sd_Wdt"])
            P.op("dve", lambda e: e.memset(XC[:, 0:2], 0.0), writes=["sd_XCh"])
            P.op("dve", lambda e: e.memset(XC[:, L + 2:L + 4], 0.0), writes=["sd_XCh"])
            h3 = lambda T: T.rearrange("p (h q) -> p h q", h=NH)

            def load_wt(ft, slot):
                c0 = c_x + ft * 128
                P.dma("pool", lambda e: e.dma_start(out=Wt[slot][:], in_=self.w_in_ab[e_][:, c0:c0 + 128].rearrange("(kt p) f -> p kt f", p=128)),
                      writes=[f"sd_Wt{slot}"])

            load_wt(9, 0)
            self.conv_ft(e_, 9, Wt[0], "sd_Wt0", XC, lambda qc: (CTf[:, qc * 512:(qc + 1) * 512], "sd_CTf"))
            DTf = sb("DTf", [128, 16, 32], F32); ATf = sb("ATf", [128, 16, 32], F32); ECSf = sb("ECSf", [128, 16, 32], F32)
            DDTf = sb("DDTf", [128, 16, 32], F32); CDXf = sb("CDXf", [128, 16, 32], F32)
            for ci in range(16):
                for kt in range(8):
                    P.op("pe", lambda e, ci=ci, kt=kt: e.matmul(ps[0][:, ci * 32:(ci + 1) * 32], hT[:, kt, ci * 128:(ci + 1) * 128], Wdt[:, kt, :],
                                                                 start=(kt == 0), stop=(kt == 7)), reads=["hT", "sd_Wdt"], writes=[("ps", 0)])
            bfull = lambda G: G[:, e_, :].unsqueeze(1).to_broadcast([128, 16, 32])
            P.op("dve", lambda e: e.tensor_tensor(out=DTf[:], in0=ps[0][:].rearrange("p (c h) -> p c h", c=16), in1=bfull(self.dtb), op=ALU.add),
                 reads=[("ps", 0), "dtb"], writes=["sd_DTf"])
            P.op("act", lambda e: e.activation(out=DTf[:], in_=DTf[:], func=AF.Exp), reads=["sd_DTf"], writes=["sd_DTf"])
            P.op("act", lambda e: e.activation(out=DTf[:], in_=DTf[:], func=AF.Ln, bias=1.0), reads=["sd_DTf"], writes=["sd_DTf"])
            P.op("dve", lambda e: e.tensor_tensor(out=ATf[:], in0=DTf[:], in1=bfull(self.Aneg), op=ALU.mult), reads=["sd_DTf", "Aneg"], writes=["sd_ATf"])
            for ci in range(16):
                for d in range(2):
                    rhs = ATf[:, ci, d * 16:(d + 1) * 16]
                    osl = slice(ci * 32 + d * 16, ci * 32 + d * 16 + 16)
                    m_ecs = MK[:, 0 if d == 0 else 2, :]
                    m_dte = MK[:, 1 if d == 0 else 3, :]
                    P.op("pe", lambda e, rhs=rhs, osl=osl, m_ecs=m_ecs: e.matmul(ps[1][:, osl], m_ecs, rhs, start=True, stop=True),
                         reads=["MK", "sd_ATf"], writes=[("ps", 1)])
                    P.op("pe", lambda e, rhs=rhs, osl=osl, m_dte=m_dte: e.matmul(ps[2][:, osl], m_dte, rhs, start=True, stop=True),
                         reads=["MK", "sd_ATf"], writes=[("ps", 2)])
                    P.op("pe", lambda e, rhs=rhs, osl=osl: e.matmul(ps[3][:, osl], self.onesf[:], rhs, start=True, stop=True),
                         reads=["onesf", "sd_ATf"], writes=[("ps", 3)])
            flf = lambda T: T[:].rearrange("p c h -> p (c h)")
            P.op("act", lambda e: e.activation(out=flf(ECSf), in_=ps[1][:], func=AF.Exp), reads=[("ps", 1)], writes=["sd_ECSf"])
            P.op("act", lambda e: e.activation(out=flf(DDTf), in_=ps[2][:], func=AF.Exp), reads=[("ps", 2)], writes=["sd_DDTf"])
            P.op("act", lambda e: e.activation(out=flf(CDXf), in_=ps[3][:], func=AF.Exp), reads=[("ps", 3)], writes=["sd_CDXf"])
            P.op("dve", lambda e: e.tensor_mul(out=DDTf[:], in0=DDTf[:], in1=DTf[:]), reads=["sd_DDTf", "sd_DTf"], writes=["sd_DDTf"])
            for pi in range(NP):
                gi = (pi * NH) // 8
                h0 = pi * NH
                hsl = slice(h0, h0 + NH)
                ho = slice((1 - gi) * 64, (2 - gi) * 64)
                k0 = pi * NF
                if (pi * NH) % 8 == 0:
                    load_wt(8, 1)
                    self.conv_ft(e_, 8, Wt[1], "sd_Wt1", XC, lambda qc: (BTm[:, qc * 512:(qc + 1) * 512], "sd_BTm"))
                    P.op("dve", lambda e, ho=ho: e.memset(BTm[ho, :], 0.0), reads=[], writes=["sd_BTm"])
                for fl in range(NF):
                    ft = pi * NF + fl
                    load_wt(ft, fl % 2)

                    def dst_fn(qc):
                        return (xtf[qc % 2][:], f"sd_xtf{qc % 2}")

                    def post_fn(qc, fl=fl):
                        pv = ps[7][:].bitcast(BF16)
                        for cl in range(4):
                            P.op("pe", lambda e, cl=cl, pv=pv: e.transpose(pv[:, cl * 128:(cl + 1) * 128], xtf[qc % 2][:, cl * 128:(cl + 1) * 128], idb[:]),
                                 reads=[f"sd_xtf{qc % 2}", "identb"], writes=[("ps", 7)])
                        self.evac("dve", XS[:, qc * 4:(qc + 1) * 4, fl * 128:(fl + 1) * 128], pv[:, 0:512].rearrange("p (c f) -> p c f", c=4),
                                  [("ps", 7)], ["sd_XS"])

                    self.conv_ft(e_, ft, Wt[fl % 2], f"sd_Wt{fl % 2}", XC, dst_fn, post_fn)
                for q in range(4):
                    P.dma("pool", lambda e, q=q, pi=pi: e.dma_start(
                        out=Wz[:, 2 * q:2 * q + 2, :], in_=self.w_in_ab[e_][:, pi * NW:(pi + 1) * NW].rearrange("(kt p) f -> p kt f", p=128)[:, 2 * q:2 * q + 2, :]),
                        writes=["sd_Wz"])
                v4 = lambda T: T[:].rearrange("p c (d h) -> p c d h", d=2)
                f4 = lambda T: T[:].rearrange("p c (d h) -> p c d h", d=2)[:, :, :, hsl]
                for (dst, src, dk, sk_) in ((DT, DTf, "sd_DT", "sd_DTf"), (AT, ATf, "sd_AT", "sd_ATf"), (ECS, ECSf, "sd_ECS", "sd_ECSf"),
                                            (DDT, DDTf, "sd_DDT", "sd_DDTf"), (CDX, CDXf, "sd_CDX", "sd_CDXf")):
                    P.op("dve", lambda e, dst=dst, src=src: e.tensor_copy(out=v4(dst), in_=f4(src)), reads=[sk_], writes=[dk])
                P.op("dve", lambda e: e.tensor_copy(out=AH[:], in_=AT[:]), reads=["sd_AT"], writes=["sd_AH"])
                P.op("dve", lambda e: e.tensor_copy(out=t16[:], in_=AH[:]), reads=["sd_AH"], writes=["sd_t16"])
                P.op("dve", lambda e: e.tensor_sub(out=t16[:], in0=AT[:], in1=t16[:]), reads=["sd_AT", "sd_t16"], writes=["sd_t16"])
                P.op("dve", lambda e: e.tensor_copy(out=AL[:], in_=t16[:]), reads=["sd_t16"], writes=["sd_AL"])

                def btok(ci):
                    pv = ps[7][:].bitcast(BF16)
                    Btk = Btk_l[ci % 2]
                    P.op("pe", lambda e: e.transpose(pv[:, 512:640], BTm[:, ci * 128:(ci + 1) * 128], idb[:]), reads=["sd_BTm", "identb"], writes=[("ps", 7)])
                    self.evac("act", Btk[:], pv[:, 512:640], [("ps", 7)], [f"sd_Btk{ci % 2}"])

                def xscale(dst, dkey, src_scale, ci, d):
                    P.op("dve", lambda e: e.tensor_tensor(out=dst[:], in0=h3(XS[:, ci, :]),
                                                           in1=src_scale[:, ci, d * NH:(d + 1) * NH].unsqueeze(2).to_broadcast([128, NH, 64]), op=ALU.mult),
                         reads=["sd_XS", "sd_DT", "sd_DDT"], writes=[dkey])

                def state_prep(ci, d, bank=5):
                    XDD = XDD_l[ci % 2]
                    Btk = Btk_l[ci % 2]
                    xscale(XDD, f"sd_XDD{ci % 2}", DDT, ci, d)
                    btok(ci)
                    P.op("pe", lambda e: e.matmul(ps[bank][:, 0:NW], Btk[:], XDD[:].rearrange("p h q -> p (h q)"), start=True, stop=True),
                         reads=[f"sd_Btk{ci % 2}", f"sd_XDD{ci % 2}"], writes=[("ps", bank)])

                def state_apply(ci, d, bank=5):
                    P.op("dve", lambda e: e.tensor_tensor(out=h3(Hst[:]), in0=h3(Hst[:]),
                                                          in1=CDX[:, ci, d * NH:(d + 1) * NH].unsqueeze(2).to_broadcast([128, NH, 64]), op=ALU.mult),
                         reads=["sd_Hst", "sd_CDX"], writes=["sd_Hst"])
                    P.op("dve", lambda e: e.tensor_tensor(out=Hst[:], in0=Hst[:], in1=ps[bank][:, 0:NW], op=ALU.add), reads=["sd_Hst", ("ps", bank)], writes=["sd_Hst"])

                hp_view = lambda ci: yT[:, k0:k0 + NF, ci * 128:(ci + 1) * 128]
                hkey = lambda ci: ("yTr", pi, ci)
                P.op("dve", lambda e: e.memset(Hst[:], 0.0), writes=["sd_Hst"])
                state_prep(15, 1, 5)
                for ci in range(15, -1, -1):
                    if ci - 1 > 0:
                        state_prep(ci - 1, 1, 5 + (ci % 2))
                    P.op("act", lambda e, ci=ci: e.copy(out=hp_view(ci), in_=Hst[:].rearrange("p (a b) -> p a b", a=NF)), reads=["sd_Hst"], writes=[hkey(ci)])
                    if ci > 0:
                        state_apply(ci, 1, 5 + ((ci + 1) % 2))
                P.op("dve", lambda e: e.memset(Hst[:], 0.0), writes=["sd_Hst"])
                dkeys = lambda d: (f"sd_RH{d}", f"sd_RL{d}", f"sd_E{d}", f"sd_CBM{d}", f"sd_XDT{d}", f"sd_tY{d}")

                def stage_A(ci):
                    for d in range(2):
                        RH, RL = RH_l[d], RL_l[d]
                        kRH, kRL = dkeys(d)[0:2]
                        mrow = MKb[:, 0 if d == 0 else 2, :]
                        for (R, A_, rk) in ((RH, AH, kRH), (RL, AL, kRL)):
                            P.op("dve", lambda e, R=R, A_=A_, d=d, mrow=mrow: e.tensor_tensor(
                                out=R[:], in0=A_[:, ci, d * NH:(d + 1) * NH].unsqueeze(2).to_broadcast([128, NH, 128]),
                                in1=mrow.unsqueeze(1).to_broadcast([128, NH, 128]), op=ALU.mult), reads=["sd_AH", "sd_AL", "MKb"], writes=[rk])

                def stage_Z(ci):
                    csl = slice(ci * 128, (ci + 1) * 128)
                    cp = ci % 2
                    P.op("pe", lambda e: e.matmul(ps[2][:, 0:128], BTm[:, csl], CTf[:, csl], start=True, stop=True), reads=["sd_BTm", "sd_CTf"], writes=[("ps", 2)])
                    if ci % 4 == 0:
                        for c4 in range(4):
                            csl4 = slice((ci + c4) * 128, (ci + c4 + 1) * 128)
                            for kt in range(8):
                                P.op("pe", lambda e, kt=kt, csl4=csl4: e.matmul(ps[6][:, 0:NW], hT[:, kt, csl4], Wz[:, kt, :], start=(kt == 0), stop=(kt == 7)),
                                     reads=["hT", "sd_Wz"], writes=[("ps", 6)])
                            P.op("act", lambda e, c4=c4: e.activation(out=SZ4[:, c4, :], in_=ps[6][:, 0:NW], func=AF.Silu), reads=[("ps", 6)], writes=[("sd_SZ", c4)])
                    P.op("act", lambda e: e.copy(out=Hbf_l[cp][:], in_=Hst[:]), reads=["sd_Hst"], writes=[f"sd_Hbf{cp}"])
                    P.op("act", lambda e: e.copy(out=HPt_l[cp][:].rearrange("p (a b) -> p a b", a=NF), in_=hp_view(ci)), reads=[hkey(ci)], writes=[f"sd_HPt{cp}"])

                def stage_B(ci):
                    for d in range(2):
                        RH, RL, Et = RH_l[d], RL_l[d], Et_l[d]
                        kRH, kRL, kE = dkeys(d)[0:3]
                        mlhs = MKb[:, 1 if d == 0 else 3, :]
                        P.op("pe", lambda e: e.matmul(ps[d][:, 0:NH * 128], mlhs, RH[:].rearrange("p h i -> p (h i)"), start=True, stop=False),
                             reads=["MKb", kRH], writes=[("ps", d)])
                        P.op("pe", lambda e: e.matmul(ps[d][:, 0:NH * 128], mlhs, RL[:].rearrange("p h i -> p (h i)"), start=False, stop=True),
                             reads=["MKb", kRL], writes=[("ps", d)])
                        P.op("act", lambda e: e.activation(out=Et[:].rearrange("p h i -> p (h i)"), in_=ps[d][:, 0:NH * 128], func=AF.Exp),
                             reads=[("ps", d)], writes=[kE])

                def stage_C(ci):
                    for d in range(2):
                        Et, CBM, XDT = Et_l[d], CBM_l[d], XDT_l[d]
                        kE, kCBM, kXDT = dkeys(d)[2:5]
                        xscale(XDT, kXDT, DT, ci, d)
                        P.op("dve", lambda e: e.tensor_tensor(out=CBM[:], in0=ps[2][:, 0:128], in1=MK[:, 4 + d, :], op=ALU.mult), reads=[("ps", 2), "MK"], writes=[kCBM])
                        P.op("dve", lambda e: e.tensor_tensor(out=Et[:], in0=Et[:], in1=CBM[:].unsqueeze(1).to_broadcast([128, NH, 128]), op=ALU.mult),
                             reads=[kE, kCBM], writes=[kE])

                def stage_D(ci):
                    csl = slice(ci * 128, (ci + 1) * 128)
                    cp = ci % 2
                    for d in range(2):
                        Et, XDT = Et_l[d], XDT_l[d]
                        kE, kXDT = dkeys(d)[2], dkeys(d)[4]
                        bk = 3 + d
                        for h in range(NH):
                            P.op("pe", lambda e, h=h: e.matmul(ps[bk][:, h * 64:(h + 1) * 64], Et[:, h, :], XDT[:, h, :], start=True, stop=True),
                                 reads=[kE, kXDT], writes=[("ps", bk)])
                        hsrc, hk = (Hbf_l[cp], f"sd_Hbf{cp}") if d == 0 else (HPt_l[cp], f"sd_HPt{cp}")
                        P.op("pe", lambda e: e.matmul(ps[bk][:, NW:2 * NW], CTf[:, csl], hsrc[:], start=True, stop=True), reads=["sd_CTf", hk], writes=[("ps", bk)])

                def stage_E(ci):
                    cp = ci % 2
                    Yacc, kYacc = Yacc_l[cp], f"sd_Yacc{cp}"
                    for d in range(2):
                        tY, ktY = tY_l[d], dkeys(d)[5]
                        bk = 3 + d
                        P.op("dve", lambda e: e.tensor_tensor(out=tY[:], in0=h3(ps[bk][:, NW:2 * NW]),
                                                              in1=ECS[:, ci, d * NH:(d + 1) * NH].unsqueeze(2).to_broadcast([128, NH, 64]), op=ALU.mult),
                             reads=[("ps", bk), "sd_ECS"], writes=[ktY])
                        if d == 0:
                            P.op("dve", lambda e: e.tensor_tensor(out=Yacc[:], in0=tY[:], in1=h3(ps[bk][:, 0:NW]), op=ALU.add),
                                 reads=[ktY, ("ps", bk)], writes=[kYacc])
                        else:
                            P.op("dve", lambda e: e.tensor_tensor(out=tY[:], in0=tY[:], in1=h3(ps[bk][:, 0:NW]), op=ALU.add),
                                 reads=[ktY, ("ps", bk)], writes=[ktY])
                            P.op("dve", lambda e: e.tensor_add(out=Yacc[:], in0=Yacc[:], in1=tY[:]), reads=[ktY, kYacc], writes=[kYacc])
                    tY = tY_l[0]
                    P.op("dve", lambda e: e.tensor_tensor(out=tY[:], in0=h3(XS[:, ci, :]),
                                                          in1=self.Dsk[:, e_, hsl].unsqueeze(2).to_broadcast([128, NH, 64]), op=ALU.mult),
                         reads=["sd_XS", "Dsk"], writes=["sd_tY0"])
                    P.op("dve", lambda e: e.tensor_add(out=Yacc[:], in0=Yacc[:], in1=tY[:]), reads=["sd_tY0", kYacc], writes=[kYacc])

                def stage_T(ci):
                    csl = slice(ci * 128, (ci + 1) * 128)
                    cp = ci % 2
                    Yacc, GT = Yacc_l[cp], GT_l[cp]
                    kYacc, kSZ, kGT, kHPt = f"sd_Yacc{cp}", ("sd_SZ", ci % 4), f"sd_GT{cp}", f"sd_HPt{cp}"
                    P.op("dve", lambda e: e.tensor_tensor(out=GT[:], in0=Yacc[:].rearrange("p h q -> p (h q)"), in1=SZ4[:, ci % 4, :], op=ALU.mult),
                         reads=[kYacc, kSZ], writes=[kGT])
                    pv = ps[7][:].bitcast(BF16)
                    for fl in range(NF):
                        P.op("pe", lambda e, fl=fl: e.transpose(pv[:, fl * 128:(fl + 1) * 128], GT[:, fl * 128:(fl + 1) * 128], idb[:]),
                             reads=[kGT, "identb"], writes=[("ps", 7)])
                    self.evac("act", yT[:, k0:k0 + NF, csl], pv[:, 0:NF * 128].rearrange("p (f t) -> p f t", f=NF), [("ps", 7), kHPt], [hkey(ci)])

                stage_A(0)
                for ci in range(16):
                    stage_Z(ci)
                    stage_B(ci)
                    if ci < 15:
                        state_prep(ci, 0, 5)
                    if ci + 1 < 16:
                        stage_A(ci + 1)
                    stage_C(ci)
                    stage_D(ci)
                    if ci < 15:
                        state_apply(ci, 0, 5)
                    stage_E(ci)
                    stage_T(ci)
            P.barrier()
        with contextlib.ExitStack() as st3:
            SQ = [self.sb(st3, f"sd_SQ{i}", [128, L], BF16) for i in range(2)]
            for kt in range(8):
                sq = SQ[kt % 2]
                P.op("dve", lambda e, kt=kt, sq=sq: e.tensor_tensor(out=sq[:], in0=yT[:, kt, :], in1=yT[:, kt, :], op=ALU.mult),
                     reads=["ab_yT"], writes=[f"sd_SQ{kt % 2}"])
                for j in range(16):
                    P.op("pe", lambda e, j=j, kt=kt, sq=sq: e.matmul(ps[0][:, j:j + 1], sq[:, j::16], self.onesb[:], start=(kt == 0 and j == 0), stop=(kt == 7)),
                         reads=[f"sd_SQ{kt % 2}", "onesb"], writes=[("ps", 0)])
            P.op("act", lambda e: e.activation(out=self.rs_ssd[:], in_=ps[0][:, 0:16], func=AF.Ln, bias=self.epsc[:, 0:1], scale=1.0 / 1024),
                 reads=[("ps", 0), "epsc"], writes=["rs_ssd"])
            P.op("act", lambda e: e.activation(out=self.rs_ssd[:], in_=self.rs_ssd[:], func=AF.Exp, scale=-0.5), reads=["rs_ssd"], writes=["rs_ssd"])
            P.barrier()


class Builder(S5Mixin, ABMixin, SSDMixin, BuilderBase):
    pass
```

```python
import contextlib
import math
import numpy as np
import concourse.bass as bass
import concourse.mybir as mybir
from concourse.bass_utils import run_bass_kernel_spmd

F32 = mybir.dt.float32
BF16 = mybir.dt.bfloat16
AF = mybir.ActivationFunctionType
ALU = mybir.AluOpType
AX = mybir.AxisListType

L = 2048
D = 1024
EPS = 1e-5
IN_AB = 6432
N_CORES = 8


class _Rec:
    def __init__(self):
        self.call = None

    def __getattr__(self, name):
        def f(*a, **k):
            self.call = (name, a, k)
            return self
        return f


def _record(fn):
    if fn is None:
        return None
    r = _Rec()
    fn(r)
    assert r.call is not None
    return r.call


class Prog:
    ENGS = ("pe", "act", "dve", "pool", "sp")

    def __init__(self, nc, same_engine_sync=True):
        self.nc = nc
        self.same = same_engine_sync
        self.streams = {e: [] for e in self.ENGS}
        self.cnt = {e: 0 for e in self.ENGS}
        self.dcnt = {e: 0 for e in self.ENGS}
        self.seen = {e: {} for e in self.ENGS}
        self.lastw = {}
        self.readers = {}
        self.n_ops = 0

    def _deps(self, eng, reads, writes):
        ev = []
        for k in reads:
            w = self.lastw.get(k)
            if w is not None:
                ev.append(w)
        for k in writes:
            w = self.lastw.get(k)
            if w is not None:
                ev.append(w)
            ev.extend(self.readers.get(k, ()))
        waits = {}
        seen = self.seen[eng]
        own = "c_" + eng
        for (s, v) in ev:
            if s == own and (eng == "pe" or not self.same):
                continue
            if seen.get(s, 0) >= v:
                continue
            if waits.get(s, 0) < v:
                waits[s] = v
        for s, v in waits.items():
            seen[s] = v
        return list(waits.items())

    def _commit(self, tok, reads, writes):
        for k in reads:
            self.readers.setdefault(k, []).append(tok)
        for k in writes:
            self.lastw[k] = tok
            self.readers[k] = []

    def op(self, eng, fn, reads=(), writes=()):
        waits = self._deps(eng, reads, writes)
        self.cnt[eng] += 1
        tok = ("c_" + eng, self.cnt[eng])
        self.streams[eng].append((waits, _record(fn), tok[0], 1))
        self._commit(tok, reads, writes)
        self.n_ops += 1

    NDS = 16

    def dma(self, eng, fn, reads=(), writes=()):
        waits = self._deps(eng, reads, writes)
        n = self.dcnt[eng]
        self.dcnt[eng] += 1
        r = n % self.NDS
        k = n // self.NDS + 1
        sname = f"d_{eng}_{r}"
        if k > 1 and self.seen[eng].get(sname, 0) < 16 * (k - 1):
            waits = [w for w in waits if w[0] != sname] + [(sname, 16 * (k - 1))]
            self.seen[eng][sname] = 16 * (k - 1)
        tok = (sname, 16 * k)
        self.streams[eng].append((waits, _record(fn), sname, 16))
        self._commit(tok, reads, writes)
        self.n_ops += 1

    def all_tokens(self):
        fin = []
        for e in self.ENGS:
            if self.cnt[e]:
                fin.append(("c_" + e, self.cnt[e]))
            n = self.dcnt[e]
            for r in range(min(n, self.NDS)):
                k = (n - 1 - r) // self.NDS + 1
                fin.append((f"d_{e}_{r}", 16 * k))
        return fin

    def barrier(self):
        fin = self.all_tokens()
        for e in self.ENGS:
            waits = []
            for (s, v) in fin:
                if self.seen[e].get(s, 0) >= v:
                    continue
                if s == "c_" + e and e == "pe":
                    continue
                waits.append((s, v))
                self.seen[e][s] = v
            if waits:
                self.streams[e].append((waits, None, None, 0))

    def emit(self, final_wait_eng="sp"):
        nc = self.nc
        names = ["c_" + e for e in self.ENGS]
        for e in self.ENGS:
            names += [f"d_{e}_{r}" for r in range(min(self.dcnt[e], self.NDS))]
        fin = self.all_tokens()
        with contextlib.ExitStack() as st:
            sems = {n: st.enter_context(nc.semaphore(n)) for n in names}
            block = st.enter_context(nc.Block())

            def mk(ename):
                def body(engine):
                    for (waits, fn, sname, inc) in self.streams[ename]:
                        for (s, v) in waits:
                            engine.wait_ge(sems[s], v)
                        if fn is not None:
                            name, a, k = fn
                            ins = getattr(engine, name)(*a, **k)
                            ins.then_inc(sems[sname], inc)
                    if ename == final_wait_eng:
                        for (s, v) in fin:
                            engine.wait_ge(sems[s], v)
                return body

            block.tensor(mk("pe"))
            block.scalar(mk("act"))
            block.vector(mk("dve"))
            block.gpsimd(mk("pool"))
            block.sync(mk("sp"))


class BuilderBase:
    def __init__(self, nseq, layers, final_norm=True):
        self.nseq = nseq
        self.layers = layers
        self.final_norm = final_norm
        self.nc = bass.Bass("TRN2", target_bir_lowering=False)
        self.P = Prog(self.nc)
        self.dram = {}
        self.uid = 0
        import os
        self.stop = os.environ.get("K_STOP", "")

    def din(self, name, shape, dt=F32):
        t = self.nc.dram_tensor(name, list(shape), dt, kind="ExternalInput").ap()
        self.dram[name] = t
        return t

    def sb(self, st, name, shape, dt):
        self.uid += 1
        return st.enter_context(self.nc.sbuf_tensor(f"{name}_u{self.uid}", list(shape), dt))

    def build(self):
        nc, P = self.nc, self.P
        ns = self.nseq
        x_all = self.din("x_all", [ns, L, D])
        self.norm_w = self.din("norm_w", [4, D])
        self.final_norm_w = self.din("final_norm_w", [D])
        ident_d = self.din("ident", [128, 128])
        y_all = nc.dram_tensor("y_all", [ns, L, D], F32, kind="ExternalOutput").ap()
        self.declare_weights()
        with contextlib.ExitStack() as st0:
            self._st_small = st0
            self.identb = self.sb(st0, "identb", [128, 128], BF16)
            self.nwT = self.sb(st0, "nwT", [128, 4, 8], F32)
            self.ss = self.sb(st0, "ss", [128, 16], F32)
            self.rstd = self.sb(st0, "rstd", [128, 16], F32)
            self.epsc = self.sb(st0, "epsc", [128, 1], F32)
            self.A16 = self.sb(st0, "A16", [128, 2, 2, 64], F32)
            self.ps = [st0.enter_context(nc.psum_tensor(f"ps{i}", [128, 512], F32)) for i in range(8)]
            P.op("dve", lambda e: e.memset(self.epsc[:], EPS), writes=["epsc"])
            P.dma("pool", lambda e: e.dma_start(out=self.identb[:], in_=ident_d), writes=["identb"])
            P.dma("sp", lambda e: e.dma_start(out=self.nwT[:], in_=self.norm_w.rearrange("l (k p) -> p l k", p=128),
                                              allow_slow_non_contiguous=True), writes=["nw"])
            self.prologue()
            st = st0
            self.x_sb = self.sb(st, "x_sb", [128, 16, D], F32)
            self.hT = self.sb(st, "hT", [128, 8, L], BF16)
            x_sb = self.x_sb
            for s in range(ns):
                xv = x_all[s].rearrange("(c j) d -> c j d", j=16)
                for q in range(4):
                    P.dma("sp", lambda e, q=q, xv=xv: e.dma_start(out=x_sb[:, 4 * q:4 * q + 4, :], in_=xv[:, 4 * q:4 * q + 4, :]),
                          writes=[("x", j) for j in range(4 * q, 4 * q + 4)])
                for (kind, idx, lnum) in self.layers:
                    self.rms_to_hT(lnum)
                    if self.stop in ("pro", "p1", "p2", "p3", "p4", "p5"):
                        continue
                    if kind == "s5":
                        self.s5_layer(idx)
                    elif kind == "ab":
                        self.ab_layer(idx, lnum)
                yv = y_all[s].rearrange("(c j) d -> c j d", j=16)
                with contextlib.ExitStack() as stf:
                    obs = [self.sb(stf, f"ob{i}", [128, D], F32) for i in range(2)]
                    self.junk = self.sb(stf, "junk", [128, D], BF16)
                    self.fnw = self.sb(stf, "fnw", [128, D], F32)
                    P.dma("sp", lambda e: e.dma_start(out=self.fnw[:], in_=self.final_norm_w.partition_broadcast(128)), writes=["nw"])
                    for j in range(16):
                        ob = obs[j % 2]
                        okey = f"ob{j % 2}"
                        if self.final_norm:
                            self.rms_stats(j)
                            P.op("dve", lambda e, j=j, ob=ob: e.scalar_tensor_tensor(
                                out=ob[:], in0=x_sb[:, j, :], scalar=self.rstd[:, j:j + 1], in1=self.fnw[:],
                                op0=ALU.mult, op1=ALU.mult), reads=[("x", j), "rstd", "nw"], writes=[okey])
                        else:
                            P.op("dve", lambda e, j=j, ob=ob: e.tensor_copy(out=ob[:], in_=x_sb[:, j, :]),
                                 reads=[("x", j)], writes=[okey])
                        P.dma("sp", lambda e, j=j, ob=ob, yv=yv: e.dma_start(out=yv[:, j, :], in_=ob[:]),
                              reads=[okey], writes=[("yout", s, j)])
                    P.barrier()
            P.emit()
        return nc

    def declare_weights(self):
        kinds = set(k for (k, _, _) in self.layers)
        self.kinds = kinds
        if "s5" in kinds:
            self.s5_declare()
        if "ab" in kinds:
            self.ab_declare()

    def prologue(self):
        if "s5" in self.kinds:
            for o in sorted(set(i for (k, i, _) in self.layers if k == "s5")):
                self.s5_prologue(o)
        if "ab" in self.kinds:
            self.ab_prologue()

    def rms_stats(self, j):
        P = self.P
        P.op("act", lambda e: e.activation(out=self.junk[:], in_=self.x_sb[:, j, :], func=AF.Square,
                                           accum_out=self.ss[:, j:j + 1]),
             reads=[("x", j)], writes=["junk", "ss"])
        P.op("act", lambda e: e.activation(out=self.rstd[:, j:j + 1], in_=self.ss[:, j:j + 1], func=AF.Sqrt,
                                           bias=self.epsc[:, 0:1], scale=1.0 / D),
             reads=["ss", "epsc"], writes=["rstd"])
        P.op("dve", lambda e: e.reciprocal(out=self.rstd[:, j:j + 1], in_=self.rstd[:, j:j + 1]),
             reads=["rstd"], writes=["rstd"])

    def rms_to_hT(self, lnum):
        P = self.P
        with contextlib.ExitStack() as st:
            hbs = [self.sb(st, f"hb{i}", [128, D], BF16) for i in range(2)]
            self.junk = self.sb(st, "junk", [128, D], BF16)
            for j in range(16):
                self.rms_stats(j)
                hb = hbs[j % 2]
                hk = f"hb{j % 2}"
                P.op("dve", lambda e, j=j, hb=hb: e.tensor_scalar(out=hb[:], in0=self.x_sb[:, j, :], scalar1=self.rstd[:, j:j + 1],
                                                                  scalar2=None, op0=ALU.mult), reads=[("x", j), "rstd"], writes=[hk])
                pt = self.ps[j % 2]
                pk = ("ps", j % 2)
                ptv = pt[:].bitcast(BF16)
                for kt in range(8):
                    P.op("pe", lambda e, kt=kt, hb=hb, ptv=ptv: e.transpose(ptv[:, kt * 128:(kt + 1) * 128],
                                                                             hb[:, kt * 128:(kt + 1) * 128], self.identb[:]),
                         reads=[hk, "identb"], writes=[pk])
                P.op("dve", lambda e, j=j, ptv=ptv: e.tensor_tensor(out=self.hT[:, :, j::16], in0=ptv.rearrange("p (k c) -> p k c", k=8),
                                                                    in1=self.nwT[:, lnum, :].unsqueeze(2).to_broadcast([128, 8, 128]), op=ALU.mult),
                     reads=[pk, "nw"], writes=["hT"])
            P.barrier()


def _consts():
    mf, mb, idm = _s5_masks()
    return {"ident": np.eye(128, dtype=np.float32), "identf": np.eye(128, dtype=np.float32), "att_bidx": _att_bidx(), "ssd_masks": _ssd_masks(),
            "s5_ktab": _s5_ktab(), "s5_mf": mf, "s5_mb": mb, "s5_idm": idm}


_CACHE = {}


def kernel(**inputs):
    xp = np.asarray(inputs["x_prompt"], dtype=np.float32)
    xs = np.asarray(inputs["x_sample"], dtype=np.float32)
    layers = [("ab", 0, 0), ("s5", 0, 1), ("ab", 1, 2), ("s5", 1, 3)]
    b = Builder(6, layers)
    nc = b.build()
    consts = _consts()
    in_maps = []
    for i in range(N_CORES):
        xa = np.concatenate([xp[2 * i:2 * i + 2], xs[4 * i:4 * i + 4]], axis=0)
        m = {"x_all": np.ascontiguousarray(xa)}
        for k in b.dram:
            if k == "x_all":
                continue
            if k in consts:
                m[k] = consts[k]
            else:
                m[k] = np.ascontiguousarray(np.asarray(inputs[k], dtype=np.float32))
        in_maps.append(m)
    res = run_bass_kernel_spmd(nc, in_maps, core_ids=list(range(N_CORES)))
    yp = np.empty_like(xp)
    ys = np.empty_like(xs)
    for i in range(N_CORES):
        y = res.results[i]["y_all"]
        yp[2 * i:2 * i + 2] = y[0:2]
        ys[4 * i:4 * i + 4] = y[2:6]
    return (yp, ys)


TWO_PI = 2.0 * math.pi


def _s5_ktab():
    kt = np.zeros((128, 5, 16), np.float32)
    idx = np.arange(16, dtype=np.float32)
    kt[:64, 0] = -idx
    kt[:64, 1] = 15 - idx
    kt[:64, 2] = idx
    kt[:64, 3] = idx + 1
    kt[64:, 0] = idx
    kt[64:, 1] = idx
    kt[64:, 2] = -idx
    kt[64:, 3] = 16 - idx
    kt[:, 4, 0] = 16
    kt[:, 4, 1] = 1
    return kt


def _s5_masks():
    j = (np.arange(256) // 16)[:, None]
    i = (np.arange(256) // 16)[None, :]
    mf = (i >= j).astype(np.float32).reshape(2, 128, 256)
    mb = (j >= i).astype(np.float32).reshape(2, 128, 256)
    idm = np.eye(256, dtype=np.float32).reshape(2, 128, 256)
    return mf, mb, idm


class S5Mixin:
    GB = 2

    def s5_declare(self):
        nc = self.nc
        for nm, shp in [("w_in_c", [2, 1024, 2048]), ("s5_lambda_re", [2, 2, 64, 64]), ("s5_lambda_im", [2, 2, 64, 64]),
                        ("s5_log_dt", [2, 2, 64]), ("s5_B_re", [2, 64, 64, 16]), ("s5_B_im", [2, 64, 64, 16]),
                        ("s5_C_re", [2, 2, 64, 16, 64]), ("s5_C_im", [2, 2, 64, 16, 64]), ("s5_D", [2, 1024]),
                        ("w_glu", [2, 1024, 1024]), ("b_glu", [2, 1024]), ("w_out_c", [2, 1024, 1024])]:
            setattr(self, nm, self.din(nm, shp))
        self.s5_ktab = self.din("s5_ktab", [128, 5, 16])
        self.s5_mf = self.din("s5_mf", [2, 128, 256])
        self.s5_mb = self.din("s5_mb", [2, 128, 256])
        self.s5_idm = self.din("s5_idm", [2, 128, 256])
        self.identf_d = self.din("identf", [128, 128])
        import os
        kd = "ExternalOutput" if os.environ.get("K_DEBUG") else "Internal"
        self.LS = nc.dram_tensor("s5_LS", [2, 64, 128, 512], BF16, kind=kd).ap()
        self.SWS = nc.dram_tensor("s5_SWS", [2, 64, 128, 512], BF16, kind=kd).ap()
        self.WXS = nc.dram_tensor("s5_WXS", [2, 64, 128, 1024], BF16, kind=kd).ap()

    def s5_prologue(self, o):
        nc, P = self.nc, self.P
        I32 = mybir.dt.int32
        GB = self.GB
        with contextlib.ExitStack() as st:
            sb = lambda n, s, d=F32: self.sb(st, f"s5p_{n}", s, d)
            identf = sb("identf", [128, 128]); ktab = sb("ktab", [128, 5, 16])
            mf = sb("mf", [128, 2, 256]); mb = sb("mb", [128, 2, 256]); idm = sb("idm", [128, 2, 256])
            P.dma("sp", lambda e: e.dma_start(out=identf[:], in_=self.identf_d), writes=["s5p_identf"])
            P.dma("sp", lambda e: e.dma_start(out=ktab[:], in_=self.s5_ktab), writes=["s5p_ktab"])
            P.dma("sp", lambda e: e.dma_start(out=mf[:], in_=self.s5_mf.rearrange("k p f -> p k f")), writes=["s5p_mf"])
            P.dma("sp", lambda e: e.dma_start(out=mb[:], in_=self.s5_mb.rearrange("k p f -> p k f")), writes=["s5p_mb"])
            P.dma("sp", lambda e: e.dma_start(out=idm[:], in_=self.s5_idm.rearrange("k p f -> p k f")), writes=["s5p_idm"])
            raw = sb("raw", [64, 2, 2, 64])
            P.dma("sp", lambda e: e.dma_start(out=raw[:, 0], in_=self.s5_lambda_re[o].rearrange("d g n -> g d n")), writes=["s5p_raw"])
            P.dma("sp", lambda e: e.dma_start(out=raw[:, 1], in_=self.s5_lambda_im[o].rearrange("d g n -> g d n")), writes=["s5p_raw"])
            LR = sb("LR", [128, 64]); LI = sb("LI", [128, 64]); STEP = sb("STEP", [128, 64])
            for ri in range(2):
                P.op("pe", lambda e, ri=ri: e.transpose(self.ps[0][:, ri * 64:(ri + 1) * 64], raw[:, ri, :, :], identf[0:64, 0:64]),
                     reads=["s5p_raw", "s5p_identf"], writes=[("ps", 0)])
            for d in range(2):
                hs = slice(d * 64, (d + 1) * 64)
                P.dma("sp", lambda e, hs=hs, d=d: e.dma_start(out=STEP[hs, :], in_=self.s5_log_dt[o, d].partition_broadcast(64)),
                      writes=["s5p_STEP"])
            P.op("dve", lambda e: e.tensor_copy(out=LR[:], in_=self.ps[0][:, 0:64]), reads=[("ps", 0)], writes=["s5p_LR"])
            P.op("dve", lambda e: e.tensor_copy(out=LI[:], in_=self.ps[0][:, 64:128]), reads=[("ps", 0)], writes=["s5p_LI"])
            P.op("act", lambda e: e.activation(out=STEP[:], in_=STEP[:], func=AF.Exp), reads=["s5p_STEP"], writes=["s5p_STEP"])
            LSt = sb("LSt", [128, 64]); TH = sb("TH", [128, 64])
            P.op("dve", lambda e: e.tensor_mul(out=LSt[:], in0=LR[:], in1=STEP[:]), reads=["s5p_LR", "s5p_STEP"], writes=["s5p_LSt"])
            P.op("dve", lambda e: e.tensor_mul(out=TH[:], in0=LI[:], in1=STEP[:]), reads=["s5p_LI", "s5p_STEP"], writes=["s5p_TH"])
            if self.stop == "p1":
                P.barrier(); return
            ER = [sb(f"ER{t}", [128, 64, 16]) for t in range(5)]
            EI = [sb(f"EI{t}", [128, 64, 16]) for t in range(5)]
            arg = sb("arg", [128, 64, 16]); ti = sb("ti", [128, 64, 16], I32); tf = sb("tf", [128, 64, 16]); tg = sb("tg", [128, 64, 16])
            mag = sb("mag", [128, 64, 16]); sn = sb("sn", [128, 64, 16])
            shp = [128, 64, 16]

            def reduce_turns(key):
                P.op("dve", lambda e: e.tensor_copy(out=ti[:], in_=arg[:]), reads=[key], writes=["s5p_ti"])
                P.op("dve", lambda e: e.tensor_copy(out=tf[:], in_=ti[:]), reads=["s5p_ti"], writes=["s5p_tf"])
                P.op("dve", lambda e: e.tensor_sub(out=arg[:], in0=arg[:], in1=tf[:]), reads=[key, "s5p_tf"], writes=[key])
                P.op("dve", lambda e: e.tensor_scalar(out=tg[:], in0=arg[:], scalar1=0.5, scalar2=None, op0=ALU.is_gt), reads=[key], writes=["s5p_tg"])
                P.op("dve", lambda e: e.tensor_sub(out=arg[:], in0=arg[:], in1=tg[:]), reads=[key, "s5p_tg"], writes=[key])
                P.op("dve", lambda e: e.tensor_scalar(out=tg[:], in0=arg[:], scalar1=-0.5, scalar2=None, op0=ALU.is_lt), reads=[key], writes=["s5p_tg"])
                P.op("dve", lambda e: e.tensor_add(out=arg[:], in0=arg[:], in1=tg[:]), reads=[key, "s5p_tg"], writes=[key])

            for t in range(5):
                kb = ktab[:, t, :].unsqueeze(1).to_broadcast(shp)
                thb = TH[:].unsqueeze(2).to_broadcast(shp)
                lsb = LSt[:].unsqueeze(2).to_broadcast(shp)
                P.op("dve", lambda e, kb=kb, lsb=lsb: e.tensor_tensor(out=mag[:], in0=lsb, in1=kb, op=ALU.mult),
                     reads=["s5p_LSt", "s5p_ktab"], writes=["s5p_mag"])
                P.op("act", lambda e: e.activation(out=mag[:], in_=mag[:], func=AF.Exp), reads=["s5p_mag"], writes=["s5p_mag"])
                for which in range(2):
                    P.op("dve", lambda e, kb=kb, thb=thb: e.tensor_tensor(out=arg[:], in0=thb, in1=kb, op=ALU.mult),
                         reads=["s5p_TH", "s5p_ktab"], writes=["s5p_arg"])
                    P.op("dve", lambda e, which=which: e.tensor_scalar(out=arg[:], in0=arg[:], scalar1=1.0 / TWO_PI,
                                                                        scalar2=0.25 * which, op0=ALU.mult, op1=ALU.add),
                         reads=["s5p_arg"], writes=["s5p_arg"])
                    reduce_turns("s5p_arg")
                    P.op("act", lambda e: e.activation(out=sn[:], in_=arg[:], func=AF.Sin, scale=TWO_PI), reads=["s5p_arg"], writes=["s5p_sn"])
                    dst = EI[t] if which == 0 else ER[t]
                    P.op("dve", lambda e, dst=dst: e.tensor_mul(out=dst[:], in0=mag[:], in1=sn[:]),
                         reads=["s5p_mag", "s5p_sn"], writes=[f"s5p_E{t}{which}"])
            if self.stop == "p2":
                P.barrier(); return
            ekeys = lambda t: [f"s5p_E{t}0", f"s5p_E{t}1"]
            P.op("dve", lambda e: e.tensor_copy(out=self.A16[:, o, 0, :], in_=ER[4][:, :, 0]), reads=ekeys(4), writes=["A16"])
            P.op("dve", lambda e: e.tensor_copy(out=self.A16[:, o, 1, :], in_=EI[4][:, :, 0]), reads=ekeys(4), writes=["A16"])
            nr = sb("nr", [128, 64]); den = sb("den", [128, 64]); t1 = sb("t1", [128, 64]); t2 = sb("t2", [128, 64])
            cr = sb("cr", [128, 64]); ci = sb("ci", [128, 64])
            a1r = ER[4][:, :, 1]; a1i = EI[4][:, :, 1]
            K = ["s5p_nr", "s5p_den", "s5p_t1", "s5p_t2", "s5p_cr", "s5p_ci", "s5p_LR", "s5p_LI"] + ekeys(4)
            ops = [
                lambda e: e.tensor_scalar(out=nr[:], in0=a1r, scalar1=-1.0, scalar2=None, op0=ALU.add),
                lambda e: e.tensor_mul(out=den[:], in0=LR[:], in1=LR[:]),
                lambda e: e.tensor_mul(out=t1[:], in0=LI[:], in1=LI[:]),
                lambda e: e.tensor_add(out=den[:], in0=den[:], in1=t1[:]),
                lambda e: e.reciprocal(out=den[:], in_=den[:]),
                lambda e: e.tensor_mul(out=t1[:], in0=nr[:], in1=LR[:]),
                lambda e: e.tensor_mul(out=t2[:], in0=a1i, in1=LI[:]),
                lambda e: e.tensor_add(out=t1[:], in0=t1[:], in1=t2[:]),
                lambda e: e.tensor_mul(out=cr[:], in0=t1[:], in1=den[:]),
                lambda e: e.tensor_mul(out=t1[:], in0=a1i, in1=LR[:]),
                lambda e: e.tensor_mul(out=t2[:], in0=nr[:], in1=LI[:]),
                lambda e: e.tensor_sub(out=t1[:], in0=t1[:], in1=t2[:]),
                lambda e: e.tensor_mul(out=ci[:], in0=t1[:], in1=den[:]),
            ]
            for f in ops:
                P.op("dve", f, reads=K, writes=K[:6])
            CAr = [sb(f"CAr{t}", shp) for t in range(2)]
            CAi = [sb(f"CAi{t}", shp) for t in range(2)]
            crb = cr[:].unsqueeze(2).to_broadcast(shp)
            cib = ci[:].unsqueeze(2).to_broadcast(shp)
            for t in range(2):
                kk = ["s5p_cr", "s5p_ci", "s5p_arg", "s5p_mag"] + ekeys(t)
                P.op("dve", lambda e, t=t: e.tensor_tensor(out=arg[:], in0=ER[t][:], in1=crb, op=ALU.mult), reads=kk, writes=["s5p_arg"])
                P.op("dve", lambda e, t=t: e.tensor_tensor(out=mag[:], in0=EI[t][:], in1=cib, op=ALU.mult), reads=kk, writes=["s5p_mag"])
                P.op("dve", lambda e, t=t: e.tensor_sub(out=CAr[t][:], in0=arg[:], in1=mag[:]), reads=kk, writes=[f"s5p_CAr{t}"])
                P.op("dve", lambda e, t=t: e.tensor_tensor(out=arg[:], in0=EI[t][:], in1=crb, op=ALU.mult), reads=kk, writes=["s5p_arg"])
                P.op("dve", lambda e, t=t: e.tensor_tensor(out=mag[:], in0=ER[t][:], in1=cib, op=ALU.mult), reads=kk, writes=["s5p_mag"])
                P.op("dve", lambda e, t=t: e.tensor_add(out=CAi[t][:], in0=arg[:], in1=mag[:]), reads=kk, writes=[f"s5p_CAi{t}"])
            if self.stop == "p3":
                P.barrier(); return
            BR = sb("BR", [128, 64, 16]); BI = sb("BI", [128, 64, 16])
            for d in range(2):
                hs = slice(d * 64, (d + 1) * 64)
                P.dma("sp", lambda e, hs=hs: e.dma_start(out=BR[hs], in_=self.s5_B_re[o].rearrange("g n c -> n g c")), writes=["s5p_BR"])
                P.dma("sp", lambda e, hs=hs: e.dma_start(out=BI[hs], in_=self.s5_B_im[o].rearrange("g n c -> n g c")), writes=["s5p_BI"])
            CR = sb("CR", [128, 64, 16]); CI = sb("CI", [128, 64, 16])
            craw = sb("craw", [128, 8, 2, 64])
            for ri, (src, dstc) in enumerate([(self.s5_C_re, CR), (self.s5_C_im, CI)]):
                for d in range(2):
                    P.dma("sp", lambda e, src=src, d=d: e.dma_start(out=craw[:, :, d, :], in_=src[o, d].rearrange("(gt g) c n -> (g c) gt n", g=8)),
                          writes=["s5p_craw"])
                for gt in range(8):
                    P.op("pe", lambda e, gt=gt: e.transpose(self.ps[1 + (gt // 4)][:, (gt % 4) * 128:(gt % 4 + 1) * 128],
                                                             craw[:, gt, :, :], identf[:]),
                         reads=["s5p_craw", "s5p_identf"], writes=[("ps", 1 + gt // 4)])
                for hh in range(2):
                    P.op("dve", lambda e, hh=hh, dstc=dstc: e.tensor_copy(
                        out=dstc[:, hh * 32:(hh + 1) * 32, :], in_=self.ps[1 + hh][:, :].rearrange("p (g c) -> p g c", c=16)),
                        reads=[("ps", 1 + hh)], writes=[f"s5p_C{ri}"])
            ckeys = ["s5p_C0", "s5p_C1"]
            DG = sb("DG", [128, 64])
            for jl in range(8):
                P.dma("sp", lambda e, jl=jl: e.dma_start(out=DG[jl * 16:(jl + 1) * 16, :], in_=self.s5_D[o].rearrange("(g c) -> c g", c=16),
                                                        allow_slow_non_contiguous=True), writes=["s5p_DG"])
            if self.stop == "p4":
                P.barrier(); return
            bshape = [128, GB, 16, 16]
            prod = {nm: sb(nm, bshape) for nm in ["Pr", "Pi", "Sr", "Si", "Qr", "QiN", "Wr", "Wi"]}
            tmpa = [sb(f"tmpa{i}", bshape) for i in range(2)]
            tmpb = [sb(f"tmpb{i}", bshape) for i in range(2)]
            Lout = [sb(f"Lout{i}", [128, GB, 2, 256], BF16) for i in range(2)]
            SWout = [sb(f"SWout{i}", [128, GB, 512], BF16) for i in range(2)]
            WXout = [sb(f"WXout{i}", [128, GB, 2, 2, 256], BF16) for i in range(2)]
            QFB = {nm: sb("FB" + nm, [128, 2, GB, 256]) for nm in ["Qr", "QiN"]}
            for i in range(2):
                P.op("dve", lambda e, i=i: e.memset(WXout[i][:], 0.0), writes=[f"s5p_WXout{i}"])
            for nm in ["Qr", "QiN"]:
                P.op("dve", lambda e, nm=nm: e.memset(QFB[nm][:], 0.0), writes=["s5p_FB" + nm])
            l1 = [sb(f"l1_{i}", [128, 256]) for i in range(2)]
            l2 = [sb(f"l2_{i}", [128, 256]) for i in range(2)]
            nb = 64 // GB
            for b in range(nb):
                g0 = b * GB
                gs = slice(g0, g0 + GB)
                pb = b % 2

                def cprod(eng, slot, outr, outi, Ar, Ai, Br, Bi, a_over_ch, keysA, keysB, neg_i=False):
                    Ab = lambda X: X[:, gs, :].unsqueeze(3).to_broadcast(bshape)
                    Bb = lambda X: X[:, gs, :].unsqueeze(2).to_broadcast(bshape)
                    ta, tb = tmpa[slot], tmpb[slot]
                    rk = keysA + keysB
                    ka, kb_ = f"s5p_tmpa{slot}", f"s5p_tmpb{slot}"
                    P.op(eng, lambda e: e.tensor_tensor(out=ta[:], in0=Ab(Ar), in1=Bb(Br), op=ALU.mult), reads=rk, writes=[ka])
                    P.op(eng, lambda e: e.tensor_tensor(out=tb[:], in0=Ab(Ai), in1=Bb(Bi), op=ALU.mult), reads=rk, writes=[kb_])
                    P.op(eng, lambda e: e.tensor_sub(out=prod[outr][:], in0=ta[:], in1=tb[:]), reads=[ka, kb_], writes=["s5p_" + outr])
                    P.op(eng, lambda e: e.tensor_tensor(out=ta[:], in0=Ab(Ar), in1=Bb(Bi), op=ALU.mult), reads=rk, writes=[ka])
                    P.op(eng, lambda e: e.tensor_tensor(out=tb[:], in0=Ab(Ai), in1=Bb(Br), op=ALU.mult), reads=rk, writes=[kb_])
                    if neg_i:
                        P.op("dve", lambda e: e.scalar_tensor_tensor(out=prod[outi][:], in0=ta[:], scalar=-1.0, in1=tb[:], op0=ALU.mult, op1=ALU.subtract),
                             reads=[ka, kb_], writes=["s5p_" + outi])
                    else:
                        P.op(eng, lambda e: e.tensor_add(out=prod[outi][:], in0=ta[:], in1=tb[:]), reads=[ka, kb_], writes=["s5p_" + outi])

                bk = ["s5p_BR", "s5p_BI"]
                cprod("dve", 0, "Pr", "Pi", CAr[0], CAi[0], BR, BI, True, ["s5p_CAr0", "s5p_CAi0"], bk)
                cprod("dve", 1, "Sr", "Si", CAr[1], CAi[1], BR, BI, True, ["s5p_CAr1", "s5p_CAi1"], bk)
                cprod("dve", 0, "Qr", "QiN", ER[2], EI[2], CR, CI, True, ekeys(2), ckeys, neg_i=True)
                cprod("dve", 1, "Wr", "Wi", ER[3], EI[3], CR, CI, True, ekeys(3), ckeys, neg_i=True)
                for nm in ["Qr", "QiN"]:
                    for d in range(2):
                        hs = slice(d * 64, (d + 1) * 64)
                        P.op("dve", lambda e, nm=nm, d=d, hs=hs: e.tensor_copy(out=QFB[nm][hs, d, :, :],
                                                                                 in_=prod[nm][hs].rearrange("p g i c -> p g (i c)")),
                             reads=["s5p_" + nm], writes=["s5p_FB" + nm])
                import os
                sub = int(os.environ.get("K_SUB", "99"))
                if sub == 0:
                    P.barrier(); return
                for gl in range(GB):
                    g = g0 + gl
                    for kh in range(2):
                        if sub == 1 and (gl, kh) == (0, 1):
                            P.barrier(); return
                        bank = 3 + (gl * 2 + kh) % 4
                        pt = self.ps[bank]
                        for d in range(2):
                            outp = pt[:, d * 256:(d + 1) * 256]
                            P.op("pe", lambda e, d=d, outp=outp, gl=gl, kh=kh: e.matmul(
                                outp, prod["Pr"][:, gl, kh * 8:(kh + 1) * 8, :], QFB["Qr"][:, d, gl, :], start=True, stop=False),
                                reads=["s5p_Pr", "s5p_FBQr"], writes=[("ps", bank)])
                            P.op("pe", lambda e, d=d, outp=outp, gl=gl, kh=kh: e.matmul(
                                outp, prod["Pi"][:, gl, kh * 8:(kh + 1) * 8, :], QFB["QiN"][:, d, gl, :], start=False, stop=True),
                                reads=["s5p_Pi", "s5p_FBQiN"], writes=[("ps", bank)])
                        sl = (gl * 2 + kh) % 2
                        P.op("dve", lambda e, pt=pt, kh=kh, sl=sl: e.tensor_tensor(out=l1[sl][:], in0=pt[:, 0:256], in1=mf[:, kh, :], op=ALU.mult),
                             reads=[("ps", bank), "s5p_mf"], writes=[f"s5p_l1_{sl}"])
                        P.op("dve", lambda e, pt=pt, kh=kh, sl=sl: e.tensor_tensor(out=l2[sl][:], in0=pt[:, 256:512], in1=mb[:, kh, :], op=ALU.mult),
                             reads=[("ps", bank), "s5p_mb"], writes=[f"s5p_l2_{sl}"])
                        P.op("dve", lambda e, sl=sl: e.tensor_add(out=l1[sl][:], in0=l1[sl][:], in1=l2[sl][:]),
                             reads=[f"s5p_l1_{sl}", f"s5p_l2_{sl}"], writes=[f"s5p_l1_{sl}"])
                        P.op("dve", lambda e, sl=sl, kh=kh, g=g, gl=gl: e.scalar_tensor_tensor(
                            out=Lout[pb][:, gl, kh, :], in0=idm[:, kh, :], scalar=DG[:, g:g + 1], in1=l1[sl][:], op0=ALU.mult, op1=ALU.add),
                            reads=[f"s5p_l1_{sl}", "s5p_idm", "s5p_DG"], writes=[f"s5p_Lout{pb}"])
                    if sub == 2:
                        P.barrier(); return
                    for kh in range(2):
                        for ri, nm in enumerate(["Sr", "Si"]):
                            q = kh * 2 + ri
                            P.op("pe", lambda e, q=q, nm=nm, gl=gl, kh=kh: e.transpose(
                                self.ps[7][:, q * 128:(q + 1) * 128], prod[nm][:, gl, kh * 8:(kh + 1) * 8, :], identf[:]),
                                reads=["s5p_" + nm, "s5p_identf"], writes=[("ps", 7)])
                    P.op("act", lambda e, gl=gl: e.copy(out=SWout[pb][:, gl, :], in_=self.ps[7][:]), reads=[("ps", 7)], writes=[f"s5p_SWout{pb}"])
                    for d in range(2):
                        hs = slice(d * 64, (d + 1) * 64)
                        for ri, nm in enumerate(["Wr", "Wi"]):
                            P.op("act", lambda e, gl=gl, d=d, hs=hs, ri=ri, nm=nm: e.copy(
                                out=WXout[pb][hs, gl, d, ri, :], in_=prod[nm][hs, gl].rearrange("p i c -> p (i c)")),
                                reads=["s5p_" + nm], writes=[f"s5p_WXout{pb}"])
                if self.stop == "p5":
                    P.barrier(); return
                P.dma("sp", lambda e, gs=gs, pb=pb: e.dma_start(out=self.LS[o, gs].rearrange("g p f -> p g f"),
                                                                in_=Lout[pb][:].rearrange("p g k f -> p g (k f)")),
                      reads=[f"s5p_Lout{pb}"], writes=[("LS", o)])
                P.dma("sp", lambda e, gs=gs, pb=pb: e.dma_start(out=self.SWS[o, gs].rearrange("g p f -> p g f"), in_=SWout[pb][:]),
                      reads=[f"s5p_SWout{pb}"], writes=[("SWS", o)])
                P.dma("sp", lambda e, gs=gs, pb=pb: e.dma_start(out=self.WXS[o, gs].rearrange("g p f -> p g f"),
                                                                in_=WXout[pb][:].rearrange("p g d k f -> p g (d k f)")),
                      reads=[f"s5p_WXout{pb}"], writes=[("WXS", o)])
            P.barrier()

    def s5_layer(self, o):
        nc, P = self.nc, self.P
        GB = self.GB
        x_sb, hT, ps = self.x_sb, self.hT, self.ps
        idb = self.identb

        def load_w(st, name, src_ap, cols):
            t = self.sb(st, name, [128, 8, cols], BF16)
            v = src_ap.rearrange("(kt p) f -> p kt f", p=128)
            for q in range(4):
                P.dma("pool", lambda e, q=q: e.dma_start(out=t[:, 2 * q:2 * q + 2, :], in_=v[:, 2 * q:2 * q + 2, :]), writes=[name])
            return t

        with contextlib.ExitStack() as stL:
            Z = self.sb(stL, "s5_Z", [128, 16, D], BF16)
            zk = lambda j: ("s5_Z", j)
            with contextlib.ExitStack() as st:
                Wu = load_w(st, "s5_Wu", self.w_in_c[o][:, 0:1024], 1024)
                for j in range(16):
                    for half in range(2):
                        bank = (j * 2 + half) % 4
                        for kt in range(8):
                            P.op("pe", lambda e, j=j, half=half, kt=kt, bank=bank: e.matmul(
                                ps[bank][:], hT[:, kt, j::16], Wu[:, kt, half * 512:(half + 1) * 512], start=(kt == 0), stop=(kt == 7)),
                                reads=["hT", "s5_Wu"], writes=[("ps", bank)])
                        eng = "act" if (j + half) % 2 == 0 else "dve"
                        self.evac(eng, Z[:, j, half * 512:(half + 1) * 512], ps[bank][:], [("ps", bank)], [zk(j)])
                P.barrier()
            if self.stop == "A":
                return
            with contextlib.ExitStack() as st:
                SRI = self.sb(st, "s5_SRI", [128, 2, 64, 128], BF16)
                UG = [self.sb(st, f"s5_UG{i}", [128, 2, 128], BF16) for i in range(2)]
                XS = [self.sb(st, f"s5_XS{i}", [128, 2, 64], F32) for i in range(2)]
                identf = self.sb(st, "s5_identf", [128, 128], F32)
                P.dma("sp", lambda e: e.dma_start(out=identf[:], in_=self.identf_d), writes=["s5_identf"])

                STG = [self.sb(st, "s5_STG0", [128, 8, 16, 16], BF16)] * 2

                def make_ug(g, slot):
                    bank = slot
                    ft, g8 = g // 8, g % 8
                    stg = STG[0]
                    sk_ = "s5_STG0"
                    if g8 == 0:
                        P.op("dve", lambda e, stg=stg, ft=ft: e.tensor_copy(
                            out=stg[:], in_=Z[:, :, ft * 128:(ft + 1) * 128].rearrange("p j (g c) -> p g j c", c=16)),
                            reads=[zk(j) for j in range(16)], writes=[sk_])
                    for kh in range(2):
                        P.op("pe", lambda e, kh=kh, bank=bank, stg=stg, g8=g8: e.transpose(
                            ps[bank][:].bitcast(BF16)[:, kh * 128:(kh + 1) * 128], stg[:, g8, kh * 8:(kh + 1) * 8, :], idb[:]),
                            reads=[sk_, "identb"], writes=[("ps", bank)])
                    self.evac("act", UG[slot][:], ps[bank][:].bitcast(BF16)[:, 0:256].rearrange("p (k c) -> p k c", k=2), [("ps", bank)], [f"s5_UG{slot}"])

                stB = contextlib.ExitStack()
                SWt = [self.sb(stB, f"s5_SWt{i}", [128, GB, 512], BF16) for i in range(2)]
                for g in range(64):
                    gl, b = g % GB, g // GB
                    pb = b % 2
                    if gl == 0:
                        P.dma("sp", lambda e, b=b, pb=pb: e.dma_start(out=SWt[pb][:], in_=self.SWS[o, b * GB:(b + 1) * GB].rearrange("g p f -> p g f")),
                              reads=[("SWS", o)], writes=[f"s5_SWt{pb}"])
                    slot = g % 2
                    make_ug(g, slot)
                    bank = 2 + g % 2
                    for d in range(2):
                        for ri in range(2):
                            for kh in range(2):
                                rhs = UG[slot][:, kh, :] if d == 0 else UG[slot][:, kh, ::-1]
                                q = kh * 2 + ri
                                P.op("pe", lambda e, d=d, ri=ri, kh=kh, rhs=rhs, q=q, gl=gl, pb=pb, bank=bank: e.matmul(
                                    ps[bank][:, (d * 2 + ri) * 128:(d * 2 + ri + 1) * 128], SWt[pb][:, gl, q * 128:(q + 1) * 128], rhs,
                                    start=(kh == 0), stop=(kh == 1)),
                                    reads=[f"s5_SWt{pb}", f"s5_UG{slot}"], writes=[("ps", bank)])
                    for d in range(2):
                        hs = slice(d * 64, (d + 1) * 64)
                        self.evac("dve" if d == 0 else "act", SRI[hs, :, g, :],
                                  ps[bank][hs, d * 256:(d + 1) * 256].rearrange("p (r c) -> p r c", r=2), [("ps", bank)], [("s5_S", g, d)])
                P.barrier()
                stB.close()
                if self.stop == "B":
                    return
                P.op("dve", lambda e: e.memset(XS[0][:], 0.0), writes=["s5_XS0"])
                AAt = self.sb(st, "s5_AAt", [128, 2, 64], F32)
                ABt = self.sb(st, "s5_ABt", [128, 2, 64], F32)
                U = self.sb(st, "s5_U", [128, 2, 64], F32)
                V = self.sb(st, "s5_V", [128, 2, 64], F32)
                P.op("dve", lambda e: e.tensor_copy(out=AAt[:, 0, :], in_=self.A16[:, o, 0, :]), reads=["A16"], writes=["s5_AAt"])
                P.op("dve", lambda e: e.tensor_copy(out=AAt[:, 1, :], in_=self.A16[:, o, 0, :]), reads=["A16"], writes=["s5_AAt"])
                P.op("dve", lambda e: e.tensor_scalar(out=ABt[:, 0, :], in0=self.A16[:, o, 1, :], scalar1=-1.0, scalar2=None, op0=ALU.mult), reads=["A16"], writes=["s5_ABt"])
                P.op("dve", lambda e: e.tensor_copy(out=ABt[:, 1, :], in_=self.A16[:, o, 1, :]), reads=["A16"], writes=["s5_ABt"])
                for c in range(128):
                    cur, nxt = XS[c % 2], XS[(c + 1) % 2]
                    ck, nk = f"s5_XS{c % 2}", f"s5_XS{(c + 1) % 2}"
                    sk = ("s5_Sc", c)
                    P.op("dve", lambda e, cur=cur: e.tensor_tensor(out=U[:], in0=cur[:], in1=AAt[:], op=ALU.mult), reads=[ck, "s5_AAt"], writes=["s5_U"])
                    P.op("dve", lambda e, cur=cur: e.tensor_tensor(out=V[:], in0=cur[:, ::-1, :], in1=ABt[:], op=ALU.mult), reads=[ck, "s5_ABt"], writes=["s5_V"])
                    P.op("dve", lambda e: e.tensor_add(out=U[:], in0=U[:], in1=V[:]), reads=["s5_U", "s5_V"], writes=["s5_U"])
                    P.op("dve", lambda e, c=c, nxt=nxt: e.tensor_tensor(out=nxt[:], in0=U[:], in1=SRI[:, :, :, c], op=ALU.add), reads=["s5_U", sk], writes=[nk])
                    P.op("act", lambda e, c=c, cur=cur: e.copy(out=SRI[:, :, :, c], in_=cur[:]), reads=[ck], writes=[sk])
                P.barrier()
                if self.stop == "C":
                    return
                Lt = [self.sb(st, f"s5_Lt{i}", [128, GB, 512], BF16) for i in range(2)]
                WXt = [self.sb(st, f"s5_WXt{i}", [128, GB, 1024], BF16) for i in range(2)]
                YS = [self.sb(st, f"s5_YS{i}", [128, 2, 128], F32) for i in range(2)]
                YF = self.sb(st, "s5_YF", [128, 16, 32], F32)
                G1 = self.sb(st, "s5_G1", [128, 16, 32], F32)
                for g in range(64):
                    gl, b = g % GB, g // GB
                    pb = b % 2
                    if gl == 0:
                        P.dma("sp", lambda e, b=b, pb=pb: e.dma_start(out=Lt[pb][:], in_=self.LS[o, b * GB:(b + 1) * GB].rearrange("g p f -> p g f")),
                              reads=[("LS", o)], writes=[f"s5_Lt{pb}"])
                        P.dma("sp", lambda e, b=b, pb=pb: e.dma_start(out=WXt[pb][:], in_=self.WXS[o, b * GB:(b + 1) * GB].rearrange("g p f -> p g f")),
                              reads=[("WXS", o)], writes=[f"s5_WXt{pb}"])
                    slot = g % 2
                    make_ug(g, slot)
                    bank = 2 + g % 2
                    for mh in range(2):
                        outp = ps[bank][:, mh * 128:(mh + 1) * 128]
                        ms = slice(mh * 128, (mh + 1) * 128)
                        mms = []
                        for kh in range(2):
                            mms.append((Lt[pb][:, gl, kh * 256 + mh * 128:kh * 256 + (mh + 1) * 128], UG[slot][:, kh, :], [f"s5_Lt{pb}", f"s5_UG{slot}"]))
                        for ri in range(2):
                            c0 = ri * 256 + mh * 128
                            mms.append((WXt[pb][:, gl, c0:c0 + 128], SRI[:, ri, g, :], [f"s5_WXt{pb}", ("s5_S", g, 0), ("s5_S", g, 1)]))
                        for ri in range(2):
                            c0 = 512 + ri * 256 + mh * 128
                            mms.append((WXt[pb][:, gl, c0:c0 + 128], SRI[:, ri, g, ::-1], [f"s5_WXt{pb}", ("s5_S", g, 0), ("s5_S", g, 1)]))
                        for n_, (lt, rh, rk) in enumerate(mms):
                            P.op("pe", lambda e, lt=lt, rh=rh, outp=outp, n_=n_: e.matmul(outp, lt, rh, start=(n_ == 0), stop=(n_ == len(mms) - 1)),
                                 reads=rk, writes=[("ps", bank)])
                    ys = YS[g % 2]
                    self.evac("act", ys[:], ps[bank][:, 0:256].rearrange("p (m c) -> p m c", m=2), [("ps", bank)], [f"s5_YS{g % 2}"])
                    tb = 4 + g % 2
                    for mh in range(2):
                        P.op("pe", lambda e, mh=mh, ys=ys, tb=tb: e.transpose(ps[tb][:, mh * 128:(mh + 1) * 128], ys[:, mh, :], identf[:]),
                             reads=[f"s5_YS{g % 2}", "s5_identf"], writes=[("ps", tb)])
                    g4 = g % 2
                    self.evac("dve", YF[:, :, g4 * 16:(g4 + 1) * 16], ps[tb][:, 0:256].rearrange("p (i c) -> p i c", c=16), [("ps", tb)], ["s5_YF"])
                    if g4 == 1:
                        f0 = (g // 2) * 32
                        P.op("dve", lambda e: e.tensor_mul(out=G1[:], in0=YF[:], in1=YF[:]), reads=["s5_YF"], writes=["s5_G1"])
                        P.op("dve", lambda e: e.tensor_scalar(out=G1[:], in0=G1[:], scalar1=0.044715, scalar2=1.0, op0=ALU.mult, op1=ALU.add),
                             reads=["s5_G1"], writes=["s5_G1"])
                        P.op("dve", lambda e: e.tensor_mul(out=G1[:], in0=G1[:], in1=YF[:]), reads=["s5_G1", "s5_YF"], writes=["s5_G1"])
                        P.op("act", lambda e: e.activation(out=G1[:], in_=G1[:], func=AF.Sigmoid, scale=1.5957691216057308), reads=["s5_G1"], writes=["s5_G1"])
                        P.op("dve", lambda e, f0=f0: e.tensor_mul(out=Z[:, :, f0:f0 + 32], in0=G1[:], in1=YF[:]),
                             reads=["s5_G1", "s5_YF"], writes=[zk(j) for j in range(16)])
                P.barrier()
            if self.stop == "D":
                return
            with contextlib.ExitStack() as st:
                Wg = load_w(st, "s5_Wg", self.w_glu[o], 1024)
                Wz = load_w(st, "s5_Wz", self.w_in_c[o][:, 1024:2048], 1024)
                bg = self.sb(st, "s5_bg", [128, D], F32)
                P.dma("sp", lambda e: e.dma_start(out=bg[:], in_=self.b_glu[o].partition_broadcast(128)), writes=["s5_bg"])
                hgT = [self.sb(st, f"s5_hgT{i}", [128, 8, 128], BF16) for i in range(2)]
                gl_t = [self.sb(st, f"s5_gl{i}", [128, 512], F32) for i in range(2)]
                sz_t = [self.sb(st, f"s5_sz{i}", [128, 512], F32) for i in range(2)]
                for j in range(16):
                    hk = f"s5_hgT{j % 2}"
                    self.transpose8(Z[:, j, :], hgT[j % 2], [zk(j)], hk, bank=j % 2)
                    for half in range(2):
                        hsl = slice(half * 512, (half + 1) * 512)
                        s2 = (j * 2 + half) % 2
                        bg_ = 2 + s2
                        bz_ = 4 + s2
                        for kt in range(8):
                            P.op("pe", lambda e, kt=kt, j=j, hsl=hsl, bg_=bg_: e.matmul(ps[bg_][:], hgT[j % 2][:, kt, :], Wg[:, kt, hsl], start=(kt == 0), stop=(kt == 7)),
                                 reads=[hk, "s5_Wg"], writes=[("ps", bg_)])
                        for kt in range(8):
                            P.op("pe", lambda e, kt=kt, j=j, hsl=hsl, bz_=bz_: e.matmul(ps[bz_][:], hT[:, kt, j::16], Wz[:, kt, hsl], start=(kt == 0), stop=(kt == 7)),
                                 reads=["hT", "s5_Wz"], writes=[("ps", bz_)])
                        glt, szt = gl_t[s2], sz_t[s2]
                        P.op("dve", lambda e, glt=glt, bg_=bg_, hsl=hsl: e.tensor_tensor(out=glt[:], in0=ps[bg_][:], in1=bg[:, hsl], op=ALU.add),
                             reads=[("ps", bg_), "s5_bg"], writes=[f"s5_gl{s2}"])
                        P.op("act", lambda e, glt=glt: e.activation(out=glt[:], in_=glt[:], func=AF.Sigmoid), reads=[f"s5_gl{s2}"], writes=[f"s5_gl{s2}"])
                        P.op("act", lambda e, szt=szt, bz_=bz_: e.activation(out=szt[:], in_=ps[bz_][:], func=AF.Silu), reads=[("ps", bz_)], writes=[f"s5_sz{s2}"])
                        P.op("dve", lambda e, glt=glt, szt=szt: e.tensor_mul(out=glt[:], in0=glt[:], in1=szt[:]), reads=[f"s5_gl{s2}", f"s5_sz{s2}"], writes=[f"s5_gl{s2}"])
                        P.op("dve", lambda e, glt=glt, j=j, hsl=hsl: e.tensor_mul(out=Z[:, j, hsl], in0=Z[:, j, hsl], in1=glt[:]),
                             reads=[f"s5_gl{s2}", zk(j), hk], writes=[zk(j)])
                P.barrier()
            if self.stop == "E":
                return
            with contextlib.ExitStack() as st:
                Wo = load_w(st, "s5_Wo", self.w_out_c[o], 1024)
                mT = [self.sb(st, f"s5_mT{i}", [128, 8, 128], BF16) for i in range(2)]
                for j in range(16):
                    mk = f"s5_mT{j % 2}"
                    self.transpose8(Z[:, j, :], mT[j % 2], [zk(j)], mk, bank=j % 2)
                    for half in range(2):
                        hsl = slice(half * 512, (half + 1) * 512)
                        bank = 2 + (j * 2 + half) % 4
                        for kt in range(8):
                            P.op("pe", lambda e, kt=kt, j=j, hsl=hsl, bank=bank: e.matmul(ps[bank][:], mT[j % 2][:, kt, :], Wo[:, kt, hsl], start=(kt == 0), stop=(kt == 7)),
                                 reads=[mk, "s5_Wo"], writes=[("ps", bank)])
                        P.op("dve", lambda e, j=j, hsl=hsl, bank=bank: e.tensor_tensor(out=x_sb[:, j, hsl], in0=x_sb[:, j, hsl], in1=ps[bank][:], op=ALU.add),
                             reads=[("ps", bank), ("x", j)], writes=[("x", j)])
                P.barrier()

    def evac(self, eng, out, in_, reads, writes):
        if eng == "act":
            self.P.op("act", lambda e: e.copy(out=out, in_=in_), reads=reads, writes=writes)
        else:
            self.P.op(eng, lambda e: e.tensor_copy(out=out, in_=in_), reads=reads, writes=writes)

    def transpose8(self, src, dst, rkeys, wkey, bank):
        P = self.P
        pv = self.ps[bank][:].bitcast(BF16)
        for kt in range(8):
            P.op("pe", lambda e, kt=kt: e.transpose(pv[:, kt * 128:(kt + 1) * 128], src[:, kt * 128:(kt + 1) * 128], self.identb[:]),
                 reads=list(rkeys) + ["identb"], writes=[("ps", bank)])
        self.evac("act", dst[:], pv.rearrange("p (k c) -> p k c", k=8), [("ps", bank)], [wkey])


def _att_bidx():
    import jax
    import jax.numpy as jnp
    k = np.arange(128)[:, None, None]
    d = np.arange(-1, 2)[None, :, None]
    q = np.arange(128)[None, None, :]
    rel = (k - q - 128 * d).astype(np.int32)
    half, max_exact = 16, 8
    try:
        cpu = jax.devices("cpu")[0]
        with jax.default_device(cpu):
            r = jnp.asarray(rel)
            n = jnp.abs(r)
            large = max_exact + (jnp.log(jnp.maximum(n, 1).astype(jnp.float32) / max_exact)
                                 / math.log(128 / max_exact) * (half - max_exact)).astype(jnp.int32)
            large = jnp.minimum(large, half - 1)
            out = jnp.where(r > 0, half, 0) + jnp.where(n < max_exact, n, large)
            return np.asarray(out).astype(np.float32)
    except Exception:
        n = np.abs(rel)
        large = max_exact + (np.log(np.maximum(n, 1).astype(np.float32) / np.float32(max_exact))
                             / np.float32(math.log(128 / max_exact)) * np.float32(half - max_exact)).astype(np.int32)
        large = np.minimum(large, half - 1)
        return (np.where(rel > 0, half, 0) + np.where(n < max_exact, n, large)).astype(np.float32)


class ABMixin:
    def ab_declare(self):
        for nm, shp in [("rel_bias", [32, 8]), ("w_in_ab", [2, 1024, IN_AB]), ("conv_w", [2, 5, 1280]), ("conv_b", [2, 1280]),
                        ("ssd_dt_bias", [2, 2, 16]), ("ssd_A_log", [2, 2, 16]), ("ssd_D", [2, 16]), ("ssd_norm_w", [2, 1024]),
                        ("diff_lambda", [2, 4, 64]), ("diff_subln_w", [2, 128]), ("w_out_ab", [2, 2048, 1024])]:
            setattr(self, nm, self.din(nm, shp))
        self.att_bidx = self.din("att_bidx", [128, 3, 128])
        self.ssd_declare()

    def ab_prologue(self):
        nc, P = self.nc, self.P
        st0 = self._st_small
        self.ssd_prologue()
        self.RB = self.sb(st0, "RB", [128, 256], F32)
        self.NBS = nc.dram_tensor("att_NBS", [128, 2 * 8 * 384], BF16, kind="Internal").ap()
        self.NEGLAM = self.sb(st0, "NEGLAM", [128, 2], F32)
        self.SLW = self.sb(st0, "SLW", [128, 2, 128], F32)
        P.dma("sp", lambda e: e.dma_start(out=self.RB[:], in_=self.rel_bias.rearrange("b h -> (b h)").partition_broadcast(128)), writes=["RB"])
        with contextlib.ExitStack() as st:
            sb = lambda n, s, d=F32: self.sb(st, f"abp_{n}", s, d)
            bidx = sb("bidx", [128, 384]); msk = sb("msk", [128, 384]); acc = sb("acc", [128, 8, 384]); t32 = sb("t32", [128, 8, 384])
            self.NB = sb("NBp", [128, 2, 8, 384], BF16)
            P.dma("sp", lambda e: e.dma_start(out=bidx[:], in_=self.att_bidx.rearrange("k d q -> k (d q)")), writes=["abp_bidx"])
            P.op("dve", lambda e: e.memset(acc[:], 0.0), writes=["abp_acc"])
            for b in range(32):
                P.op("dve", lambda e, b=b: e.tensor_scalar(out=msk[:], in0=bidx[:], scalar1=float(b), scalar2=None, op0=ALU.is_equal),
                     reads=["abp_bidx"], writes=["abp_msk"])
                for h in range(8):
                    P.op("dve", lambda e, b=b, h=h: e.scalar_tensor_tensor(out=acc[:, h, :], in0=msk[:], scalar=self.RB[:, b * 8 + h:b * 8 + h + 1],
                                                                            in1=acc[:, h, :], op0=ALU.mult, op1=ALU.add),
                         reads=["abp_msk", "RB", "abp_acc"], writes=["abp_acc"])
            P.op("dve", lambda e: e.tensor_scalar(out=acc[:], in0=acc[:], scalar1=8.0, scalar2=None, op0=ALU.mult), reads=["abp_acc"], writes=["abp_acc"])
            P.op("dve", lambda e: e.tensor_copy(out=self.NB[:, 0], in_=acc[:]), reads=["abp_acc"], writes=["NB"])
            P.op("dve", lambda e: e.tensor_copy(out=t32[:], in_=self.NB[:, 0]), reads=["NB"], writes=["abp_t32"])
            P.op("dve", lambda e: e.tensor_sub(out=t32[:], in0=acc[:], in1=t32[:]), reads=["abp_acc", "abp_t32"], writes=["abp_t32"])
            P.op("dve", lambda e: e.tensor_copy(out=self.NB[:, 1], in_=t32[:]), reads=["abp_t32"], writes=["NB"])
            P.dma("sp", lambda e: e.dma_start(out=self.NBS, in_=self.NB[:].rearrange("p a h f -> p (a h f)")), reads=["NB"], writes=["NBS"])
            dl = sb("dl", [128, 2, 4, 64]); pj = sb("pj", [128, 64]); pr = sb("pr", [128, 4])
            P.dma("sp", lambda e: e.dma_start(out=dl[:].rearrange("p e a d -> p (e a d)"),
                                              in_=self.diff_lambda.rearrange("e a d -> (e a d)").partition_broadcast(128)), writes=["abp_dl"])
            sw = sb("sw", [128, 2, 128])
            P.dma("sp", lambda e: e.dma_start(out=sw[:].rearrange("p e d -> p (e d)"),
                                              in_=self.diff_subln_w.rearrange("e d -> (e d)").partition_broadcast(128)), writes=["abp_sw"])
            for e_ in range(2):
                lam_init = 0.8 - 0.6 * math.exp(-0.3 * (2 * e_))
                for a in range(2):
                    P.op("dve", lambda e, e_=e_, a=a: e.scalar_tensor_tensor(out=pj[:], in0=dl[:, e_, 2 * a, :], scalar=1.0, in1=dl[:, e_, 2 * a + 1, :],
                                                                              op0=ALU.mult, op1=ALU.mult, accum_out=pr[:, e_ * 2 + a:e_ * 2 + a + 1]),
                         reads=["abp_dl"], writes=["abp_pj", "abp_pr"])
                P.op("act", lambda e, e_=e_: e.activation(out=pr[:, e_ * 2:e_ * 2 + 2], in_=pr[:, e_ * 2:e_ * 2 + 2], func=AF.Exp), reads=["abp_pr"], writes=["abp_pr"])
                P.op("dve", lambda e, e_=e_: e.tensor_sub(out=self.NEGLAM[:, e_:e_ + 1], in0=pr[:, e_ * 2 + 1:e_ * 2 + 2], in1=pr[:, e_ * 2:e_ * 2 + 1]),
                     reads=["abp_pr"], writes=["NEGLAM"])
                P.op("dve", lambda e, e_=e_, lam_init=lam_init: e.tensor_scalar(out=self.NEGLAM[:, e_:e_ + 1], in0=self.NEGLAM[:, e_:e_ + 1],
                                                                                  scalar1=-lam_init, scalar2=None, op0=ALU.add),
                     reads=["NEGLAM"], writes=["NEGLAM"])
                P.op("dve", lambda e, e_=e_, lam_init=lam_init: e.tensor_scalar(out=self.SLW[:, e_, :], in0=sw[:, e_, :], scalar1=1.0 - lam_init,
                                                                                  scalar2=None, op0=ALU.mult),
                     reads=["abp_sw"], writes=["SLW"])
            P.barrier()

    def ab_layer(self, e_, lnum):
        P = self.P
        with contextlib.ExitStack() as stL:
            yT = self.sb(stL, "ab_yT", [128, 8, L], BF16)
            if "nossd" not in self.stop:
                self.ssd_part(e_, yT)
                self.out_proj_half(e_, yT, 0, rstd=True)
            if "noatt" not in self.stop:
                self.att_part(e_, yT)
                self.out_proj_half(e_, yT, 1, rstd=False)

    def out_proj_half(self, e_, yT, half_idx, rstd):
        P = self.P
        ps, x_sb = self.ps, self.x_sb
        with contextlib.ExitStack() as st:
            Wo = self.sb(st, "ab_Wo", [128, 8, 1024], BF16)
            v = self.w_out_ab[e_][half_idx * 1024:(half_idx + 1) * 1024, :].rearrange("(kt p) f -> p kt f", p=128)
            for q in range(4):
                P.dma("pool", lambda e, q=q: e.dma_start(out=Wo[:, 2 * q:2 * q + 2, :], in_=v[:, 2 * q:2 * q + 2, :]), writes=["ab_Wo"])
            if rstd:
                for kt in range(8):
                    P.op("dve", lambda e, kt=kt: e.tensor_scalar(out=Wo[:, kt, :], in0=Wo[:, kt, :], scalar1=self.snw[:, e_, kt:kt + 1],
                                                                                       scalar2=None, op0=ALU.mult), reads=["ab_Wo", "snw"], writes=["ab_Wo"])
            for j in range(16):
                for hf in range(2):
                    hsl = slice(hf * 512, (hf + 1) * 512)
                    bank = (j * 2 + hf) % 4
                    for kt in range(8):
                        P.op("pe", lambda e, kt=kt, j=j, hsl=hsl, bank=bank: e.matmul(ps[bank][:], yT[:, kt, j::16], Wo[:, kt, hsl], start=(kt == 0), stop=(kt == 7)),
                             reads=["ab_yT", "ab_Wo"], writes=[("ps", bank)])
                    if rstd:
                        P.op("dve", lambda e, j=j, hsl=hsl, bank=bank: e.scalar_tensor_tensor(
                            out=x_sb[:, j, hsl], in0=ps[bank][:], scalar=self.rs_ssd[:, j:j + 1], in1=x_sb[:, j, hsl], op0=ALU.mult, op1=ALU.add),
                            reads=[("ps", bank), ("x", j), "rs_ssd"], writes=[("x", j)])
                    else:
                        P.op("dve", lambda e, j=j, hsl=hsl, bank=bank: e.tensor_tensor(out=x_sb[:, j, hsl], in0=x_sb[:, j, hsl], in1=ps[bank][:], op=ALU.add),
                             reads=[("ps", bank), ("x", j)], writes=[("x", j)])
            P.barrier()

    def att_part(self, e_, yT):
        P = self.P
        ps, hT, idb = self.ps, self.hT, self.identb
        c_q = 1024 + 1280 + 32
        with contextlib.ExitStack() as st:
            sb = lambda n, s, d=BF16: self.sb(st, f"at_{n}", s, d)
            W = [sb(f"W{i}", [128, 4, 8, 128]) for i in range(2)]
            qTc = [sb(f"qT{c}", [128, L]) for c in range(2)]
            kT = sb("kT", [128, L])
            Vaug = sb("Vaug", [128, 16, 130])
            SG = sb("SG", [128, 16, 128])
            PT = [sb(f"PT{i}", [128, 512]) for i in range(4)]
            accs = sb("accs", [128, 8, 129], F32)
            rr = sb("rr", [128, 8], F32)
            o4 = sb("o4", [128, 4, 128], F32)
            t4 = sb("t4", [128, 4, 128], F32)
            ssq = sb("ssq", [128, 4], F32)
            y4 = sb("y4", [128, 4, 128], BF16)
            NBt = sb("NB", [128, 2, 8, 384])
            P.dma("sp", lambda e: e.dma_start(out=NBt[:].rearrange("p a h f -> p (a h f)"), in_=self.NBS), reads=["NBS"], writes=["at_NB"])
            P.op("dve", lambda e: e.memset(qTc[0][:], 0.0), writes=["at_qT0"])
            P.op("dve", lambda e: e.memset(qTc[1][:], 0.0), writes=["at_qT1"])
            P.op("dve", lambda e: e.memset(Vaug[:], 1.0), writes=["at_Vaug"])

            def load_w(h):
                wt = W[h % 2]
                for s_ in range(4):
                    c0 = c_q + s_ * 1024 + h * 128
                    P.dma("pool", lambda e, s_=s_, c0=c0, wt=wt: e.dma_start(
                        out=wt[:, s_], in_=self.w_in_ab[e_][:, c0:c0 + 128].rearrange("(kt p) f -> p kt f", p=128)), writes=[f"at_W{h % 2}"])

            load_w(0)
            for h in range(8):
                wt = W[h % 2]
                wk = f"at_W{h % 2}"
                if h + 1 < 8:
                    load_w(h + 1)
                for s_ in range(2):
                    for qc in range(4):
                        pb_ = 7 if (s_ * 4 + qc) % 2 == 0 else 3
                        for kt in range(8):
                            P.op("pe", lambda e, s_=s_, qc=qc, kt=kt, wt=wt, pb_=pb_: e.matmul(ps[pb_][:], wt[:, s_, kt, :], hT[:, kt, qc * 512:(qc + 1) * 512],
                                                                                              start=(kt == 0), stop=(kt == 7)),
                                 reads=[wk, "hT"], writes=[("ps", pb_)])
                        csl = slice(qc * 512, (qc + 1) * 512)
                        if s_ == 0:
                            self.evac("dve", qTc[0][0:64, csl], ps[pb_][0:64, :], [("ps", pb_)], ["at_qT0"])
                            self.evac("act", qTc[1][64:128, csl], ps[pb_][64:128, :], [("ps", pb_)], ["at_qT1"])
                        else:
                            self.evac("dve", kT[:, csl], ps[pb_][:], [("ps", pb_)], ["at_kT"])
                for s_ in (2, 3):
                    for kq in range(4):
                        pb_ = 7 if kq % 2 == 0 else 3
                        for kl in range(4):
                            kb = kq * 4 + kl
                            for kt in range(8):
                                P.op("pe", lambda e, s_=s_, kb=kb, kl=kl, kt=kt, wt=wt, pb_=pb_: e.matmul(
                                    ps[pb_][:, kl * 128:(kl + 1) * 128], hT[:, kt, kb * 128:(kb + 1) * 128], wt[:, s_, kt, :], start=(kt == 0), stop=(kt == 7)),
                                    reads=[wk, "hT"], writes=[("ps", pb_)])
                        src = ps[pb_][:].rearrange("p (k f) -> p k f", k=4)
                        if s_ == 2:
                            self.evac("dve", Vaug[:, kq * 4:(kq + 1) * 4, 0:128], src, [("ps", pb_)], ["at_Vaug"])
                        else:
                            P.op("act", lambda e, kq=kq, src=src: e.activation(out=SG[:, kq * 4:(kq + 1) * 4, :], in_=src, func=AF.Silu),
                                 reads=[("ps", pb_)], writes=["at_SG"])
                iters = [(qc, kb, comp) for qc in range(4) for kb in range(16) for comp in range(2)]
                n_it = len(iters)
                PRE = 3

                def emit_S(it):
                    qc, kb, comp = iters[it]
                    qsl = slice(qc * 512, (qc + 1) * 512)
                    sbank = it % 4
                    near = [ql for ql in range(4) if abs(qc * 4 + ql - kb) <= 1]
                    P.op("pe", lambda e: e.matmul(ps[sbank][:], kT[:, kb * 128:(kb + 1) * 128], qTc[comp][:, qsl], start=True, stop=(len(near) == 0)),
                         reads=["at_kT", f"at_qT{comp}"], writes=[("ps", sbank)])
                    for ni, ql in enumerate(near):
                        d = qc * 4 + ql - kb
                        for hl in range(2):
                            last = (ni == len(near) - 1) and hl == 1
                            P.op("pe", lambda e, ql=ql, d=d, hl=hl, last=last: e.matmul(
                                ps[sbank][:, ql * 128:(ql + 1) * 128], idb[:], NBt[:, hl, h, (d + 1) * 128:(d + 2) * 128], start=False, stop=last),
                                reads=["identb", "at_NB"], writes=[("ps", sbank)])

                def emit_exp(it):
                    qc, kb, comp = iters[it]
                    sbank = it % 4
                    pt = PT[it % 4]
                    ptk = f"at_PT{it % 4}"
                    segs = []
                    for ql in range(4):
                        d = qc * 4 + ql - kb
                        ty = 0 if abs(d) <= 1 else (1 if d <= -2 else 2)
                        if segs and segs[-1][0] == ty:
                            segs[-1][2] = ql + 1
                        else:
                            segs.append([ty, ql, ql + 1])
                    for (ty, a_, b_) in segs:
                        csl = slice(a_ * 128, b_ * 128)
                        if ty == 0:
                            P.op("act", lambda e, csl=csl: e.activation(out=pt[:, csl], in_=ps[sbank][:, csl], func=AF.Exp, scale=0.125),
                                 reads=[("ps", sbank)], writes=[ptk])
                        else:
                            col = (31 if ty == 1 else 15) * 8 + h
                            P.op("act", lambda e, csl=csl, col=col: e.activation(
                                out=pt[:, csl], in_=ps[sbank][:, csl], func=AF.Exp, bias=self.RB[:, col:col + 1], scale=0.125),
                                reads=[("ps", sbank), "RB"], writes=[ptk])

                def emit_PV(it):
                    qc, kb, comp = iters[it]
                    pt = PT[it % 4]
                    ptk = f"at_PT{it % 4}"
                    for ql in range(4):
                        a_ = comp * 4 + ql
                        abank = 4 + a_ // 3
                        off = (a_ % 3) * 129
                        P.op("pe", lambda e, ql=ql, abank=abank, off=off: e.matmul(
                            ps[abank][:, off:off + 129], pt[:, ql * 128:(ql + 1) * 128], Vaug[:, kb, 0:129], start=(kb == 0 and off == 0), stop=(kb == 15)),
                            reads=[ptk, "at_Vaug"], writes=[("ps", abank)])

                def stage_A(qc):
                    for bk, (a0, a1) in enumerate([(0, 3), (3, 6), (6, 8)]):
                        P.op("dve", lambda e, bk=bk, a0=a0, a1=a1: e.tensor_copy(
                            out=accs[:, a0:a1, :], in_=ps[4 + bk][:, 0:(a1 - a0) * 129].rearrange("p (a f) -> p a f", f=129)),
                            reads=[("ps", 4 + bk)], writes=["at_accs"])
                    P.op("dve", lambda e: e.reciprocal(out=rr[:], in_=accs[:, :, 128]), reads=["at_accs"], writes=["at_rr"])
                    P.op("dve", lambda e: e.tensor_scalar(out=rr[:, 4:8], in0=rr[:, 4:8], scalar1=self.NEGLAM[:, e_:e_ + 1], scalar2=None, op0=ALU.mult),
                         reads=["at_rr", "NEGLAM"], writes=["at_rr"])
                    P.op("dve", lambda e: e.tensor_tensor(out=o4[:], in0=accs[:, 0:4, 0:128], in1=rr[:, 0:4].unsqueeze(2).to_broadcast([128, 4, 128]), op=ALU.mult),
                         reads=["at_accs", "at_rr"], writes=["at_o4"])
                    P.op("dve", lambda e: e.tensor_tensor(out=t4[:], in0=accs[:, 4:8, 0:128], in1=rr[:, 4:8].unsqueeze(2).to_broadcast([128, 4, 128]), op=ALU.mult),
                         reads=["at_accs", "at_rr"], writes=["at_t4"])
                    P.op("dve", lambda e: e.tensor_add(out=o4[:], in0=o4[:], in1=t4[:]), reads=["at_o4", "at_t4"], writes=["at_o4"])
                    P.op("dve", lambda e: e.tensor_mul(out=t4[:], in0=o4[:], in1=o4[:]), reads=["at_o4", "at_t4"], writes=["at_t4"])
                    P.op("dve", lambda e: e.tensor_reduce(out=ssq[:], in_=t4[:], axis=AX.X, op=ALU.add), reads=["at_t4"], writes=["at_ssq"])

                def stage_B(qc):
                    P.op("act", lambda e: e.activation(out=ssq[:], in_=ssq[:], func=AF.Ln, bias=self.epsc[:, 0:1], scale=1.0 / 128), reads=["at_ssq", "epsc"], writes=["at_ssq"])
                    P.op("act", lambda e: e.activation(out=ssq[:], in_=ssq[:], func=AF.Exp, scale=-0.5), reads=["at_ssq"], writes=["at_ssq"])

                def stage_C(qc):
                    P.op("dve", lambda e: e.tensor_tensor(out=o4[:], in0=o4[:], in1=ssq[:].unsqueeze(2).to_broadcast([128, 4, 128]), op=ALU.mult),
                         reads=["at_o4", "at_ssq"], writes=["at_o4"])
                    P.op("dve", lambda e: e.tensor_tensor(out=o4[:], in0=o4[:], in1=self.SLW[:, e_, :].unsqueeze(1).to_broadcast([128, 4, 128]), op=ALU.mult),
                         reads=["at_o4", "SLW"], writes=["at_o4"])
                    P.op("dve", lambda e: e.tensor_tensor(out=y4[:], in0=o4[:], in1=SG[:, qc * 4:(qc + 1) * 4, :], op=ALU.mult),
                         reads=["at_o4", "at_SG"], writes=["at_y4"])
                    pv = ps[7][:].bitcast(BF16)
                    for ql in range(4):
                        P.op("pe", lambda e, ql=ql, pv=pv: e.transpose(pv[:, ql * 128:(ql + 1) * 128], y4[:, ql, :], idb[:]), reads=["at_y4", "identb"], writes=[("ps", 7)])
                    self.evac("dve", yT[:, h, qc * 512:(qc + 1) * 512], pv[:, 0:512], [("ps", 7)], ["ab_yT"])

                deferred = {}
                for i0 in range(min(PRE, n_it)):
                    emit_S(i0)
                for it in range(n_it):
                    qc, kb, comp = iters[it]
                    if it + PRE < n_it:
                        emit_S(it + PRE)
                    emit_exp(it)
                    emit_PV(it)
                    for fn_ in deferred.pop(it, []):
                        fn_()
                    if kb == 15 and comp == 1:
                        stage_A(qc)
                        if qc < 3:
                            deferred.setdefault(it + 5, []).append(lambda qc=qc: stage_B(qc))
                            deferred.setdefault(it + 10, []).append(lambda qc=qc: stage_C(qc))
                        else:
                            stage_B(qc)
                            stage_C(qc)
            P.barrier()

def _ssd_masks():
    k = np.arange(128)[:, None]
    i = np.arange(128)[None, :]
    m = np.zeros((6, 128, 128), np.float32)
    m[0] = (k <= i)
    m[1] = (k > i)
    m[2] = (k >= i)
    m[3] = (k < i)
    m[4] = (i >= k)
    m[5] = (k >= i)
    return m


class SSDMixin:
    def ssd_declare(self):
        self.ssd_masks_d = self.din("ssd_masks", [6, 128, 128])

    def ssd_prologue(self):
        P = self.P
        st0 = self._st_small
        self.MK = self.sb(st0, "MK", [128, 6, 128], F32)
        self.MKb = self.sb(st0, "MKb", [128, 4, 128], BF16)
        self.onesf = self.sb(st0, "onesf", [128, 128], F32)
        self.onesb = self.sb(st0, "onesb", [128, 1], BF16)
        self.convw = self.sb(st0, "convw", [128, 2, 10, 5], F32)
        self.convb = self.sb(st0, "convb", [128, 2, 10], F32)
        self.dtb = self.sb(st0, "dtb", [128, 2, 32], F32)
        self.Aneg = self.sb(st0, "Aneg", [128, 2, 32], F32)
        self.Dsk = self.sb(st0, "Dsk", [128, 2, 16], F32)
        self.snw = self.sb(st0, "snw", [128, 2, 8], F32)
        self.rs_ssd = self.sb(st0, "rs_ssd", [128, 16], F32)
        P.dma("sp", lambda e: e.dma_start(out=self.MK[:], in_=self.ssd_masks_d.rearrange("m k i -> k m i")), writes=["MK"])
        P.dma("pool", lambda e: e.dma_start(out=self.MKb[:], in_=self.ssd_masks_d[0:4].rearrange("m k i -> k m i")), writes=["MKb"])
        P.op("dve", lambda e: e.memset(self.onesf[:], 1.0), writes=["onesf"])
        P.op("dve", lambda e: e.memset(self.onesb[:], 1.0), writes=["onesb"])
        for e_ in range(2):
            for k in range(5):
                P.dma("sp", lambda e, e_=e_, k=k: e.dma_start(out=self.convw[:, e_, :, k], in_=self.conv_w[e_, k].rearrange("(f p) -> p f", p=128),
                                                              allow_slow_non_contiguous=True), writes=["convw"])
            P.dma("sp", lambda e, e_=e_: e.dma_start(out=self.convb[:, e_], in_=self.conv_b[e_].rearrange("(f p) -> p f", p=128),
                                                     allow_slow_non_contiguous=True), writes=["convb"])
            P.dma("sp", lambda e, e_=e_: e.dma_start(out=self.snw[:, e_], in_=self.ssd_norm_w[e_].rearrange("(f p) -> p f", p=128),
                                                     allow_slow_non_contiguous=True), writes=["snw"])
        P.dma("sp", lambda e: e.dma_start(out=self.dtb[:].rearrange("p e h -> p (e h)"),
                                          in_=self.ssd_dt_bias.rearrange("e d h -> (e d h)").partition_broadcast(128)), writes=["dtb"])
        P.dma("sp", lambda e: e.dma_start(out=self.Aneg[:].rearrange("p e h -> p (e h)"),
                                          in_=self.ssd_A_log.rearrange("e d h -> (e d h)").partition_broadcast(128)), writes=["Aneg"])
        P.dma("sp", lambda e: e.dma_start(out=self.Dsk[:].rearrange("p e h -> p (e h)"),
                                          in_=self.ssd_D.rearrange("e h -> (e h)").partition_broadcast(128)), writes=["Dsk"])
        P.op("act", lambda e: e.activation(out=self.Aneg[:], in_=self.Aneg[:], func=AF.Exp), reads=["Aneg"], writes=["Aneg"])
        P.op("dve", lambda e: e.tensor_scalar(out=self.Aneg[:], in0=self.Aneg[:], scalar1=-1.0, scalar2=None, op0=ALU.mult), reads=["Aneg"], writes=["Aneg"])

    def conv_ft(self, e_, ft, Wt, wk, XC, dst_fn, post_fn=None):
        P = self.P
        ps, hT = self.ps, self.hT
        for qc in range(4):
            bank = 5 + qc % 2
            for kt in range(8):
                P.op("pe", lambda e, qc=qc, kt=kt, bank=bank: e.matmul(ps[bank][:], Wt[:, kt, :], hT[:, kt, qc * 512:(qc + 1) * 512], start=(kt == 0), stop=(kt == 7)),
                     reads=[wk, "hT"], writes=[("ps", bank)])
            self.evac("act" if qc % 2 == 0 else "dve", XC[:, 2 + qc * 512:2 + (qc + 1) * 512], ps[bank][:], [("ps", bank)], [("sd_XC", qc)])
        acc = self.sd_acc
        cw = self.convw[:, e_, ft, :]
        for qc in range(4):
            rk = [("sd_XC", q) for q in range(max(0, qc - 1), min(4, qc + 2))] + ["sd_XCh"]
            o = qc * 512
            P.op("dve", lambda e, o=o: e.tensor_scalar(out=acc[:], in0=XC[:, o:o + 512], scalar1=cw[:, 0:1], scalar2=self.convb[:, e_, ft:ft + 1], op0=ALU.mult, op1=ALU.add),
                 reads=rk + ["convw", "convb"], writes=["sd_acc"])
            for k in range(1, 5):
                P.op("dve", lambda e, k=k, o=o: e.scalar_tensor_tensor(out=acc[:], in0=XC[:, o + k:o + k + 512], scalar=cw[:, k:k + 1], in1=acc[:], op0=ALU.mult, op1=ALU.add),
                     reads=rk + ["convw", "sd_acc"], writes=["sd_acc"])
            dst, dkey = dst_fn(qc)
            P.op("act", lambda e, dst=dst: e.activation(out=dst, in_=acc[:], func=AF.Silu), reads=["sd_acc"], writes=[dkey])
            if post_fn is not None:
                post_fn(qc)

    NH = 4

    def ssd_part(self, e_, yT):
        P = self.P
        ps, hT, idb = self.ps, self.hT, self.identb
        c_x, c_dt = 1024, 2304
        MK, MKb = self.MK, self.MKb
        NH = self.NH
        NW = NH * 64
        NF = NW // 128
        NP = 16 // NH
        with contextlib.ExitStack() as st:
            sb = lambda n, s, d=BF16: self.sb(st, f"sd_{n}", s, d)
            CTf = sb("CTf", [128, L]); BTm = sb("BTm", [128, L])
            Wdt = sb("Wdt", [128, 8, 32])
            DT = sb("DT", [128, 16, 2 * NH], F32); ECS = sb("ECS", [128, 16, 2 * NH], F32); DDT = sb("DDT", [128, 16, 2 * NH], F32)
            CDX = sb("CDX", [128, 16, 2 * NH], F32); AT = sb("AT", [128, 16, 2 * NH], F32)
            AH = sb("AH", [128, 16, 2 * NH]); AL = sb("AL", [128, 16, 2 * NH]); t16 = sb("t16", [128, 16, 2 * NH], F32)
            XS = sb("XS", [128, 16, NW])
            Wz = sb("Wz", [128, 8, NW])
            Hst = sb("Hst", [128, NW], F32)
            Hbf_l = [sb(f"Hbf{i}", [128, NW]) for i in range(2)]; HPt_l = [sb(f"HPt{i}", [128, NW]) for i in range(2)]
            RH_l = [sb(f"RH{i}", [128, NH, 128]) for i in range(2)]; RL_l = [sb(f"RL{i}", [128, NH, 128]) for i in range(2)]
            Et_l = [sb(f"E{i}", [128, NH, 128]) for i in range(2)]; CBM_l = [sb(f"CBM{i}", [128, 128]) for i in range(2)]
            XDT_l = [sb(f"XDT{i}", [128, NH, 64]) for i in range(2)]; XDD_l = [sb(f"XDD{i}", [128, NH, 64]) for i in range(2)]
            Yacc_l = [sb(f"Yacc{i}", [128, NH, 64], F32) for i in range(2)]; tY_l = [sb(f"tY{i}", [128, NH, 64], F32) for i in range(2)]
            SZ4 = sb("SZ4", [128, 4, NW]); GT_l = [sb(f"GT{i}", [128, NW]) for i in range(2)]
            Btk_l = [sb(f"Btk{i}", [128, 128]) for i in range(2)]
            XC = sb("XC", [128, L + 4], F32)
            self.sd_acc = sb("acc", [128, 512], F32)
            Wt = [sb(f"Wt{i}", [128, 8, 128]) for i in range(2)]
            xtf = [sb(f"xtf{i}", [128, 512]) for i in range(2)]
            P.dma("pool", lambda e: e.dma_start(out=Wdt[:], in_=self.w_in_ab[e_][:, c_dt:c_dt + 32].rearrange("(kt p) f -> p kt f", p=128)), writes=["sd_Wdt"])
            P.op("dve", lambda e: e.memset(XC[:, 0:2], 0.0), writes=["sd_XCh"])
            P.op("dve", lambda e: e.memset(XC[:, L + 2:L + 4], 0.0), writes=["sd_XCh"])
            h3 = lambda T: T.rearrange("p (h q) -> p h q", h=NH)

            def load_wt(ft, slot):
                c0 = c_x + ft * 128
                P.dma("pool", lambda e: e.dma_start(out=Wt[slot][:], in_=self.w_in_ab[e_][:, c0:c0 + 128].rearrange("(kt p) f -> p kt f", p=128)),
                      writes=[f"sd_Wt{slot}"])

            load_wt(9, 0)
            self.conv_ft(e_, 9, Wt[0], "sd_Wt0", XC, lambda qc: (CTf[:, qc * 512:(qc + 1) * 512], "sd_CTf"))
            DTf = sb("DTf", [128, 16, 32], F32); ATf = sb("ATf", [128, 16, 32], F32); ECSf = sb("ECSf", [128, 16, 32], F32)
            DDTf = sb("DDTf", [128, 16, 32], F32); CDXf = sb("CDXf", [128, 16, 32], F32)
            for ci in range(16):
                for kt in range(8):
                    P.op("pe", lambda e, ci=ci, kt=kt: e.matmul(ps[0][:, ci * 32:(ci + 1) * 32], hT[:, kt, ci * 128:(ci + 1) * 128], Wdt[:, kt, :],
                                                                 start=(kt == 0), stop=(kt == 7)), reads=["hT", "sd_Wdt"], writes=[("ps", 0)])
            bfull = lambda G: G[:, e_, :].unsqueeze(1).to_broadcast([128, 16, 32])
            P.op("dve", lambda e: e.tensor_tensor(out=DTf[:], in0=ps[0][:].rearrange("p (c h) -> p c h", c=16), in1=bfull(self.dtb), op=ALU.add),
                 reads=[("ps", 0), "dtb"], writes=["sd_DTf"])
            P.op("act", lambda e: e.activation(out=DTf[:], in_=DTf[:], func=AF.Exp), reads=["sd_DTf"], writes=["sd_DTf"])
            P.op("act", lambda e: e.activation(out=DTf[:], in_=DTf[:], func=AF.Ln, bias=1.0), reads=["sd_DTf"], writes=["sd_DTf"])
            P.op("dve", lambda e: e.tensor_tensor(out=ATf[:], in0=DTf[:], in1=bfull(self.Aneg), op=ALU.mult), reads=["sd_DTf", "Aneg"], writes=["sd_ATf"])
            for ci in range(16):
                for d in range(2):
                    rhs = ATf[:, ci, d * 16:(d + 1) * 16]
                    osl = slice(ci * 32 + d * 16, ci * 32 + d * 16 + 16)
                    m_ecs = MK[:, 0 if d == 0 else 2, :]
                    m_dte = MK[:, 1 if d == 0 else 3, :]
                    P.op("pe", lambda e, rhs=rhs, osl=osl, m_ecs=m_ecs: e.matmul(ps[1][:, osl], m_ecs, rhs, start=True, stop=True),
                         reads=["MK", "sd_ATf"], writes=[("ps", 1)])
                    P.op("pe", lambda e, rhs=rhs, osl=osl, m_dte=m_dte: e.matmul(ps[2][:, osl], m_dte, rhs, start=True, stop=True),
                         reads=["MK", "sd_ATf"], writes=[("ps", 2)])
                    P.op("pe", lambda e, rhs=rhs, osl=osl: e.matmul(ps[3][:, osl], self.onesf[:], rhs, start=True, stop=True),
                         reads=["onesf", "sd_ATf"], writes=[("ps", 3)])
            flf = lambda T: T[:].rearrange("p c h -> p (c h)")
            P.op("act", lambda e: e.activation(out=flf(ECSf), in_=ps[1][:], func=AF.Exp), reads=[("ps", 1)], writes=["sd_ECSf"])
            P.op("act", lambda e: e.activation(out=flf(DDTf), in_=ps[2][:], func=AF.Exp), reads=[("ps", 2)], writes=["sd_DDTf"])
            P.op("act", lambda e: e.activation(out=flf(CDXf), in_=ps[3][:], func=AF.Exp), reads=[("ps", 3)], writes=["sd_CDXf"])
            P.op("dve", lambda e: e.tensor_mul(out=DDTf[:], in0=DDTf[:], in1=DTf[:]), reads=["sd_DDTf", "sd_DTf"], writes=["sd_DDTf"])
            for pi in range(NP):
                gi = (pi * NH) // 8
                h0 = pi * NH
                hsl = slice(h0, h0 + NH)
                ho = slice((1 - gi) * 64, (2 - gi) * 64)
                k0 = pi * NF
                if (pi * NH) % 8 == 0:
                    load_wt(8, 1)
                    self.conv_ft(e_, 8, Wt[1], "sd_Wt1", XC, lambda qc: (BTm[:, qc * 512:(qc + 1) * 512], "sd_BTm"))
                    P.op("dve", lambda e, ho=ho: e.memset(BTm[ho, :], 0.0), reads=[], writes=["sd_BTm"])
                for fl in range(NF):
                    ft = pi * NF + fl
                    load_wt(ft, fl % 2)

                    def dst_fn(qc):
                        return (xtf[qc % 2][:], f"sd_xtf{qc % 2}")

                    def post_fn(qc, fl=fl):
                        pv = ps[7][:].bitcast(BF16)
                        for cl in range(4):
                            P.op("pe", lambda e, cl=cl, pv=pv: e.transpose(pv[:, cl * 128:(cl + 1) * 128], xtf[qc % 2][:, cl * 128:(cl + 1) * 128], idb[:]),
                                 reads=[f"sd_xtf{qc % 2}", "identb"], writes=[("ps", 7)])
                        self.evac("dve", XS[:, qc * 4:(qc + 1) * 4, fl * 128:(fl + 1) * 128], pv[:, 0:512].rearrange("p (c f) -> p c f", c=4),
                                  [("ps", 7)], ["sd_XS"])

                    self.conv_ft(e_, ft, Wt[fl % 2], f"sd_Wt{fl % 2}", XC, dst_fn, post_fn)
                for q in range(4):
                    P.dma("pool", lambda e, q=q, pi=pi: e.dma_start(
                        out=Wz[:, 2 * q:2 * q + 2, :], in_=self.w_in_ab[e_][:, pi * NW:(pi + 1) * NW].rearrange("(kt p) f -> p kt f", p=128)[:, 2 * q:2 * q + 2, :]),
                        writes=["sd_Wz"])
                v4 = lambda T: T[:].rearrange("p c (d h) -> p c d h", d=2)
                f4 = lambda T: T[:].rearrange("p c (d h) -> p c d h", d=2)[:, :, :, hsl]
                for (dst, src, dk, sk_) in ((DT, DTf, "sd_DT", "sd_DTf"), (AT, ATf, "sd_AT", "sd_ATf"), (ECS, ECSf, "sd_ECS", "sd_ECSf"),
                                            (DDT, DDTf, "sd_DDT", "sd_DDTf"), (CDX, CDXf, "sd_CDX", "sd_CDXf")):
                    P.op("dve", lambda e, dst=dst, src=src: e.tensor_copy(out=v4(dst), in_=f4(src)), reads=[sk_], writes=[dk])
                P.op("dve", lambda e: e.tensor_copy(out=AH[:], in_=AT[:]), reads=["sd_AT"], writes=["sd_AH"])
                P.op("dve", lambda e: e.tensor_copy(out=t16[:], in_=AH[:]), reads=["sd_AH"], writes=["sd_t16"])
                P.op("dve", lambda e: e.tensor_sub(out=t16[:], in0=AT[:], in1=t16[:]), reads=["sd_AT", "sd_t16"], writes=["sd_t16"])
                P.op("dve", lambda e: e.tensor_copy(out=AL[:], in_=t16[:]), reads=["sd_t16"], writes=["sd_AL"])

                def btok(ci):
                    pv = ps[7][:].bitcast(BF16)
                    Btk = Btk_l[ci % 2]
                    P.op("pe", lambda e: e.transpose(pv[:, 512:640], BTm[:, ci * 128:(ci + 1) * 128], idb[:]), reads=["sd_BTm", "identb"], writes=[("ps", 7)])
                    self.evac("act", Btk[:], pv[:, 512:640], [("ps", 7)], [f"sd_Btk{ci % 2}"])

                def xscale(dst, dkey, src_scale, ci, d):
                    P.op("dve", lambda e: e.tensor_tensor(out=dst[:], in0=h3(XS[:, ci, :]),
                                                           in1=src_scale[:, ci, d * NH:(d + 1) * NH].unsqueeze(2).to_broadcast([128, NH, 64]), op=ALU.mult),
                         reads=["sd_XS", "sd_DT", "sd_DDT"], writes=[dkey])

                def state_prep(ci, d, bank=5):
                    XDD = XDD_l[ci % 2]
                    Btk = Btk_l[ci % 2]
                    xscale(XDD, f"sd_XDD{ci % 2}", DDT, ci, d)
                    btok(ci)
                    P.op("pe", lambda e: e.matmul(ps[bank][:, 0:NW], Btk[:], XDD[:].rearrange("p h q -> p (h q)"), start=True, stop=True),
                         reads=[f"sd_Btk{ci % 2}", f"sd_XDD{ci % 2}"], writes=[("ps", bank)])

                def state_apply(ci, d, bank=5):
                    P.op("dve", lambda e: e.tensor_tensor(out=h3(Hst[:]), in0=h3(Hst[:]),
                                                          in1=CDX[:, ci, d * NH:(d + 1) * NH].unsqueeze(2).to_broadcast([128, NH, 64]), op=ALU.mult),
                         reads=["sd_Hst", "sd_CDX"], writes=["sd_Hst"])
                    P.op("dve", lambda e: e.tensor_tensor(out=Hst[:], in0=Hst[:], in1=ps[bank][:, 0:NW], op=ALU.add), reads=["sd_Hst", ("ps", bank)], writes=["sd_Hst"])

                hp_view = lambda ci: yT[:, k0:k0 + NF, ci * 128:(ci + 1) * 128]
                hkey = lambda ci: ("yTr", pi, ci)
                P.op("dve", lambda e: e.memset(Hst[:], 0.0), writes=["sd_Hst"])
                state_prep(15, 1, 5)
                for ci in range(15, -1, -1):
                    if ci - 1 > 0:
                        state_prep(ci - 1, 1, 5 + (ci % 2))
                    P.op("act", lambda e, ci=ci: e.copy(out=hp_view(ci), in_=Hst[:].rearrange("p (a b) -> p a b", a=NF)), reads=["sd_Hst"], writes=[hkey(ci)])
                    if ci > 0:
                        state_apply(ci, 1, 5 + ((ci + 1) % 2))
                P.op("dve", lambda e: e.memset(Hst[:], 0.0), writes=["sd_Hst"])
                dkeys = lambda d: (f"sd_RH{d}", f"sd_RL{d}", f"sd_E{d}", f"sd_CBM{d}", f"sd_XDT{d}", f"sd_tY{d}")

                def stage_A(ci):
                    for d in range(2):
                        RH, RL = RH_l[d], RL_l[d]
                        kRH, kRL = dkeys(d)[0:2]
                        mrow = MKb[:, 0 if d == 0 else 2, :]
                        for (R, A_, rk) in ((RH, AH, kRH), (RL, AL, kRL)):
                            P.op("dve", lambda e, R=R, A_=A_, d=d, mrow=mrow: e.tensor_tensor(
                                out=R[:], in0=A_[:, ci, d * NH:(d + 1) * NH].unsqueeze(2).to_broadcast([128, NH, 128]),
                                in1=mrow.unsqueeze(1).to_broadcast([128, NH, 128]), op=ALU.mult), reads=["sd_AH", "sd_AL", "MKb"], writes=[rk])

                def stage_Z(ci):
                    csl = slice(ci * 128, (ci + 1) * 128)
                    cp = ci % 2
                    P.op("pe", lambda e: e.matmul(ps[2][:, 0:128], BTm[:, csl], CTf[:, csl], start=True, stop=True), reads=["sd_BTm", "sd_CTf"], writes=[("ps", 2)])
                    if ci % 4 == 0:
                        for c4 in range(4):
                            csl4 = slice((ci + c4) * 128, (ci + c4 + 1) * 128)
                            for kt in range(8):
                                P.op("pe", lambda e, kt=kt, csl4=csl4: e.matmul(ps[6][:, 0:NW], hT[:, kt, csl4], Wz[:, kt, :], start=(kt == 0), stop=(kt == 7)),
                                     reads=["hT", "sd_Wz"], writes=[("ps", 6)])
                            P.op("act", lambda e, c4=c4: e.activation(out=SZ4[:, c4, :], in_=ps[6][:, 0:NW], func=AF.Silu), reads=[("ps", 6)], writes=[("sd_SZ", c4)])
                    P.op("act", lambda e: e.copy(out=Hbf_l[cp][:], in_=Hst[:]), reads=["sd_Hst"], writes=[f"sd_Hbf{cp}"])
                    P.op("act", lambda e: e.copy(out=HPt_l[cp][:].rearrange("p (a b) -> p a b", a=NF), in_=hp_view(ci)), reads=[hkey(ci)], writes=[f"sd_HPt{cp}"])

                def stage_B(ci):
                    for d in range(2):
                        RH, RL, Et = RH_l[d], RL_l[d], Et_l[d]
                        kRH, kRL, kE = dkeys(d)[0:3]
                        mlhs = MKb[:, 1 if d == 0 else 3, :]
                        P.op("pe", lambda e: e.matmul(ps[d][:, 0:NH * 128], mlhs, RH[:].rearrange("p h i -> p (h i)"), start=True, stop=False),
                             reads=["MKb", kRH], writes=[("ps", d)])
                        P.op("pe", lambda e: e.matmul(ps[d][:, 0:NH * 128], mlhs, RL[:].rearrange("p h i -> p (h i)"), start=False, stop=True),
                             reads=["MKb", kRL], writes=[("ps", d)])
                        P.op("act", lambda e: e.activation(out=Et[:].rearrange("p h i -> p (h i)"), in_=ps[d][:, 0:NH * 128], func=AF.Exp),
                             reads=[("ps", d)], writes=[kE])

                def stage_C(ci):
                    for d in range(2):
                        Et, CBM, XDT = Et_l[d], CBM_l[d], XDT_l[d]
                        kE, kCBM, kXDT = dkeys(d)[2:5]
                        xscale(XDT, kXDT, DT, ci, d)
                        P.op("dve", lambda e: e.tensor_tensor(out=CBM[:], in0=ps[2][:, 0:128], in1=MK[:, 4 + d, :], op=ALU.mult), reads=[("ps", 2), "MK"], writes=[kCBM])
                        P.op("dve", lambda e: e.tensor_tensor(out=Et[:], in0=Et[:], in1=CBM[:].unsqueeze(1).to_broadcast([128, NH, 128]), op=ALU.mult),
                             reads=[kE, kCBM], writes=[kE])

                def stage_D(ci):
                    csl = slice(ci * 128, (ci + 1) * 128)
                    cp = ci % 2
                    for d in range(2):
                        Et, XDT = Et_l[d], XDT_l[d]
                        kE, kXDT = dkeys(d)[2], dkeys(d)[4]
                        bk = 3 + d
                        for h in range(NH):
                            P.op("pe", lambda e, h=h: e.matmul(ps[bk][:, h * 64:(h + 1) * 64], Et[:, h, :], XDT[:, h, :], start=True, stop=True),
                                 reads=[kE, kXDT], writes=[("ps", bk)])
                        hsrc, hk = (Hbf_l[cp], f"sd_Hbf{cp}") if d == 0 else (HPt_l[cp], f"sd_HPt{cp}")
                        P.op("pe", lambda e: e.matmul(ps[bk][:, NW:2 * NW], CTf[:, csl], hsrc[:], start=True, stop=True), reads=["sd_CTf", hk], writes=[("ps", bk)])

                def stage_E(ci):
                    cp = ci % 2
                    Yacc, kYacc = Yacc_l[cp], f"sd_Yacc{cp}"
                    for d in range(2):
                        tY, ktY = tY_l[d], dkeys(d)[5]
                        bk = 3 + d
                        P.op("dve", lambda e: e.tensor_tensor(out=tY[:], in0=h3(ps[bk][:, NW:2 * NW]),
                                                              in1=ECS[:, ci, d * NH:(d + 1) * NH].unsqueeze(2).to_broadcast([128, NH, 64]), op=ALU.mult),
                             reads=[("ps", bk), "sd_ECS"], writes=[ktY])
                        if d == 0:
                            P.op("dve", lambda e: e.tensor_tensor(out=Yacc[:], in0=tY[:], in1=h3(ps[bk][:, 0:NW]), op=ALU.add),
                                 reads=[ktY, ("ps", bk)], writes=[kYacc])
                        else:
                            P.op("dve", lambda e: e.tensor_tensor(out=tY[:], in0=tY[:], in1=h3(ps[bk][:, 0:NW]), op=ALU.add),
                                 reads=[ktY, ("ps", bk)], writes=[ktY])
                            P.op("dve", lambda e: e.tensor_add(out=Yacc[:], in0=Yacc[:], in1=tY[:]), reads=[ktY, kYacc], writes=[kYacc])
                    tY = tY_l[0]
                    P.op("dve", lambda e: e.tensor_tensor(out=tY[:], in0=h3(XS[:, ci, :]),
                                                          in1=self.Dsk[:, e_, hsl].unsqueeze(2).to_broadcast([128, NH, 64]), op=ALU.mult),
                         reads=["sd_XS", "Dsk"], writes=["sd_tY0"])
                    P.op("dve", lambda e: e.tensor_add(out=Yacc[:], in0=Yacc[:], in1=tY[:]), reads=["sd_tY0", kYacc], writes=[kYacc])

                def stage_T(ci):
                    csl = slice(ci * 128, (ci + 1) * 128)
                    cp = ci % 2
                    Yacc, GT = Yacc_l[cp], GT_l[cp]
                    kYacc, kSZ, kGT, kHPt = f"sd_Yacc{cp}", ("sd_SZ", ci % 4), f"sd_GT{cp}", f"sd_HPt{cp}"
                    P.op("dve", lambda e: e.tensor_tensor(out=GT[:], in0=Yacc[:].rearrange("p h q -> p (h q)"), in1=SZ4[:, ci % 4, :], op=ALU.mult),
                         reads=[kYacc, kSZ], writes=[kGT])
                    pv = ps[7][:].bitcast(BF16)
                    for fl in range(NF):
                        P.op("pe", lambda e, fl=fl: e.transpose(pv[:, fl * 128:(fl + 1) * 128], GT[:, fl * 128:(fl + 1) * 128], idb[:]),
                             reads=[kGT, "identb"], writes=[("ps", 7)])
                    self.evac("act", yT[:, k0:k0 + NF, csl], pv[:, 0:NF * 128].rearrange("p (f t) -> p f t", f=NF), [("ps", 7), kHPt], [hkey(ci)])

                stage_A(0)
                for ci in range(16):
                    stage_Z(ci)
                    stage_B(ci)
                    if ci < 15:
                        state_prep(ci, 0, 5)
                    if ci + 1 < 16:
                        stage_A(ci + 1)
                    stage_C(ci)
                    stage_D(ci)
                    if ci < 15:
                        state_apply(ci, 0, 5)
                    stage_E(ci)
                    stage_T(ci)
            P.barrier()
        with contextlib.ExitStack() as st3:
            SQ = [self.sb(st3, f"sd_SQ{i}", [128, L], BF16) for i in range(2)]
            for kt in range(8):
                sq = SQ[kt % 2]
                P.op("dve", lambda e, kt=kt, sq=sq: e.tensor_tensor(out=sq[:], in0=yT[:, kt, :], in1=yT[:, kt, :], op=ALU.mult),
                     reads=["ab_yT"], writes=[f"sd_SQ{kt % 2}"])
                for j in range(16):
                    P.op("pe", lambda e, j=j, kt=kt, sq=sq: e.matmul(ps[0][:, j:j + 1], sq[:, j::16], self.onesb[:], start=(kt == 0 and j == 0), stop=(kt == 7)),
                         reads=[f"sd_SQ{kt % 2}", "onesb"], writes=[("ps", 0)])
            P.op("act", lambda e: e.activation(out=self.rs_ssd[:], in_=ps[0][:, 0:16], func=AF.Ln, bias=self.epsc[:, 0:1], scale=1.0 / 1024),
                 reads=[("ps", 0), "epsc"], writes=["rs_ssd"])
            P.op("act", lambda e: e.activation(out=self.rs_ssd[:], in_=self.rs_ssd[:], func=AF.Exp, scale=-0.5), reads=["rs_ssd"], writes=["rs_ssd"])
            P.barrier()


class Builder(S5Mixin, ABMixin, SSDMixin, BuilderBase):
    pass
```

```python
import contextlib
import math
import numpy as np
import concourse.bass as bass
import concourse.mybir as mybir
from concourse.bass_utils import run_bass_kernel_spmd

F32 = mybir.dt.float32
BF16 = mybir.dt.bfloat16
AF = mybir.ActivationFunctionType
ALU = mybir.AluOpType
AX = mybir.AxisListType

L = 2048
D = 1024
EPS = 1e-5
IN_AB = 6432
N_CORES = 8


class _Rec:
    def __init__(self):
        self.call = None

    def __getattr__(self, name):
        def f(*a, **k):
            self.call = (name, a, k)
            return self
        return f


def _record(fn):
    if fn is None:
        return None
    r = _Rec()
    fn(r)
    assert r.call is not None
    return r.call


class Prog:
    ENGS = ("pe", "act", "dve", "pool", "sp")

    def __init__(self, nc, same_engine_sync=True):
        self.nc = nc
        self.same = same_engine_sync
        self.streams = {e: [] for e in self.ENGS}
        self.cnt = {e: 0 for e in self.ENGS}
        self.dcnt = {e: 0 for e in self.ENGS}
        self.seen = {e: {} for e in self.ENGS}
        self.lastw = {}
        self.readers = {}
        self.n_ops = 0

    def _deps(self, eng, reads, writes):
        ev = []
        for k in reads:
            w = self.lastw.get(k)
            if w is not None:
                ev.append(w)
        for k in writes:
            w = self.lastw.get(k)
            if w is not None:
                ev.append(w)
            ev.extend(self.readers.get(k, ()))
        waits = {}
        seen = self.seen[eng]
        own = "c_" + eng
        for (s, v) in ev:
            if s == own and (eng == "pe" or not self.same):
                continue
            if seen.get(s, 0) >= v:
                continue
            if waits.get(s, 0) < v:
                waits[s] = v
        for s, v in waits.items():
            seen[s] = v
        return list(waits.items())

    def _commit(self, tok, reads, writes):
        for k in reads:
            self.readers.setdefault(k, []).append(tok)
        for k in writes:
            self.lastw[k] = tok
            self.readers[k] = []

    def op(self, eng, fn, reads=(), writes=()):
        waits = self._deps(eng, reads, writes)
        self.cnt[eng] += 1
        tok = ("c_" + eng, self.cnt[eng])
        self.streams[eng].append((waits, _record(fn), tok[0], 1))
        self._commit(tok, reads, writes)
        self.n_ops += 1

    NDS = 16

    def dma(self, eng, fn, reads=(), writes=()):
        waits = self._deps(eng, reads, writes)
        n = self.dcnt[eng]
        self.dcnt[eng] += 1
        r = n % self.NDS
        k = n // self.NDS + 1
        sname = f"d_{eng}_{r}"
        if k > 1 and self.seen[eng].get(sname, 0) < 16 * (k - 1):
            waits = [w for w in waits if w[0] != sname] + [(sname, 16 * (k - 1))]
            self.seen[eng][sname] = 16 * (k - 1)
        tok = (sname, 16 * k)
        self.streams[eng].append((waits, _record(fn), sname, 16))
        self._commit(tok, reads, writes)
        self.n_ops += 1

    def all_tokens(self):
        fin = []
        for e in self.ENGS:
            if self.cnt[e]:
                fin.append(("c_" + e, self.cnt[e]))
            n = self.dcnt[e]
            for r in range(min(n, self.NDS)):
                k = (n - 1 - r) // self.NDS + 1
                fin.append((f"d_{e}_{r}", 16 * k))
        return fin

    def barrier(self):
        fin = self.all_tokens()
        for e in self.ENGS:
            waits = []
            for (s, v) in fin:
                if self.seen[e].get(s, 0) >= v:
                    continue
                if s == "c_" + e and e == "pe":
                    continue
                waits.append((s, v))
                self.seen[e][s] = v
            if waits:
                self.streams[e].append((waits, None, None, 0))

    def emit(self, final_wait_eng="sp"):
        nc = self.nc
        names = ["c_" + e for e in self.ENGS]
        for e in self.ENGS:
            names += [f"d_{e}_{r}" for r in range(min(self.dcnt[e], self.NDS))]
        fin = self.all_tokens()
        with contextlib.ExitStack() as st:
            sems = {n: st.enter_context(nc.semaphore(n)) for n in names}
            block = st.enter_context(nc.Block())

            def mk(ename):
                def body(engine):
                    for (waits, fn, sname, inc) in self.streams[ename]:
                        for (s, v) in waits:
                            engine.wait_ge(sems[s], v)
                        if fn is not None:
                            name, a, k = fn
                            ins = getattr(engine, name)(*a, **k)
                            ins.then_inc(sems[sname], inc)
                    if ename == final_wait_eng:
                        for (s, v) in fin:
                            engine.wait_ge(sems[s], v)
                return body

            block.tensor(mk("pe"))
            block.scalar(mk("act"))
            block.vector(mk("dve"))
            block.gpsimd(mk("pool"))
            block.sync(mk("sp"))


class BuilderBase:
    def __init__(self, nseq, layers, final_norm=True):
        self.nseq = nseq
        self.layers = layers
        self.final_norm = final_norm
        self.nc = bass.Bass("TRN2", target_bir_lowering=False)
        self.P = Prog(self.nc)
        self.dram = {}
        self.uid = 0
        import os
        self.stop = os.environ.get("K_STOP", "")

    def din(self, name, shape, dt=F32):
        t = self.nc.dram_tensor(name, list(shape), dt, kind="ExternalInput").ap()
        self.dram[name] = t
        return t

    def sb(self, st, name, shape, dt):
        self.uid += 1
        return st.enter_context(self.nc.sbuf_tensor(f"{name}_u{self.uid}", list(shape), dt))

    def build(self):
        nc, P = self.nc, self.P
        ns = self.nseq
        x_all = self.din("x_all", [ns, L, D])
        self.norm_w = self.din("norm_w", [4, D])
        self.final_norm_w = self.din("final_norm_w", [D])
        ident_d = self.din("ident", [128, 128])
        y_all = nc.dram_tensor("y_all", [ns, L, D], F32, kind="ExternalOutput").ap()
        self.declare_weights()
        with contextlib.ExitStack() as st0:
            self._st_small = st0
            self.identb = self.sb(st0, "identb", [128, 128], BF16)
            self.nwT = self.sb(st0, "nwT", [128, 4, 8], F32)
            self.ss = self.sb(st0, "ss", [128, 16], F32)
            self.rstd = self.sb(st0, "rstd", [128, 16], F32)
            self.epsc = self.sb(st0, "epsc", [128, 1], F32)
            self.A16 = self.sb(st0, "A16", [128, 2, 2, 64], F32)
            self.ps = [st0.enter_context(nc.psum_tensor(f"ps{i}", [128, 512], F32)) for i in range(8)]
            P.op("dve", lambda e: e.memset(self.epsc[:], EPS), writes=["epsc"])
            P.dma("pool", lambda e: e.dma_start(out=self.identb[:], in_=ident_d), writes=["identb"])
            P.dma("sp", lambda e: e.dma_start(out=self.nwT[:], in_=self.norm_w.rearrange("l (k p) -> p l k", p=128),
                                              allow_slow_non_contiguous=True), writes=["nw"])
            self.prologue()
            st = st0
            self.x_sb = self.sb(st, "x_sb", [128, 16, D], F32)
            self.hT = self.sb(st, "hT", [128, 8, L], BF16)
            x_sb = self.x_sb
            for s in range(ns):
                xv = x_all[s].rearrange("(c j) d -> c j d", j=16)
                for q in range(4):
                    P.dma("sp", lambda e, q=q, xv=xv: e.dma_start(out=x_sb[:, 4 * q:4 * q + 4, :], in_=xv[:, 4 * q:4 * q + 4, :]),
                          writes=[("x", j) for j in range(4 * q, 4 * q + 4)])
                for (kind, idx, lnum) in self.layers:
                    self.rms_to_hT(lnum)
                    if self.stop in ("pro", "p1", "p2", "p3", "p4", "p5"):
                        continue
                    if kind == "s5":
                        self.s5_layer(idx)
                    elif kind == "ab":
                        self.ab_layer(idx, lnum)
                yv = y_all[s].rearrange("(c j) d -> c j d", j=16)
                with contextlib.ExitStack() as stf:
                    obs = [self.sb(stf, f"ob{i}", [128, D], F32) for i in range(2)]
                    self.junk = self.sb(stf, "junk", [128, D], BF16)
                    self.fnw = self.sb(stf, "fnw", [128, D], F32)
                    P.dma("sp", lambda e: e.dma_start(out=self.fnw[:], in_=self.final_norm_w.partition_broadcast(128)), writes=["nw"])
                    for j in range(16):
                        ob = obs[j % 2]
                        okey = f"ob{j % 2}"
                        if self.final_norm:
                            self.rms_stats(j)
                            P.op("dve", lambda e, j=j, ob=ob: e.scalar_tensor_tensor(
                                out=ob[:], in0=x_sb[:, j, :], scalar=self.rstd[:, j:j + 1], in1=self.fnw[:],
                                op0=ALU.mult, op1=ALU.mult), reads=[("x", j), "rstd", "nw"], writes=[okey])
                        else:
                            P.op("dve", lambda e, j=j, ob=ob: e.tensor_copy(out=ob[:], in_=x_sb[:, j, :]),
                                 reads=[("x", j)], writes=[okey])
                        P.dma("sp", lambda e, j=j, ob=ob, yv=yv: e.dma_start(out=yv[:, j, :], in_=ob[:]),
                              reads=[okey], writes=[("yout", s, j)])
                    P.barrier()
            P.emit()
        return nc

    def declare_weights(self):
        kinds = set(k for (k, _, _) in self.layers)
        self.kinds = kinds
        if "s5" in kinds:
            self.s5_declare()
        if "ab" in kinds:
            self.ab_declare()

    def prologue(self):
        if "s5" in self.kinds:
            for o in sorted(set(i for (k, i, _) in self.layers if k == "s5")):
                self.s5_prologue(o)
        if "ab" in self.kinds:
            self.ab_prologue()

    def rms_stats(self, j):
        P = self.P
        P.op("act", lambda e: e.activation(out=self.junk[:], in_=self.x_sb[:, j, :], func=AF.Square,
                                           accum_out=self.ss[:, j:j + 1]),
             reads=[("x", j)], writes=["junk", "ss"])
        P.op("act", lambda e: e.activation(out=self.rstd[:, j:j + 1], in_=self.ss[:, j:j + 1], func=AF.Sqrt,
                                           bias=self.epsc[:, 0:1], scale=1.0 / D),
             reads=["ss", "epsc"], writes=["rstd"])
        P.op("dve", lambda e: e.reciprocal(out=self.rstd[:, j:j + 1], in_=self.rstd[:, j:j + 1]),
             reads=["rstd"], writes=["rstd"])

    def rms_to_hT(self, lnum):
        P = self.P
        with contextlib.ExitStack() as st:
            hbs = [self.sb(st, f"hb{i}", [128, D], BF16) for i in range(2)]
            self.junk = self.sb(st, "junk", [128, D], BF16)
            for j in range(16):
                self.rms_stats(j)
                hb = hbs[j % 2]
                hk = f"hb{j % 2}"
                P.op("dve", lambda e, j=j, hb=hb: e.tensor_scalar(out=hb[:], in0=self.x_sb[:, j, :], scalar1=self.rstd[:, j:j + 1],
                                                                  scalar2=None, op0=ALU.mult), reads=[("x", j), "rstd"], writes=[hk])
                pt = self.ps[j % 2]
                pk = ("ps", j % 2)
                ptv = pt[:].bitcast(BF16)
                for kt in range(8):
                    P.op("pe", lambda e, kt=kt, hb=hb, ptv=ptv: e.transpose(ptv[:, kt * 128:(kt + 1) * 128],
                                                                             hb[:, kt * 128:(kt + 1) * 128], self.identb[:]),
                         reads=[hk, "identb"], writes=[pk])
                P.op("dve", lambda e, j=j, ptv=ptv: e.tensor_tensor(out=self.hT[:, :, j::16], in0=ptv.rearrange("p (k c) -> p k c", k=8),
                                                                    in1=self.nwT[:, lnum, :].unsqueeze(2).to_broadcast([128, 8, 128]), op=ALU.mult),
                     reads=[pk, "nw"], writes=["hT"])
            P.barrier()


def _consts():
    mf, mb, idm = _s5_masks()
    return {"ident": np.eye(128, dtype=np.float32), "identf": np.eye(128, dtype=np.float32), "att_bidx": _att_bidx(), "ssd_masks": _ssd_masks(),
            "s5_ktab": _s5_ktab(), "s5_mf": mf, "s5_mb": mb, "s5_idm": idm}


_CACHE = {}


def kernel(**inputs):
    xp = np.asarray(inputs["x_prompt"], dtype=np.float32)
    xs = np.asarray(inputs["x_sample"], dtype=np.float32)
    layers = [("ab", 0, 0), ("s5", 0, 1), ("ab", 1, 2), ("s5", 1, 3)]
    b = Builder(6, layers)
    nc = b.build()
    consts = _consts()
    in_maps = []
    for i in range(N_CORES):
        xa = np.concatenate([xp[2 * i:2 * i + 2], xs[4 * i:4 * i + 4]], axis=0)
        m = {"x_all": np.ascontiguousarray(xa)}
        for k in b.dram:
            if k == "x_all":
                continue
            if k in consts:
                m[k] = consts[k]
            else:
                m[k] = np.ascontiguousarray(np.asarray(inputs[k], dtype=np.float32))
        in_maps.append(m)
    res = run_bass_kernel_spmd(nc, in_maps, core_ids=list(range(N_CORES)))
    yp = np.empty_like(xp)
    ys = np.empty_like(xs)
    for i in range(N_CORES):
        y = res.results[i]["y_all"]
        yp[2 * i:2 * i + 2] = y[0:2]
        ys[4 * i:4 * i + 4] = y[2:6]
    return (yp, ys)


TWO_PI = 2.0 * math.pi


def _s5_ktab():
    kt = np.zeros((128, 5, 16), np.float32)
    idx = np.arange(16, dtype=np.float32)
    kt[:64, 0] = -idx
    kt[:64, 1] = 15 - idx
    kt[:64, 2] = idx
    kt[:64, 3] = idx + 1
    kt[64:, 0] = idx
    kt[64:, 1] = idx
    kt[64:, 2] = -idx
    kt[64:, 3] = 16 - idx
    kt[:, 4, 0] = 16
    kt[:, 4, 1] = 1
    return kt


def _s5_masks():
    j = (np.arange(256) // 16)[:, None]
    i = (np.arange(256) // 16)[None, :]
    mf = (i >= j).astype(np.float32).reshape(2, 128, 256)
    mb = (j >= i).astype(np.float32).reshape(2, 128, 256)
    idm = np.eye(256, dtype=np.float32).reshape(2, 128, 256)
    return mf, mb, idm


class S5Mixin:
    GB = 2

    def s5_declare(self):
        nc = self.nc
        for nm, shp in [("w_in_c", [2, 1024, 2048]), ("s5_lambda_re", [2, 2, 64, 64]), ("s5_lambda_im", [2, 2, 64, 64]),
                        ("s5_log_dt", [2, 2, 64]), ("s5_B_re", [2, 64, 64, 16]), ("s5_B_im", [2, 64, 64, 16]),
                        ("s5_C_re", [2, 2, 64, 16, 64]), ("s5_C_im", [2, 2, 64, 16, 64]), ("s5_D", [2, 1024]),
                        ("w_glu", [2, 1024, 1024]), ("b_glu", [2, 1024]), ("w_out_c", [2, 1024, 1024])]:
            setattr(self, nm, self.din(nm, shp))
        self.s5_ktab = self.din("s5_ktab", [128, 5, 16])
        self.s5_mf = self.din("s5_mf", [2, 128, 256])
        self.s5_mb = self.din("s5_mb", [2, 128, 256])
        self.s5_idm = self.din("s5_idm", [2, 128, 256])
        self.identf_d = self.din("identf", [128, 128])
        import os
        kd = "ExternalOutput" if os.environ.get("K_DEBUG") else "Internal"
        self.LS = nc.dram_tensor("s5_LS", [2, 64, 128, 512], BF16, kind=kd).ap()
        self.SWS = nc.dram_tensor("s5_SWS", [2, 64, 128, 512], BF16, kind=kd).ap()
        self.WXS = nc.dram_tensor("s5_WXS", [2, 64, 128, 1024], BF16, kind=kd).ap()

    def s5_prologue(self, o):
        nc, P = self.nc, self.P
        I32 = mybir.dt.int32
        GB = self.GB
        with contextlib.ExitStack() as st:
            sb = lambda n, s, d=F32: self.sb(st, f"s5p_{n}", s, d)
            identf = sb("identf", [128, 128]); ktab = sb("ktab", [128, 5, 16])
            mf = sb("mf", [128, 2, 256]); mb = sb("mb", [128, 2, 256]); idm = sb("idm", [128, 2, 256])
            P.dma("sp", lambda e: e.dma_start(out=identf[:], in_=self.identf_d), writes=["s5p_identf"])
            P.dma("sp", lambda e: e.dma_start(out=ktab[:], in_=self.s5_ktab), writes=["s5p_ktab"])
            P.dma("sp", lambda e: e.dma_start(out=mf[:], in_=self.s5_mf.rearrange("k p f -> p k f")), writes=["s5p_mf"])
            P.dma("sp", lambda e: e.dma_start(out=mb[:], in_=self.s5_mb.rearrange("k p f -> p k f")), writes=["s5p_mb"])
            P.dma("sp", lambda e: e.dma_start(out=idm[:], in_=self.s5_idm.rearrange("k p f -> p k f")), writes=["s5p_idm"])
            raw = sb("raw", [64, 2, 2, 64])
            P.dma("sp", lambda e: e.dma_start(out=raw[:, 0], in_=self.s5_lambda_re[o].rearrange("d g n -> g d n")), writes=["s5p_raw"])
            P.dma("sp", lambda e: e.dma_start(out=raw[:, 1], in_=self.s5_lambda_im[o].rearrange("d g n -> g d n")), writes=["s5p_raw"])
            LR = sb("LR", [128, 64]); LI = sb("LI", [128, 64]); STEP = sb("STEP", [128, 64])
            for ri in range(2):
                P.op("pe", lambda e, ri=ri: e.transpose(self.ps[0][:, ri * 64:(ri + 1) * 64], raw[:, ri, :, :], identf[0:64, 0:64]),
                     reads=["s5p_raw", "s5p_identf"], writes=[("ps", 0)])
            for d in range(2):
                hs = slice(d * 64, (d + 1) * 64)
                P.dma("sp", lambda e, hs=hs, d=d: e.dma_start(out=STEP[hs, :], in_=self.s5_log_dt[o, d].partition_broadcast(64)),
                      writes=["s5p_STEP"])
            P.op("dve", lambda e: e.tensor_copy(out=LR[:], in_=self.ps[0][:, 0:64]), reads=[("ps", 0)], writes=["s5p_LR"])
            P.op("dve", lambda e: e.tensor_copy(out=LI[:], in_=self.ps[0][:, 64:128]), reads=[("ps", 0)], writes=["s5p_LI"])
            P.op("act", lambda e: e.activation(out=STEP[:], in_=STEP[:], func=AF.Exp), reads=["s5p_STEP"], writes=["s5p_STEP"])
            LSt = sb("LSt", [128, 64]); TH = sb("TH", [128, 64])
            P.op("dve", lambda e: e.tensor_mul(out=LSt[:], in0=LR[:], in1=STEP[:]), reads=["s5p_LR", "s5p_STEP"], writes=["s5p_LSt"])
            P.op("dve", lambda e: e.tensor_mul(out=TH[:], in0=LI[:], in1=STEP[:]), reads=["s5p_LI", "s5p_STEP"], writes=["s5p_TH"])
            if self.stop == "p1":
                P.barrier(); return
            ER = [sb(f"ER{t}", [128, 64, 16]) for t in range(5)]
            EI = [sb(f"EI{t}", [128, 64, 16]) for t in range(5)]
            arg = sb("arg", [128, 64, 16]); ti = sb("ti", [128, 64, 16], I32); tf = sb("tf", [128, 64, 16]); tg = sb("tg", [128, 64, 16])
            mag = sb("mag", [128, 64, 16]); sn = sb("sn", [128, 64, 16])
            shp = [128, 64, 16]

            def reduce_turns(key):
                P.op("dve", lambda e: e.tensor_copy(out=ti[:], in_=arg[:]), reads=[key], writes=["s5p_ti"])
                P.op("dve", lambda e: e.tensor_copy(out=tf[:], in_=ti[:]), reads=["s5p_ti"], writes=["s5p_tf"])
                P.op("dve", lambda e: e.tensor_sub(out=arg[:], in0=arg[:], in1=tf[:]), reads=[key, "s5p_tf"], writes=[key])
                P.op("dve", lambda e: e.tensor_scalar(out=tg[:], in0=arg[:], scalar1=0.5, scalar2=None, op0=ALU.is_gt), reads=[key], writes=["s5p_tg"])
                P.op("dve", lambda e: e.tensor_sub(out=arg[:], in0=arg[:], in1=tg[:]), reads=[key, "s5p_tg"], writes=[key])
                P.op("dve", lambda e: e.tensor_scalar(out=tg[:], in0=arg[:], scalar1=-0.5, scalar2=None, op0=ALU.is_lt), reads=[key], writes=["s5p_tg"])
                P.op("dve", lambda e: e.tensor_add(out=arg[:], in0=arg[:], in1=tg[:]), reads=[key, "s5p_tg"], writes=[key])

            for t in range(5):
                kb = ktab[:, t, :].unsqueeze(1).to_broadcast(shp)
                thb = TH[:].unsqueeze(2).to_broadcast(shp)
                lsb = LSt[:].unsqueeze(2).to_broadcast(shp)
                P.op("dve", lambda e, kb=kb, lsb=lsb: e.tensor_tensor(out=mag[:], in0=lsb, in1=kb, op=ALU.mult),
                     reads=["s5p_LSt", "s5p_ktab"], writes=["s5p_mag"])
                P.op("act", lambda e: e.activation(out=mag[:], in_=mag[:], func=AF.Exp), reads=["s5p_mag"], writes=["s5p_mag"])
                for which in range(2):
                    P.op("dve", lambda e, kb=kb, thb=thb: e.tensor_tensor(out=arg[:], in0=thb, in1=kb, op=ALU.mult),
                         reads=["s5p_TH", "s5p_ktab"], writes=["s5p_arg"])
                    P.op("dve", lambda e, which=which: e.tensor_scalar(out=arg[:], in0=arg[:], scalar1=1.0 / TWO_PI,
                                                                        scalar2=0.25 * which, op0=ALU.mult, op1=ALU.add),
                         reads=["s5p_arg"], writes=["s5p_arg"])
                    reduce_turns("s5p_arg")
                    P.op("act", lambda e: e.activation(out=sn[:], in_=arg[:], func=AF.Sin, scale=TWO_PI), reads=["s5p_arg"], writes=["s5p_sn"])
                    dst = EI[t] if which == 0 else ER[t]
                    P.op("dve", lambda e, dst=dst: e.tensor_mul(out=dst[:], in0=mag[:], in1=sn[:]),
                         reads=["s5p_mag", "s5p_sn"], writes=[f"s5p_E{t}{which}"])
            if self.stop == "p2":
                P.barrier(); return
            ekeys = lambda t: [f"s5p_E{t}0", f"s5p_E{t}1"]
            P.op("dve", lambda e: e.tensor_copy(out=self.A16[:, o, 0, :], in_=ER[4][:, :, 0]), reads=ekeys(4), writes=["A16"])
            P.op("dve", lambda e: e.tensor_copy(out=self.A16[:, o, 1, :], in_=EI[4][:, :, 0]), reads=ekeys(4), writes=["A16"])
            nr = sb("nr", [128, 64]); den = sb("den", [128, 64]); t1 = sb("t1", [128, 64]); t2 = sb("t2", [128, 64])
            cr = sb("cr", [128, 64]); ci = sb("ci", [128, 64])
            a1r = ER[4][:, :, 1]; a1i = EI[4][:, :, 1]
            K = ["s5p_nr", "s5p_den", "s5p_t1", "s5p_t2", "s5p_cr", "s5p_ci", "s5p_LR", "s5p_LI"] + ekeys(4)
            ops = [
                lambda e: e.tensor_scalar(out=nr[:], in0=a1r, scalar1=-1.0, scalar2=None, op0=ALU.add),
                lambda e: e.tensor_mul(out=den[:], in0=LR[:], in1=LR[:]),
                lambda e: e.tensor_mul(out=t1[:], in0=LI[:], in1=LI[:]),
                lambda e: e.tensor_add(out=den[:], in0=den[:], in1=t1[:]),
                lambda e: e.reciprocal(out=den[:], in_=den[:]),
                lambda e: e.tensor_mul(out=t1[:], in0=nr[:], in1=LR[:]),
                lambda e: e.tensor_mul(out=t2[:], in0=a1i, in1=LI[:]),
                lambda e: e.tensor_add(out=t1[:], in0=t1[:], in1=t2[:]),
                lambda e: e.tensor_mul(out=cr[:], in0=t1[:], in1=den[:]),
                lambda e: e.tensor_mul(out=t1[:], in0=a1i, in1=LR[:]),
                lambda e: e.tensor_mul(out=t2[:], in0=nr[:], in1=LI[:]),
                lambda e: e.tensor_sub(out=t1[:], in0=t1[:], in1=t2[:]),
                lambda e: e.tensor_mul(out=ci[:], in0=t1[:], in1=den[:]),
            ]
            for f in ops:
                P.op("dve", f, reads=K, writes=K[:6])
            CAr = [sb(f"CAr{t}", shp) for t in range(2)]
            CAi = [sb(f"CAi{t}", shp) for t in range(2)]
            crb = cr[:].unsqueeze(2).to_broadcast(shp)
            cib = ci[:].unsqueeze(2).to_broadcast(shp)
            for t in range(2):
                kk = ["s5p_cr", "s5p_ci", "s5p_arg", "s5p_mag"] + ekeys(t)
                P.op("dve", lambda e, t=t: e.tensor_tensor(out=arg[:], in0=ER[t][:], in1=crb, op=ALU.mult), reads=kk, writes=["s5p_arg"])
                P.op("dve", lambda e, t=t: e.tensor_tensor(out=mag[:], in0=EI[t][:], in1=cib, op=ALU.mult), reads=kk, writes=["s5p_mag"])
                P.op("dve", lambda e, t=t: e.tensor_sub(out=CAr[t][:], in0=arg[:], in1=mag[:]), reads=kk, writes=[f"s5p_CAr{t}"])
                P.op("dve", lambda e, t=t: e.tensor_tensor(out=arg[:], in0=EI[t][:], in1=crb, op=ALU.mult), reads=kk, writes=["s5p_arg"])
                P.op("dve", lambda e, t=t: e.tensor_tensor(out=mag[:], in0=ER[t][:], in1=cib, op=ALU.mult), reads=kk, writes=["s5p_mag"])
                P.op("dve", lambda e, t=t: e.tensor_add(out=CAi[t][:], in0=arg[:], in1=mag[:]), reads=kk, writes=[f"s5p_CAi{t}"])
            if self.stop == "p3":
                P.barrier(); return
            BR = sb("BR", [128, 64, 16]); BI = sb("BI", [128, 64, 16])
            for d in range(2):
                hs = slice(d * 64, (d + 1) * 64)
                P.dma("sp", lambda e, hs=hs: e.dma_start(out=BR[hs], in_=self.s5_B_re[o].rearrange("g n c -> n g c")), writes=["s5p_BR"])
                P.dma("sp", lambda e, hs=hs: e.dma_start(out=BI[hs], in_=self.s5_B_im[o].rearrange("g n c -> n g c")), writes=["s5p_BI"])
            CR = sb("CR", [128, 64, 16]); CI = sb("CI", [128, 64, 16])
            craw = sb("craw", [128, 8, 2, 64])
            for ri, (src, dstc) in enumerate([(self.s5_C_re, CR), (self.s5_C_im, CI)]):
                for d in range(2):
                    P.dma("sp", lambda e, src=src, d=d: e.dma_start(out=craw[:, :, d, :], in_=src[o, d].rearrange("(gt g) c n -> (g c) gt n", g=8)),
                          writes=["s5p_craw"])
                for gt in range(8):
                    P.op("pe", lambda e, gt=gt: e.transpose(self.ps[1 + (gt // 4)][:, (gt % 4) * 128:(gt % 4 + 1) * 128],
                                                             craw[:, gt, :, :], identf[:]),
                         reads=["s5p_craw", "s5p_identf"], writes=[("ps", 1 + gt // 4)])
                for hh in range(2):
                    P.op("dve", lambda e, hh=hh, dstc=dstc: e.tensor_copy(
                        out=dstc[:, hh * 32:(hh + 1) * 32, :], in_=self.ps[1 + hh][:, :].rearrange("p (g c) -> p g c", c=16)),
                        reads=[("ps", 1 + hh)], writes=[f"s5p_C{ri}"])
            ckeys = ["s5p_C0", "s5p_C1"]
            DG = sb("DG", [128, 64])
            for jl in range(8):
                P.dma("sp", lambda e, jl=jl: e.dma_start(out=DG[jl * 16:(jl + 1) * 16, :], in_=self.s5_D[o].rearrange("(g c) -> c g", c=16),
                                                        allow_slow_non_contiguous=True), writes=["s5p_DG"])
            if self.stop == "p4":
                P.barrier(); return
            bshape = [128, GB, 16, 16]
            prod = {nm: sb(nm, bshape) for nm in ["Pr", "Pi", "Sr", "Si", "Qr", "QiN", "Wr", "Wi"]}
            tmpa = [sb(f"tmpa{i}", bshape) for i in range(2)]
            tmpb = [sb(f"tmpb{i}", bshape) for i in range(2)]
            Lout = [sb(f"Lout{i}", [128, GB, 2, 256], BF16) for i in range(2)]
            SWout = [sb(f"SWout{i}", [128, GB, 512], BF16) for i in range(2)]
            WXout = [sb(f"WXout{i}", [128, GB, 2, 2, 256], BF16) for i in range(2)]
            QFB = {nm: sb("FB" + nm, [128, 2, GB, 256]) for nm in ["Qr", "QiN"]}
            for i in range(2):
                P.op("dve", lambda e, i=i: e.memset(WXout[i][:], 0.0), writes=[f"s5p_WXout{i}"])
            for nm in ["Qr", "QiN"]:
                P.op("dve", lambda e, nm=nm: e.memset(QFB[nm][:], 0.0), writes=["s5p_FB" + nm])
            l1 = [sb(f"l1_{i}", [128, 256]) for i in range(2)]
            l2 = [sb(f"l2_{i}", [128, 256]) for i in range(2)]
            nb = 64 // GB
            for b in range(nb):
                g0 = b * GB
                gs = slice(g0, g0 + GB)
                pb = b % 2

                def cprod(eng, slot, outr, outi, Ar, Ai, Br, Bi, a_over_ch, keysA, keysB, neg_i=False):
                    Ab = lambda X: X[:, gs, :].unsqueeze(3).to_broadcast(bshape)
                    Bb = lambda X: X[:, gs, :].unsqueeze(2).to_broadcast(bshape)
                    ta, tb = tmpa[slot], tmpb[slot]
                    rk = keysA + keysB
                    ka, kb_ = f"s5p_tmpa{slot}", f"s5p_tmpb{slot}"
                    P.op(eng, lambda e: e.tensor_tensor(out=ta[:], in0=Ab(Ar), in1=Bb(Br), op=ALU.mult), reads=rk, writes=[ka])
                    P.op(eng, lambda e: e.tensor_tensor(out=tb[:], in0=Ab(Ai), in1=Bb(Bi), op=ALU.mult), reads=rk, writes=[kb_])
                    P.op(eng, lambda e: e.tensor_sub(out=prod[outr][:], in0=ta[:], in1=tb[:]), reads=[ka, kb_], writes=["s5p_" + outr])
                    P.op(eng, lambda e: e.tensor_tensor(out=ta[:], in0=Ab(Ar), in1=Bb(Bi), op=ALU.mult), reads=rk, writes=[ka])
                    P.op(eng, lambda e: e.tensor_tensor(out=tb[:], in0=Ab(Ai), in1=Bb(Br), op=ALU.mult), reads=rk, writes=[kb_])
                    if neg_i:
                        P.op("dve", lambda e: e.scalar_tensor_tensor(out=prod[outi][:], in0=ta[:], scalar=-1.0, in1=tb[:], op0=ALU.mult, op1=ALU.subtract),
                             reads=[ka, kb_], writes=["s5p_" + outi])
                    else:
                        P.op(eng, lambda e: e.tensor_add(out=prod[outi][:], in0=ta[:], in1=tb[:]), reads=[ka, kb_], writes=["s5p_" + outi])

                bk = ["s5p_BR", "s5p_BI"]
                cprod("dve", 0, "Pr", "Pi", CAr[0], CAi[0], BR, BI, True, ["s5p_CAr0", "s5p_CAi0"], bk)
                cprod("dve", 1, "Sr", "Si", CAr[1], CAi[1], BR, BI, True, ["s5p_CAr1", "s5p_CAi1"], bk)
                cprod("dve", 0, "Qr", "QiN", ER[2], EI[2], CR, CI, True, ekeys(2), ckeys, neg_i=True)
                cprod("dve", 1, "Wr", "Wi", ER[3], EI[3], CR, CI, True, ekeys(3), ckeys, neg_i=True)
                for nm in ["Qr", "QiN"]:
                    for d in range(2):
                        hs = slice(d * 64, (d + 1) * 64)
                        P.op("dve", lambda e, nm=nm, d=d, hs=hs: e.tensor_copy(out=QFB[nm][hs, d, :, :],
                                                                                 in_=prod[nm][hs].rearrange("p g i c -> p g (i c)")),
                             reads=["s5p_" + nm], writes=["s5p_FB" + nm])
                import os
                sub = int(os.environ.get("K_SUB", "99"))
                if sub == 0:
                    P.barrier(); return
                for gl in range(GB):
                    g = g0 + gl
                    for kh in range(2):
                        if sub == 1 and (gl, kh) == (0, 1):
                            P.barrier(); return
                        bank = 3 + (gl * 2 + kh) % 4
                        pt = self.ps[bank]
                        for d in range(2):
                            outp = pt[:, d * 256:(d + 1) * 256]
                            P.op("pe", lambda e, d=d, outp=outp, gl=gl, kh=kh: e.matmul(
                                outp, prod["Pr"][:, gl, kh * 8:(kh + 1) * 8, :], QFB["Qr"][:, d, gl, :], start=True, stop=False),
                                reads=["s5p_Pr", "s5p_FBQr"], writes=[("ps", bank)])
                            P.op("pe", lambda e, d=d, outp=outp, gl=gl, kh=kh: e.matmul(
                                outp, prod["Pi"][:, gl, kh * 8:(kh + 1) * 8, :], QFB["QiN"][:, d, gl, :], start=False, stop=True),
                                reads=["s5p_Pi", "s5p_FBQiN"], writes=[("ps", bank)])
                        sl = (gl * 2 + kh) % 2
                        P.op("dve", lambda e, pt=pt, kh=kh, sl=sl: e.tensor_tensor(out=l1[sl][:], in0=pt[:, 0:256], in1=mf[:, kh, :], op=ALU.mult),
                             reads=[("ps", bank), "s5p_mf"], writes=[f"s5p_l1_{sl}"])
                        P.op("dve", lambda e, pt=pt, kh=kh, sl=sl: e.tensor_tensor(out=l2[sl][:], in0=pt[:, 256:512], in1=mb[:, kh, :], op=ALU.mult),
                             reads=[("ps", bank), "s5p_mb"], writes=[f"s5p_l2_{sl}"])
                        P.op("dve", lambda e, sl=sl: e.tensor_add(out=l1[sl][:], in0=l1[sl][:], in1=l2[sl][:]),
                             reads=[f"s5p_l1_{sl}", f"s5p_l2_{sl}"], writes=[f"s5p_l1_{sl}"])
                        P.op("dve", lambda e, sl=sl, kh=kh, g=g, gl=gl: e.scalar_tensor_tensor(
                            out=Lout[pb][:, gl, kh, :], in0=idm[:, kh, :], scalar=DG[:, g:g + 1], in1=l1[sl][:], op0=ALU.mult, op1=ALU.add),
                            reads=[f"s5p_l1_{sl}", "s5p_idm", "s5p_DG"], writes=[f"s5p_Lout{pb}"])
                    if sub == 2:
                        P.barrier(); return
                    for kh in range(2):
                        for ri, nm in enumerate(["Sr", "Si"]):
                            q = kh * 2 + ri
                            P.op("pe", lambda e, q=q, nm=nm, gl=gl, kh=kh: e.transpose(
                                self.ps[7][:, q * 128:(q + 1) * 128], prod[nm][:, gl, kh * 8:(kh + 1) * 8, :], identf[:]),
                                reads=["s5p_" + nm, "s5p_identf"], writes=[("ps", 7)])
                    P.op("act", lambda e, gl=gl: e.copy(out=SWout[pb][:, gl, :], in_=self.ps[7][:]), reads=[("ps", 7)], writes=[f"s5p_SWout{pb}"])
                    for d in range(2):
                        hs = slice(d * 64, (d + 1) * 64)
                        for ri, nm in enumerate(["Wr", "Wi"]):
                            P.op("act", lambda e, gl=gl, d=d, hs=hs, ri=ri, nm=nm: e.copy(
                                out=WXout[pb][hs, gl, d, ri, :], in_=prod[nm][hs, gl].rearrange("p i c -> p (i c)")),
                                reads=["s5p_" + nm], writes=[f"s5p_WXout{pb}"])
                if self.stop == "p5":
                    P.barrier(); return
                P.dma("sp", lambda e, gs=gs, pb=pb: e.dma_start(out=self.LS[o, gs].rearrange("g p f -> p g f"),
                                                                in_=Lout[pb][:].rearrange("p g k f -> p g (k f)")),
                      reads=[f"s5p_Lout{pb}"], writes=[("LS", o)])
                P.dma("sp", lambda e, gs=gs, pb=pb: e.dma_start(out=self.SWS[o, gs].rearrange("g p f -> p g f"), in_=SWout[pb][:]),
                      reads=[f"s5p_SWout{pb}"], writes=[("SWS", o)])
                P.dma("sp", lambda e, gs=gs, pb=pb: e.dma_start(out=self.WXS[o, gs].rearrange("g p f -> p g f"),
                                                                in_=WXout[pb][:].rearrange("p g d k f -> p g (d k f)")),
                      reads=[f"s5p_WXout{pb}"], writes=[("WXS", o)])
            P.barrier()

    def s5_layer(self, o):
        nc, P = self.nc, self.P
        GB = self.GB
        x_sb, hT, ps = self.x_sb, self.hT, self.ps
        idb = self.identb

        def load_w(st, name, src_ap, cols):
            t = self.sb(st, name, [128, 8, cols], BF16)
            v = src_ap.rearrange("(kt p) f -> p kt f", p=128)
            for q in range(4):
                P.dma("pool", lambda e, q=q: e.dma_start(out=t[:, 2 * q:2 * q + 2, :], in_=v[:, 2 * q:2 * q + 2, :]), writes=[name])
            return t

        with contextlib.ExitStack() as stL:
            Z = self.sb(stL, "s5_Z", [128, 16, D], BF16)
            zk = lambda j: ("s5_Z", j)
            with contextlib.ExitStack() as st:
                Wu = load_w(st, "s5_Wu", self.w_in_c[o][:, 0:1024], 1024)
                for j in range(16):
                    for half in range(2):
                        bank = (j * 2 + half) % 4
                        for kt in range(8):
                            P.op("pe", lambda e, j=j, half=half, kt=kt, bank=bank: e.matmul(
                                ps[bank][:], hT[:, kt, j::16], Wu[:, kt, half * 512:(half + 1) * 512], start=(kt == 0), stop=(kt == 7)),
                                reads=["hT", "s5_Wu"], writes=[("ps", bank)])
                        eng = "act" if (j + half) % 2 == 0 else "dve"
                        self.evac(eng, Z[:, j, half * 512:(half + 1) * 512], ps[bank][:], [("ps", bank)], [zk(j)])
                P.barrier()
            if self.stop == "A":
                return
            with contextlib.ExitStack() as st:
                SRI = self.sb(st, "s5_SRI", [128, 2, 64, 128], BF16)
                UG = [self.sb(st, f"s5_UG{i}", [128, 2, 128], BF16) for i in range(2)]
                XS = [self.sb(st, f"s5_XS{i}", [128, 2, 64], F32) for i in range(2)]
                identf = self.sb(st, "s5_identf", [128, 128], F32)
                P.dma("sp", lambda e: e.dma_start(out=identf[:], in_=self.identf_d), writes=["s5_identf"])

                STG = [self.sb(st, "s5_STG0", [128, 8, 16, 16], BF16)] * 2

                def make_ug(g, slot):
                    bank = slot
                    ft, g8 = g // 8, g % 8
                    stg = STG[0]
                    sk_ = "s5_STG0"
                    if g8 == 0:
                        P.op("dve", lambda e, stg=stg, ft=ft: e.tensor_copy(
                            out=stg[:], in_=Z[:, :, ft * 128:(ft + 1) * 128].rearrange("p j (g c) -> p g j c", c=16)),
                            reads=[zk(j) for j in range(16)], writes=[sk_])
                    for kh in range(2):
                        P.op("pe", lambda e, kh=kh, bank=bank, stg=stg, g8=g8: e.transpose(
                            ps[bank][:].bitcast(BF16)[:, kh * 128:(kh + 1) * 128], stg[:, g8, kh * 8:(kh + 1) * 8, :], idb[:]),
                            reads=[sk_, "identb"], writes=[("ps", bank)])
                    self.evac("act", UG[slot][:], ps[bank][:].bitcast(BF16)[:, 0:256].rearrange("p (k c) -> p k c", k=2), [("ps", bank)], [f"s5_UG{slot}"])

                stB = contextlib.ExitStack()
                SWt = [self.sb(stB, f"s5_SWt{i}", [128, GB, 512], BF16) for i in range(2)]
                for g in range(64):
                    gl, b = g % GB, g // GB
                    pb = b % 2
                    if gl == 0:
                        P.dma("sp", lambda e, b=b, pb=pb: e.dma_start(out=SWt[pb][:], in_=self.SWS[o, b * GB:(b + 1) * GB].rearrange("g p f -> p g f")),
                              reads=[("SWS", o)], writes=[f"s5_SWt{pb}"])
                    slot = g % 2
                    make_ug(g, slot)
                    bank = 2 + g % 2
                    for d in range(2):
                        for ri in range(2):
                            for kh in range(2):
                                rhs = UG[slot][:, kh, :] if d == 0 else UG[slot][:, kh, ::-1]
                                q = kh * 2 + ri
                                P.op("pe", lambda e, d=d, ri=ri, kh=kh, rhs=rhs, q=q, gl=gl, pb=pb, bank=bank: e.matmul(
                                    ps[bank][:, (d * 2 + ri) * 128:(d * 2 + ri + 1) * 128], SWt[pb][:, gl, q * 128:(q + 1) * 128], rhs,
                                    start=(kh == 0), stop=(kh == 1)),
                                    reads=[f"s5_SWt{pb}", f"s5_UG{slot}"], writes=[("ps", bank)])
                    for d in range(2):
                        hs = slice(d * 64, (d + 1) * 64)
                        self.evac("dve" if d == 0 else "act", SRI[hs, :, g, :],
                                  ps[bank][hs, d * 256:(d + 1) * 256].rearrange("p (r c) -> p r c", r=2), [("ps", bank)], [("s5_S", g, d)])
                P.barrier()
                stB.close()
                if self.stop == "B":
                    return
                P.op("dve", lambda e: e.memset(XS[0][:], 0.0), writes=["s5_XS0"])
                AAt = self.sb(st, "s5_AAt", [128, 2, 64], F32)
                ABt = self.sb(st, "s5_ABt", [128, 2, 64], F32)
                U = self.sb(st, "s5_U", [128, 2, 64], F32)
                V = self.sb(st, "s5_V", [128, 2, 64], F32)
                P.op("dve", lambda e: e.tensor_copy(out=AAt[:, 0, :], in_=self.A16[:, o, 0, :]), reads=["A16"], writes=["s5_AAt"])
                P.op("dve", lambda e: e.tensor_copy(out=AAt[:, 1, :], in_=self.A16[:, o, 0, :]), reads=["A16"], writes=["s5_AAt"])
                P.op("dve", lambda e: e.tensor_scalar(out=ABt[:, 0, :], in0=self.A16[:, o, 1, :], scalar1=-1.0, scalar2=None, op0=ALU.mult), reads=["A16"], writes=["s5_ABt"])
                P.op("dve", lambda e: e.tensor_copy(out=ABt[:, 1, :], in_=self.A16[:, o, 1, :]), reads=["A16"], writes=["s5_ABt"])
                for c in range(128):
                    cur, nxt = XS[c % 2], XS[(c + 1) % 2]
                    ck, nk = f"s5_XS{c % 2}", f"s5_XS{(c + 1) % 2}"
                    sk = ("s5_Sc", c)
                    P.op("dve", lambda e, cur=cur: e.tensor_tensor(out=U[:], in0=cur[:], in1=AAt[:], op=ALU.mult), reads=[ck, "s5_AAt"], writes=["s5_U"])
                    P.op("dve", lambda e, cur=cur: e.tensor_tensor(out=V[:], in0=cur[:, ::-1, :], in1=ABt[:], op=ALU.mult), reads=[ck, "s5_ABt"], writes=["s5_V"])
                    P.op("dve", lambda e: e.tensor_add(out=U[:], in0=U[:], in1=V[:]), reads=["s5_U", "s5_V"], writes=["s5_U"])
                    P.op("dve", lambda e, c=c, nxt=nxt: e.tensor_tensor(out=nxt[:], in0=U[:], in1=SRI[:, :, :, c], op=ALU.add), reads=["s5_U", sk], writes=[nk])
                    P.op("act", lambda e, c=c, cur=cur: e.copy(out=SRI[:, :, :, c], in_=cur[:]), reads=[ck], writes=[sk])
                P.barrier()
                if self.stop == "C":
                    return
                Lt = [self.sb(st, f"s5_Lt{i}", [128, GB, 512], BF16) for i in range(2)]
                WXt = [self.sb(st, f"s5_WXt{i}", [128, GB, 1024], BF16) for i in range(2)]
                YS = [self.sb(st, f"s5_YS{i}", [128, 2, 128], F32) for i in range(2)]
                YF = self.sb(st, "s5_YF", [128, 16, 32], F32)
                G1 = self.sb(st, "s5_G1", [128, 16, 32], F32)
                for g in range(64):
                    gl, b = g % GB, g // GB
                    pb = b % 2
                    if gl == 0:
                        P.dma("sp", lambda e, b=b, pb=pb: e.dma_start(out=Lt[pb][:], in_=self.LS[o, b * GB:(b + 1) * GB].rearrange("g p f -> p g f")),
                              reads=[("LS", o)], writes=[f"s5_Lt{pb}"])
                        P.dma("sp", lambda e, b=b, pb=pb: e.dma_start(out=WXt[pb][:], in_=self.WXS[o, b * GB:(b + 1) * GB].rearrange("g p f -> p g f")),
                              reads=[("WXS", o)], writes=[f"s5_WXt{pb}"])
                    slot = g % 2
                    make_ug(g, slot)
                    bank = 2 + g % 2
                    for mh in range(2):
                        outp = ps[bank][:, mh * 128:(mh + 1) * 128]
                        ms = slice(mh * 128, (mh + 1) * 128)
                        mms = []
                        for kh in range(2):
                            mms.append((Lt[pb][:, gl, kh * 256 + mh * 128:kh * 256 + (mh + 1) * 128], UG[slot][:, kh, :], [f"s5_Lt{pb}", f"s5_UG{slot}"]))
                        for ri in range(2):
                            c0 = ri * 256 + mh * 128
                            mms.append((WXt[pb][:, gl, c0:c0 + 128], SRI[:, ri, g, :], [f"s5_WXt{pb}", ("s5_S", g, 0), ("s5_S", g, 1)]))
                        for ri in range(2):
                            c0 = 512 + ri * 256 + mh * 128
                            mms.append((WXt[pb][:, gl, c0:c0 + 128], SRI[:, ri, g, ::-1], [f"s5_WXt{pb}", ("s5_S", g, 0), ("s5_S", g, 1)]))
                        for n_, (lt, rh, rk) in enumerate(mms):
                            P.op("pe", lambda e, lt=lt, rh=rh, outp=outp, n_=n_: e.matmul(outp, lt, rh, start=(n_ == 0), stop=(n_ == len(mms) - 1)),
                                 reads=rk, writes=[("ps", bank)])
                    ys = YS[g % 2]
                    self.evac("act", ys[:], ps[bank][:, 0:256].rearrange("p (m c) -> p m c", m=2), [("ps", bank)], [f"s5_YS{g % 2}"])
                    tb = 4 + g % 2
                    for mh in range(2):
                        P.op("pe", lambda e, mh=mh, ys=ys, tb=tb: e.transpose(ps[tb][:, mh * 128:(mh + 1) * 128], ys[:, mh, :], identf[:]),
                             reads=[f"s5_YS{g % 2}", "s5_identf"], writes=[("ps", tb)])
                    g4 = g % 2
                    self.evac("dve", YF[:, :, g4 * 16:(g4 + 1) * 16], ps[tb][:, 0:256].rearrange("p (i c) -> p i c", c=16), [("ps", tb)], ["s5_YF"])
                    if g4 == 1:
                        f0 = (g // 2) * 32
                        P.op("dve", lambda e: e.tensor_mul(out=G1[:], in0=YF[:], in1=YF[:]), reads=["s5_YF"], writes=["s5_G1"])
                        P.op("dve", lambda e: e.tensor_scalar(out=G1[:], in0=G1[:], scalar1=0.044715, scalar2=1.0, op0=ALU.mult, op1=ALU.add),
                             reads=["s5_G1"], writes=["s5_G1"])
                        P.op("dve", lambda e: e.tensor_mul(out=G1[:], in0=G1[:], in1=YF[:]), reads=["s5_G1", "s5_YF"], writes=["s5_G1"])
                        P.op("act", lambda e: e.activation(out=G1[:], in_=G1[:], func=AF.Sigmoid, scale=1.5957691216057308), reads=["s5_G1"], writes=["s5_G1"])
                        P.op("dve", lambda e, f0=f0: e.tensor_mul(out=Z[:, :, f0:f0 + 32], in0=G1[:], in1=YF[:]),
                             reads=["s5_G1", "s5_YF"], writes=[zk(j) for j in range(16)])
                P.barrier()
            if self.stop == "D":
                return
            with contextlib.ExitStack() as st:
                Wg = load_w(st, "s5_Wg", self.w_glu[o], 1024)
                Wz = load_w(st, "s5_Wz", self.w_in_c[o][:, 1024:2048], 1024)
                bg = self.sb(st, "s5_bg", [128, D], F32)
                P.dma("sp", lambda e: e.dma_start(out=bg[:], in_=self.b_glu[o].partition_broadcast(128)), writes=["s5_bg"])
                hgT = [self.sb(st, f"s5_hgT{i}", [128, 8, 128], BF16) for i in range(2)]
                gl_t = [self.sb(st, f"s5_gl{i}", [128, 512], F32) for i in range(2)]
                sz_t = [self.sb(st, f"s5_sz{i}", [128, 512], F32) for i in range(2)]
                for j in range(16):
                    hk = f"s5_hgT{j % 2}"
                    self.transpose8(Z[:, j, :], hgT[j % 2], [zk(j)], hk, bank=j % 2)
                    for half in range(2):
                        hsl = slice(half * 512, (half + 1) * 512)
                        s2 = (j * 2 + half) % 2
                        bg_ = 2 + s2
                        bz_ = 4 + s2
                        for kt in range(8):
                            P.op("pe", lambda e, kt=kt, j=j, hsl=hsl, bg_=bg_: e.matmul(ps[bg_][:], hgT[j % 2][:, kt, :], Wg[:, kt, hsl], start=(kt == 0), stop=(kt == 7)),
                                 reads=[hk, "s5_Wg"], writes=[("ps", bg_)])
                        for kt in range(8):
                            P.op("pe", lambda e, kt=kt, j=j, hsl=hsl, bz_=bz_: e.matmul(ps[bz_][:], hT[:, kt, j::16], Wz[:, kt, hsl], start=(kt == 0), stop=(kt == 7)),
                                 reads=["hT", "s5_Wz"], writes=[("ps", bz_)])
                        glt, szt = gl_t[s2], sz_t[s2]
                        P.op("dve", lambda e, glt=glt, bg_=bg_, hsl=hsl: e.tensor_tensor(out=glt[:], in0=ps[bg_][:], in1=bg[:, hsl], op=ALU.add),
                             reads=[("ps", bg_), "s5_bg"], writes=[f"s5_gl{s2}"])
                        P.op("act", lambda e, glt=glt: e.activation(out=glt[:], in_=glt[:], func=AF.Sigmoid), reads=[f"s5_gl{s2}"], writes=[f"s5_gl{s2}"])
                        P.op("act", lambda e, szt=szt, bz_=bz_: e.activation(out=szt[:], in_=ps[bz_][:], func=AF.Silu), reads=[("ps", bz_)], writes=[f"s5_sz{s2}"])
                        P.op("dve", lambda e, glt=glt, szt=szt: e.tensor_mul(out=glt[:], in0=glt[:], in1=szt[:]), reads=[f"s5_gl{s2}", f"s5_sz{s2}"], writes=[f"s5_gl{s2}"])
                        P.op("dve", lambda e, glt=glt, j=j, hsl=hsl: e.tensor_mul(out=Z[:, j, hsl], in0=Z[:, j, hsl], in1=glt[:]),
                             reads=[f"s5_gl{s2}", zk(j), hk], writes=[zk(j)])
                P.barrier()
            if self.stop == "E":
                return
            with contextlib.ExitStack() as st:
                Wo = load_w(st, "s5_Wo", self.w_out_c[o], 1024)
                mT = [self.sb(st, f"s5_mT{i}", [128, 8, 128], BF16) for i in range(2)]
                for j in range(16):
                    mk = f"s5_mT{j % 2}"
                    self.transpose8(Z[:, j, :], mT[j % 2], [zk(j)], mk, bank=j % 2)
                    for half in range(2):
                        hsl = slice(half * 512, (half + 1) * 512)
                        bank = 2 + (j * 2 + half) % 4
                        for kt in range(8):
                            P.op("pe", lambda e, kt=kt, j=j, hsl=hsl, bank=bank: e.matmul(ps[bank][:], mT[j % 2][:, kt, :], Wo[:, kt, hsl], start=(kt == 0), stop=(kt == 7)),
                                 reads=[mk, "s5_Wo"], writes=[("ps", bank)])
                        P.op("dve", lambda e, j=j, hsl=hsl, bank=bank: e.tensor_tensor(out=x_sb[:, j, hsl], in0=x_sb[:, j, hsl], in1=ps[bank][:], op=ALU.add),
                             reads=[("ps", bank), ("x", j)], writes=[("x", j)])
                P.barrier()

    def evac(self, eng, out, in_, reads, writes):
        if eng == "act":
            self.P.op("act", lambda e: e.copy(out=out, in_=in_), reads=reads, writes=writes)
        else:
            self.P.op(eng, lambda e: e.tensor_copy(out=out, in_=in_), reads=reads, writes=writes)

    def transpose8(self, src, dst, rkeys, wkey, bank):
        P = self.P
        pv = self.ps[bank][:].bitcast(BF16)
        for kt in range(8):
            P.op("pe", lambda e, kt=kt: e.transpose(pv[:, kt * 128:(kt + 1) * 128], src[:, kt * 128:(kt + 1) * 128], self.identb[:]),
                 reads=list(rkeys) + ["identb"], writes=[("ps", bank)])
        self.evac("act", dst[:], pv.rearrange("p (k c) -> p k c", k=8), [("ps", bank)], [wkey])


def _att_bidx():
    import jax
    import jax.numpy as jnp
    k = np.arange(128)[:, None, None]
    d = np.arange(-1, 2)[None, :, None]
    q = np.arange(128)[None, None, :]
    rel = (k - q - 128 * d).astype(np.int32)
    half, max_exact = 16, 8
    try:
        cpu = jax.devices("cpu")[0]
        with jax.default_device(cpu):
            r = jnp.asarray(rel)
            n = jnp.abs(r)
            large = max_exact + (jnp.log(jnp.maximum(n, 1).astype(jnp.float32) / max_exact)
                                 / math.log(128 / max_exact) * (half - max_exact)).astype(jnp.int32)
            large = jnp.minimum(large, half - 1)
            out = jnp.where(r > 0, half, 0) + jnp.where(n < max_exact, n, large)
            return np.asarray(out).astype(np.float32)
    except Exception:
        n = np.abs(rel)
        large = max_exact + (np.log(np.maximum(n, 1).astype(np.float32) / np.float32(max_exact))
                             / np.float32(math.log(128 / max_exact)) * np.float32(half - max_exact)).astype(np.int32)
        large = np.minimum(large, half - 1)
        return (np.where(rel > 0, half, 0) + np.where(n < max_exact, n, large)).astype(np.float32)


class ABMixin:
    def ab_declare(self):
        for nm, shp in [("rel_bias", [32, 8]), ("w_in_ab", [2, 1024, IN_AB]), ("conv_w", [2, 5, 1280]), ("conv_b", [2, 1280]),
                        ("ssd_dt_bias", [2, 2, 16]), ("ssd_A_log", [2, 2, 16]), ("ssd_D", [2, 16]), ("ssd_norm_w", [2, 1024]),
                        ("diff_lambda", [2, 4, 64]), ("diff_subln_w", [2, 128]), ("w_out_ab", [2, 2048, 1024])]:
            setattr(self, nm, self.din(nm, shp))
        self.att_bidx = self.din("att_bidx", [128, 3, 128])
        self.ssd_declare()

    def ab_prologue(self):
        nc, P = self.nc, self.P
        st0 = self._st_small
        self.ssd_prologue()
        self.RB = self.sb(st0, "RB", [128, 256], F32)
        self.NBS = nc.dram_tensor("att_NBS", [128, 2 * 8 * 384], BF16, kind="Internal").ap()
        self.NEGLAM = self.sb(st0, "NEGLAM", [128, 2], F32)
        self.SLW = self.sb(st0, "SLW", [128, 2, 128], F32)
        P.dma("sp", lambda e: e.dma_start(out=self.RB[:], in_=self.rel_bias.rearrange("b h -> (b h)").partition_broadcast(128)), writes=["RB"])
        with contextlib.ExitStack() as st:
            sb = lambda n, s, d=F32: self.sb(st, f"abp_{n}", s, d)
            bidx = sb("bidx", [128, 384]); msk = sb("msk", [128, 384]); acc = sb("acc", [128, 8, 384]); t32 = sb("t32", [128, 8, 384])
            self.NB = sb("NBp", [128, 2, 8, 384], BF16)
            P.dma("sp", lambda e: e.dma_start(out=bidx[:], in_=self.att_bidx.rearrange("k d q -> k (d q)")), writes=["abp_bidx"])
            P.op("dve", lambda e: e.memset(acc[:], 0.0), writes=["abp_acc"])
            for b in range(32):
                P.op("dve", lambda e, b=b: e.tensor_scalar(out=msk[:], in0=bidx[:], scalar1=float(b), scalar2=None, op0=ALU.is_equal),
                     reads=["abp_bidx"], writes=["abp_msk"])
                for h in range(8):
                    P.op("dve", lambda e, b=b, h=h: e.scalar_tensor_tensor(out=acc[:, h, :], in0=msk[:], scalar=self.RB[:, b * 8 + h:b * 8 + h + 1],
                                                                            in1=acc[:, h, :], op0=ALU.mult, op1=ALU.add),
                         reads=["abp_msk", "RB", "abp_acc"], writes=["abp_acc"])
            P.op("dve", lambda e: e.tensor_scalar(out=acc[:], in0=acc[:], scalar1=8.0, scalar2=None, op0=ALU.mult), reads=["abp_acc"], writes=["abp_acc"])
            P.op("dve", lambda e: e.tensor_copy(out=self.NB[:, 0], in_=acc[:]), reads=["abp_acc"], writes=["NB"])
            P.op("dve", lambda e: e.tensor_copy(out=t32[:], in_=self.NB[:, 0]), reads=["NB"], writes=["abp_t32"])
            P.op("dve", lambda e: e.tensor_sub(out=t32[:], in0=acc[:], in1=t32[:]), reads=["abp_acc", "abp_t32"], writes=["abp_t32"])
            P.op("dve", lambda e: e.tensor_copy(out=self.NB[:, 1], in_=t32[:]), reads=["abp_t32"], writes=["NB"])
            P.dma("sp", lambda e: e.dma_start(out=self.NBS, in_=self.NB[:].rearrange("p a h f -> p (a h f)")), reads=["NB"], writes=["NBS"])
            dl = sb("dl", [128, 2, 4, 64]); pj = sb("pj", [128, 64]); pr = sb("pr", [128, 4])
            P.dma("sp", lambda e: e.dma_start(out=dl[:].rearrange("p e a d -> p (e a d)"),
                                              in_=self.diff_lambda.rearrange("e a d -> (e a d)").partition_broadcast(128)), writes=["abp_dl"])
            sw = sb("sw", [128, 2, 128])
            P.dma("sp", lambda e: e.dma_start(out=sw[:].rearrange("p e d -> p (e d)"),
                                              in_=self.diff_subln_w.rearrange("e d -> (e d)").partition_broadcast(128)), writes=["abp_sw"])
            for e_ in range(2):
                lam_init = 0.8 - 0.6 * math.exp(-0.3 * (2 * e_))
                for a in range(2):
                    P.op("dve", lambda e, e_=e_, a=a: e.scalar_tensor_tensor(out=pj[:], in0=dl[:, e_, 2 * a, :], scalar=1.0, in1=dl[:, e_, 2 * a + 1, :],
                                                                              op0=ALU.mult, op1=ALU.mult, accum_out=pr[:, e_ * 2 + a:e_ * 2 + a + 1]),
                         reads=["abp_dl"], writes=["abp_pj", "abp_pr"])
                P.op("act", lambda e, e_=e_: e.activation(out=pr[:, e_ * 2:e_ * 2 + 2], in_=pr[:, e_ * 2:e_ * 2 + 2], func=AF.Exp), reads=["abp_pr"], writes=["abp_pr"])
                P.op("dve", lambda e, e_=e_: e.tensor_sub(out=self.NEGLAM[:, e_:e_ + 1], in0=pr[:, e_ * 2 + 1:e_ * 2 + 2], in1=pr[:, e_ * 2:e_ * 2 + 1]),
                     reads=["abp_pr"], writes=["NEGLAM"])
                P.op("dve", lambda e, e_=e_, lam_init=lam_init: e.tensor_scalar(out=self.NEGLAM[:, e_:e_ + 1], in0=self.NEGLAM[:, e_:e_ + 1],
                                                                                  scalar1=-lam_init, scalar2=None, op0=ALU.add),
                     reads=["NEGLAM"], writes=["NEGLAM"])
                P.op("dve", lambda e, e_=e_, lam_init=lam_init: e.tensor_scalar(out=self.SLW[:, e_, :], in0=sw[:, e_, :], scalar1=1.0 - lam_init,
                                                                                  scalar2=None, op0=ALU.mult),
                     reads=["abp_sw"], writes=["SLW"])
            P.barrier()

    def ab_layer(self, e_, lnum):
        P = self.P
        with contextlib.ExitStack() as stL:
            yT = self.sb(stL, "ab_yT", [128, 8, L], BF16)
            if "nossd" not in self.stop:
                self.ssd_part(e_, yT)
                self.out_proj_half(e_, yT, 0, rstd=True)
            if "noatt" not in self.stop:
                self.att_part(e_, yT)
                self.out_proj_half(e_, yT, 1, rstd=False)

    def out_proj_half(self, e_, yT, half_idx, rstd):
        P = self.P
        ps, x_sb = self.ps, self.x_sb
        with contextlib.ExitStack() as st:
            Wo = self.sb(st, "ab_Wo", [128, 8, 1024], BF16)
            v = self.w_out_ab[e_][half_idx * 1024:(half_idx + 1) * 1024, :].rearrange("(kt p) f -> p kt f", p=128)
            for q in range(4):
                P.dma("pool", lambda e, q=q: e.dma_start(out=Wo[:, 2 * q:2 * q + 2, :], in_=v[:, 2 * q:2 * q + 2, :]), writes=["ab_Wo"])
            if rstd:
                for kt in range(8):
                    P.op("dve", lambda e, kt=kt: e.tensor_scalar(out=Wo[:, kt, :], in0=Wo[:, kt, :], scalar1=self.snw[:, e_, kt:kt + 1],
                                                                                       scalar2=None, op0=ALU.mult), reads=["ab_Wo", "snw"], writes=["ab_Wo"])
            for j in range(16):
                for hf in range(2):
                    hsl = slice(hf * 512, (hf + 1) * 512)
                    bank = (j * 2 + hf) % 4
                    for kt in range(8):
                        P.op("pe", lambda e, kt=kt, j=j, hsl=hsl, bank=bank: e.matmul(ps[bank][:], yT[:, kt, j::16], Wo[:, kt, hsl], start=(kt == 0), stop=(kt == 7)),
                             reads=["ab_yT", "ab_Wo"], writes=[("ps", bank)])
                    if rstd:
                        P.op("dve", lambda e, j=j, hsl=hsl, bank=bank: e.scalar_tensor_tensor(
                            out=x_sb[:, j, hsl], in0=ps[bank][:], scalar=self.rs_ssd[:, j:j + 1], in1=x_sb[:, j, hsl], op0=ALU.mult, op1=ALU.add),
                            reads=[("ps", bank), ("x", j), "rs_ssd"], writes=[("x", j)])
                    else:
                        P.op("dve", lambda e, j=j, hsl=hsl, bank=bank: e.tensor_tensor(out=x_sb[:, j, hsl], in0=x_sb[:, j, hsl], in1=ps[bank][:], op=ALU.add),
                             reads=[("ps", bank), ("x", j)], writes=[("x", j)])
            P.barrier()

    def att_part(self, e_, yT):
        P = self.P
        ps, hT, idb = self.ps, self.hT, self.identb
        c_q = 1024 + 1280 + 32
        with contextlib.ExitStack() as st:
            sb = lambda n, s, d=BF16: self.sb(st, f"at_{n}", s, d)
            W = [sb(f"W{i}", [128, 4, 8, 128]) for i in range(2)]
            qTc = [sb(f"qT{c}", [128, L]) for c in range(2)]
            kT = sb("kT", [128, L])
            Vaug = sb("Vaug", [128, 16, 130])
            SG = sb("SG", [128, 16, 128])
            PT = [sb(f"PT{i}", [128, 512]) for i in range(4)]
            accs = sb("accs", [128, 8, 129], F32)
            rr = sb("rr", [128, 8], F32)
            o4 = sb("o4", [128, 4, 128], F32)
            t4 = sb("t4", [128, 4, 128], F32)
            ssq = sb("ssq", [128, 4], F32)
            y4 = sb("y4", [128, 4, 128], BF16)
            NBt = sb("NB", [128, 2, 8, 384])
            P.dma("sp", lambda e: e.dma_start(out=NBt[:].rearrange("p a h f -> p (a h f)"), in_=self.NBS), reads=["NBS"], writes=["at_NB"])
            P.op("dve", lambda e: e.memset(qTc[0][:], 0.0), writes=["at_qT0"])
            P.op("dve", lambda e: e.memset(qTc[1][:], 0.0), writes=["at_qT1"])
            P.op("dve", lambda e: e.memset(Vaug[:], 1.0), writes=["at_Vaug"])

            def load_w(h):
                wt = W[h % 2]
                for s_ in range(4):
                    c0 = c_q + s_ * 1024 + h * 128
                    P.dma("pool", lambda e, s_=s_, c0=c0, wt=wt: e.dma_start(
                        out=wt[:, s_], in_=self.w_in_ab[e_][:, c0:c0 + 128].rearrange("(kt p) f -> p kt f", p=128)), writes=[f"at_W{h % 2}"])

            load_w(0)
            for h in range(8):
                wt = W[h % 2]
                wk = f"at_W{h % 2}"
                if h + 1 < 8:
                    load_w(h + 1)
                for s_ in range(2):
                    for qc in range(4):
                        pb_ = 7 if (s_ * 4 + qc) % 2 == 0 else 3
                        for kt in range(8):
                            P.op("pe", lambda e, s_=s_, qc=qc, kt=kt, wt=wt, pb_=pb_: e.matmul(ps[pb_][:], wt[:, s_, kt, :], hT[:, kt, qc * 512:(qc + 1) * 512],
                                                                                              start=(kt == 0), stop=(kt == 7)),
                                 reads=[wk, "hT"], writes=[("ps", pb_)])
                        csl = slice(qc * 512, (qc + 1) * 512)
                        if s_ == 0:
                            self.evac("dve", qTc[0][0:64, csl], ps[pb_][0:64, :], [("ps", pb_)], ["at_qT0"])
                            self.evac("act", qTc[1][64:128, csl], ps[pb_][64:128, :], [("ps", pb_)], ["at_qT1"])
                        else:
                            self.evac("dve", kT[:, csl], ps[pb_][:], [("ps", pb_)], ["at_kT"])
                for s_ in (2, 3):
                    for kq in range(4):
                        pb_ = 7 if kq % 2 == 0 else 3
                        for kl in range(4):
                            kb = kq * 4 + kl
                            for kt in range(8):
                                P.op("pe", lambda e, s_=s_, kb=kb, kl=kl, kt=kt, wt=wt, pb_=pb_: e.matmul(
                                    ps[pb_][:, kl * 128:(kl + 1) * 128], hT[:, kt, kb * 128:(kb + 1) * 128], wt[:, s_, kt, :], start=(kt == 0), stop=(kt == 7)),
                                    reads=[wk, "hT"], writes=[("ps", pb_)])
                        src = ps[pb_][:].rearrange("p (k f) -> p k f", k=4)
                        if s_ == 2:
                            self.evac("dve", Vaug[:, kq * 4:(kq + 1) * 4, 0:128], src, [("ps", pb_)], ["at_Vaug"])
                        else:
                            P.op("act", lambda e, kq=kq, src=src: e.activation(out=SG[:, kq * 4:(kq + 1) * 4, :], in_=src, func=AF.Silu),
                                 reads=[("ps", pb_)], writes=["at_SG"])
                iters = [(qc, kb, comp) for qc in range(4) for kb in range(16) for comp in range(2)]
                n_it = len(iters)
                PRE = 3

                def emit_S(it):
                    qc, kb, comp = iters[it]
                    qsl = slice(qc * 512, (qc + 1) * 512)
                    sbank = it % 4
                    near = [ql for ql in range(4) if abs(qc * 4 + ql - kb) <= 1]
                    P.op("pe", lambda e: e.matmul(ps[sbank][:], kT[:, kb * 128:(kb + 1) * 128], qTc[comp][:, qsl], start=True, stop=(len(near) == 0)),
                         reads=["at_kT", f"at_qT{comp}"], writes=[("ps", sbank)])
                    for ni, ql in enumerate(near):
                        d = qc * 4 + ql - kb
                        for hl in range(2):
                            last = (ni == len(near) - 1) and hl == 1
                            P.op("pe", lambda e, ql=ql, d=d, hl=hl, last=last: e.matmul(
                                ps[sbank][:, ql * 128:(ql + 1) * 128], idb[:], NBt[:, hl, h, (d + 1) * 128:(d + 2) * 128], start=False, stop=last),
                                reads=["identb", "at_NB"], writes=[("ps", sbank)])

                def emit_exp(it):
                    qc, kb, comp = iters[it]
                    sbank = it % 4
                    pt = PT[it % 4]
                    ptk = f"at_PT{it % 4}"
                    segs = []
                    for ql in range(4):
                        d = qc * 4 + ql - kb
                        ty = 0 if abs(d) <= 1 else (1 if d <= -2 else 2)
                        if segs and segs[-1][0] == ty:
                            segs[-1][2] = ql + 1
                        else:
                            segs.append([ty, ql, ql + 1])
                    for (ty, a_, b_) in segs:
                        csl = slice(a_ * 128, b_ * 128)
                        if ty == 0:
                            P.op("act", lambda e, csl=csl: e.activation(out=pt[:, csl], in_=ps[sbank][:, csl], func=AF.Exp, scale=0.125),
                                 reads=[("ps", sbank)], writes=[ptk])
                        else:
                            col = (31 if ty == 1 else 15) * 8 + h
                            P.op("act", lambda e, csl=csl, col=col: e.activation(
                                out=pt[:, csl], in_=ps[sbank][:, csl], func=AF.Exp, bias=self.RB[:, col:col + 1], scale=0.125),
                                reads=[("ps", sbank), "RB"], writes=[ptk])

                def emit_PV(it):
                    qc, kb, comp = iters[it]
                    pt = PT[it % 4]
                    ptk = f"at_PT{it % 4}"
                    for ql in range(4):
                        a_ = comp * 4 + ql
                        abank = 4 + a_ // 3
                        off = (a_ % 3) * 129
                        P.op("pe", lambda e, ql=ql, abank=abank, off=off: e.matmul(
                            ps[abank][:, off:off + 129], pt[:, ql * 128:(ql + 1) * 128], Vaug[:, kb, 0:129], start=(kb == 0 and off == 0), stop=(kb == 15)),
                            reads=[ptk, "at_Vaug"], writes=[("ps", abank)])

                def stage_A(qc):
                    for bk, (a0, a1) in enumerate([(0, 3), (3, 6), (6, 8)]):
                        P.op("dve", lambda e, bk=bk, a0=a0, a1=a1: e.tensor_copy(
                            out=accs[:, a0:a1, :], in_=ps[4 + bk][:, 0:(a1 - a0) * 129].rearrange("p (a f) -> p a f", f=129)),
                            reads=[("ps", 4 + bk)], writes=["at_accs"])
                    P.op("dve", lambda e: e.reciprocal(out=rr[:], in_=accs[:, :, 128]), reads=["at_accs"], writes=["at_rr"])
                    P.op("dve", lambda e: e.tensor_scalar(out=rr[:, 4:8], in0=rr[:, 4:8], scalar1=self.NEGLAM[:, e_:e_ + 1], scalar2=None, op0=ALU.mult),
                         reads=["at_rr", "NEGLAM"], writes=["at_rr"])
                    P.op("dve", lambda e: e.tensor_tensor(out=o4[:], in0=accs[:, 0:4, 0:128], in1=rr[:, 0:4].unsqueeze(2).to_broadcast([128, 4, 128]), op=ALU.mult),
                         reads=["at_accs", "at_rr"], writes=["at_o4"])
                    P.op("dve", lambda e: e.tensor_tensor(out=t4[:], in0=accs[:, 4:8, 0:128], in1=rr[:, 4:8].unsqueeze(2).to_broadcast([128, 4, 128]), op=ALU.mult),
                         reads=["at_accs", "at_rr"], writes=["at_t4"])
                    P.op("dve", lambda e: e.tensor_add(out=o4[:], in0=o4[:], in1=t4[:]), reads=["at_o4", "at_t4"], writes=["at_o4"])
                    P.op("dve", lambda e: e.tensor_mul(out=t4[:], in0=o4[:], in1=o4[:]), reads=["at_o4", "at_t4"], writes=["at_t4"])
                    P.op("dve", lambda e: e.tensor_reduce(out=ssq[:], in_=t4[:], axis=AX.X, op=ALU.add), reads=["at_t4"], writes=["at_ssq"])

                def stage_B(qc):
                    P.op("act", lambda e: e.activation(out=ssq[:], in_=ssq[:], func=AF.Ln, bias=self.epsc[:, 0:1], scale=1.0 / 128), reads=["at_ssq", "epsc"], writes=["at_ssq"])
                    P.op("act", lambda e: e.activation(out=ssq[:], in_=ssq[:], func=AF.Exp, scale=-0.5), reads=["at_ssq"], writes=["at_ssq"])

                def stage_C(qc):
                    P.op("dve", lambda e: e.tensor_tensor(out=o4[:], in0=o4[:], in1=ssq[:].unsqueeze(2).to_broadcast([128, 4, 128]), op=ALU.mult),
                         reads=["at_o4", "at_ssq"], writes=["at_o4"])
                    P.op("dve", lambda e: e.tensor_tensor(out=o4[:], in0=o4[:], in1=self.SLW[:, e_, :].unsqueeze(1).to_broadcast([128, 4, 128]), op=ALU.mult),
                         reads=["at_o4", "SLW"], writes=["at_o4"])
                    P.op("dve", lambda e: e.tensor_tensor(out=y4[:], in0=o4[:], in1=SG[:, qc * 4:(qc + 1) * 4, :], op=ALU.mult),
                         reads=["at_o4", "at_SG"], writes=["at_y4"])
                    pv = ps[7][:].bitcast(BF16)
                    for ql in range(4):
                        P.op("pe", lambda e, ql=ql, pv=pv: e.transpose(pv[:, ql * 128:(ql + 1) * 128], y4[:, ql, :], idb[:]), reads=["at_y4", "identb"], writes=[("ps", 7)])
                    self.evac("dve", yT[:, h, qc * 512:(qc + 1) * 512], pv[:, 0:512], [("ps", 7)], ["ab_yT"])

                deferred = {}
                for i0 in range(min(PRE, n_it)):
                    emit_S(i0)
                for it in range(n_it):
                    qc, kb, comp = iters[it]
                    if it + PRE < n_it:
                        emit_S(it + PRE)
                    emit_exp(it)
                    emit_PV(it)
                    for fn_ in deferred.pop(it, []):
                        fn_()
                    if kb == 15 and comp == 1:
                        stage_A(qc)
                        if qc < 3:
                            deferred.setdefault(it + 5, []).append(lambda qc=qc: stage_B(qc))
                            deferred.setdefault(it + 10, []).append(lambda qc=qc: stage_C(qc))
                        else:
                            stage_B(qc)
                            stage_C(qc)
            P.barrier()

def _ssd_masks():
    k = np.arange(128)[:, None]
    i = np.arange(128)[None, :]
    m = np.zeros((6, 128, 128), np.float32)
    m[0] = (k <= i)
    m[1] = (k > i)
    m[2] = (k >= i)
    m[3] = (k < i)
    m[4] = (i >= k)
    m[5] = (k >= i)
    return m


class SSDMixin:
    def ssd_declare(self):
        self.ssd_masks_d = self.din("ssd_masks", [6, 128, 128])

    def ssd_prologue(self):
        P = self.P
        st0 = self._st_small
        self.MK = self.sb(st0, "MK", [128, 6, 128], F32)
        self.MKb = self.sb(st0, "MKb", [128, 4, 128], BF16)
        self.onesf = self.sb(st0, "onesf", [128, 128], F32)
        self.onesb = self.sb(st0, "onesb", [128, 1], BF16)
        self.convw = self.sb(st0, "convw", [128, 2, 10, 5], F32)
        self.convb = self.sb(st0, "convb", [128, 2, 10], F32)
        self.dtb = self.sb(st0, "dtb", [128, 2, 32], F32)
        self.Aneg = self.sb(st0, "Aneg", [128, 2, 32], F32)
        self.Dsk = self.sb(st0, "Dsk", [128, 2, 16], F32)
        self.snw = self.sb(st0, "snw", [128, 2, 8], F32)
        self.rs_ssd = self.sb(st0, "rs_ssd", [128, 16], F32)
        P.dma("sp", lambda e: e.dma_start(out=self.MK[:], in_=self.ssd_masks_d.rearrange("m k i -> k m i")), writes=["MK"])
        P.dma("pool", lambda e: e.dma_start(out=self.MKb[:], in_=self.ssd_masks_d[0:4].rearrange("m k i -> k m i")), writes=["MKb"])
        P.op("dve", lambda e: e.memset(self.onesf[:], 1.0), writes=["onesf"])
        P.op("dve", lambda e: e.memset(self.onesb[:], 1.0), writes=["onesb"])
        for e_ in range(2):
            for k in range(5):
                P.dma("sp", lambda e, e_=e_, k=k: e.dma_start(out=self.convw[:, e_, :, k], in_=self.conv_w[e_, k].rearrange("(f p) -> p f", p=128),
                                                              allow_slow_non_contiguous=True), writes=["convw"])
            P.dma("sp", lambda e, e_=e_: e.dma_start(out=self.convb[:, e_], in_=self.conv_b[e_].rearrange("(f p) -> p f", p=128),
                                                     allow_slow_non_contiguous=True), writes=["convb"])
            P.dma("sp", lambda e, e_=e_: e.dma_start(out=self.snw[:, e_], in_=self.ssd_norm_w[e_].rearrange("(f p) -> p f", p=128),
                                                     allow_slow_non_contiguous=True), writes=["snw"])
        P.dma("sp", lambda e: e.dma_start(out=self.dtb[:].rearrange("p e h -> p (e h)"),
                                          in_=self.ssd_dt_bias.rearrange("e d h -> (e d h)").partition_broadcast(128)), writes=["dtb"])
        P.dma("sp", lambda e: e.dma_start(out=self.Aneg[:].rearrange("p e h -> p (e h)"),
                                          in_=self.ssd_A_log.rearrange("e d h -> (e d h)").partition_broadcast(128)), writes=["Aneg"])
        P.dma("sp", lambda e: e.dma_start(out=self.Dsk[:].rearrange("p e h -> p (e h)"),
                                          in_=self.ssd_D.rearrange("e h -> (e h)").partition_broadcast(128)), writes=["Dsk"])
        P.op("act", lambda e: e.activation(out=self.Aneg[:], in_=self.Aneg[:], func=AF.Exp), reads=["Aneg"], writes=["Aneg"])
        P.op("dve", lambda e: e.tensor_scalar(out=self.Aneg[:], in0=self.Aneg[:], scalar1=-1.0, scalar2=None, op0=ALU.mult), reads=["Aneg"], writes=["Aneg"])

    def conv_ft(self, e_, ft, Wt, wk, XC, dst_fn, post_fn=None):
        P = self.P
        ps, hT = self.ps, self.hT
        for qc in range(4):
            bank = 5 + qc % 2
            for kt in range(8):
                P.op("pe", lambda e, qc=qc, kt=kt, bank=bank: e.matmul(ps[bank][:], Wt[:, kt, :], hT[:, kt, qc * 512:(qc + 1) * 512], start=(kt == 0), stop=(kt == 7)),
                     reads=[wk, "hT"], writes=[("ps", bank)])
            self.evac("act" if qc % 2 == 0 else "dve", XC[:, 2 + qc * 512:2 + (qc + 1) * 512], ps[bank][:], [("ps", bank)], [("sd_XC", qc)])
        acc = self.sd_acc
        cw = self.convw[:, e_, ft, :]
        for qc in range(4):
            rk = [("sd_XC", q) for q in range(max(0, qc - 1), min(4, qc + 2))] + ["sd_XCh"]
            o = qc * 512
            P.op("dve", lambda e, o=o: e.tensor_scalar(out=acc[:], in0=XC[:, o:o + 512], scalar1=cw[:, 0:1], scalar2=self.convb[:, e_, ft:ft + 1], op0=ALU.mult, op1=ALU.add),
                 reads=rk + ["convw", "convb"], writes=["sd_acc"])
            for k in range(1, 5):
                P.op("dve", lambda e, k=k, o=o: e.scalar_tensor_tensor(out=acc[:], in0=XC[:, o + k:o + k + 512], scalar=cw[:, k:k + 1], in1=acc[:], op0=ALU.mult, op1=ALU.add),
                     reads=rk + ["convw", "sd_acc"], writes=["sd_acc"])
            dst, dkey = dst_fn(qc)
            P.op("act", lambda e, dst=dst: e.activation(out=dst, in_=acc[:], func=AF.Silu), reads=["sd_acc"], writes=[dkey])
            if post_fn is not None:
                post_fn(qc)

    NH = 4

    def ssd_part(self, e_, yT):
        P = self.P
        ps, hT, idb = self.ps, self.hT, self.identb
        c_x, c_dt = 1024, 2304
        MK, MKb = self.MK, self.MKb
        NH = self.NH
        NW = NH * 64
        NF = NW // 128
        NP = 16 // NH
        with contextlib.ExitStack() as st:
            sb = lambda n, s, d=BF16: self.sb(st, f"sd_{n}", s, d)
            CTf = sb("CTf", [128, L]); BTm = sb("BTm", [128, L])
            Wdt = sb("Wdt", [128, 8, 32])
            DT = sb("DT", [128, 16, 2 * NH], F32); ECS = sb("ECS", [128, 16, 2 * NH], F32); DDT = sb("DDT", [128, 16, 2 * NH], F32)
            CDX = sb("CDX", [128, 16, 2 * NH], F32); AT = sb("AT", [128, 16, 2 * NH], F32)
            AH = sb("AH", [128, 16, 2 * NH]); AL = sb("AL", [128, 16, 2 * NH]); t16 = sb("t16", [128, 16, 2 * NH], F32)
            XS = sb("XS", [128, 16, NW])
            Wz = sb("Wz", [128, 8, NW])
            Hst = sb("Hst", [128, NW], F32)
            Hbf_l = [sb(f"Hbf{i}", [128, NW]) for i in range(2)]; HPt_l = [sb(f"HPt{i}", [128, NW]) for i in range(2)]
            RH_l = [sb(f"RH{i}", [128, NH, 128]) for i in range(2)]; RL_l = [sb(f"RL{i}", [128, NH, 128]) for i in range(2)]
            Et_l = [sb(f"E{i}", [128, NH, 128]) for i in range(2)]; CBM_l = [sb(f"CBM{i}", [128, 128]) for i in range(2)]
            XDT_l = [sb(f"XDT{i}", [128, NH, 64]) for i in range(2)]; XDD_l = [sb(f"XDD{i}", [128, NH, 64]) for i in range(2)]
            Yacc_l = [sb(f"Yacc{i}", [128, NH, 64], F32) for i in range(2)]; tY_l = [sb(f"tY{i}", [128, NH, 64], F32) for i in range(2)]
            SZ4 = sb("SZ4", [128, 4, NW]); GT_l = [sb(f"GT{i}", [128, NW]) for i in range(2)]
            Btk_l = [sb(f"Btk{i}", [128, 128]) for i in range(2)]
            XC = sb("XC", [128, L + 4], F32)
            self.sd_acc = sb("acc", [128, 512], F32)
            Wt = [sb(f"Wt{i}", [128, 8, 128]) for i in range(2)]
            xtf = [sb(f"xtf{i}", [128, 512]) for i in range(2)]
            P.dma("pool", lambda e: e.dma_start(out=Wdt[:], in_=self.w_in_ab[e_][:, c_dt:c_dt + 32].rearrange("(kt p) f -> p kt f", p=128)), writes=["sd_Wdt"])
            P.op("dve", lambda e: e.memset(XC[:, 0:2], 0.0), writes=["sd_XCh"])
            P.op("dve", lambda e: e.memset(XC[:, L + 2:L + 4], 0.0), writes=["sd_XCh"])
            h3 = lambda T: T.rearrange("p (h q) -> p h q", h=NH)

            def load_wt(ft, slot):
                c0 = c_x + ft * 128
                P.dma("pool", lambda e: e.dma_start(out=Wt[slot][:], in_=self.w_in_ab[e_][:, c0:c0 + 128].rearrange("(kt p) f -> p kt f", p=128)),
                      writes=[f"sd_Wt{slot}"])

            load_wt(9, 0)
            self.conv_ft(e_, 9, Wt[0], "sd_Wt0", XC, lambda qc: (CTf[:, qc * 512:(qc + 1) * 512], "sd_CTf"))
            DTf = sb("DTf", [128, 16, 32], F32); ATf = sb("ATf", [128, 16, 32], F32); ECSf = sb("ECSf", [128, 16, 32], F32)
            DDTf = sb("DDTf", [128, 16, 32], F32); CDXf = sb("CDXf", [128, 16, 32], F32)
            for ci in range(16):
                for kt in range(8):
                    P.op("pe", lambda e, ci=ci, kt=kt: e.matmul(ps[0][:, ci * 32:(ci + 1) * 32], hT[:, kt, ci * 128:(ci + 1) * 128], Wdt[:, kt, :],
                                                                 start=(kt == 0), stop=(kt == 7)), reads=["hT", "sd_Wdt"], writes=[("ps", 0)])
            bfull = lambda G: G[:, e_, :].unsqueeze(1).to_broadcast([128, 16, 32])
            P.op("dve", lambda e: e.tensor_tensor(out=DTf[:], in0=ps[0][:].rearrange("p (c h) -> p c h", c=16), in1=bfull(self.dtb), op=ALU.add),
                 reads=[("ps", 0), "dtb"], writes=["sd_DTf"])
            P.op("act", lambda e: e.activation(out=DTf[:], in_=DTf[:], func=AF.Exp), reads=["sd_DTf"], writes=["sd_DTf"])
            P.op("act", lambda e: e.activation(out=DTf[:], in_=DTf[:], func=AF.Ln, bias=1.0), reads=["sd_DTf"], writes=["sd_DTf"])
            P.op("dve", lambda e: e.tensor_tensor(out=ATf[:], in0=DTf[:], in1=bfull(self.Aneg), op=ALU.mult), reads=["sd_DTf", "Aneg"], writes=["sd_ATf"])
            for ci in range(16):
                for d in range(2):
                    rhs = ATf[:, ci, d * 16:(d + 1) * 16]
                    osl = slice(ci * 32 + d * 16, ci * 32 + d * 16 + 16)
                    m_ecs = MK[:, 0 if d == 0 else 2, :]
                    m_dte = MK[:, 1 if d == 0 else 3, :]
                    P.op("pe", lambda e, rhs=rhs, osl=osl, m_ecs=m_ecs: e.matmul(ps[1][:, osl], m_ecs, rhs, start=True, stop=True),
                         reads=["MK", "sd_ATf"], writes=[("ps", 1)])
                    P.op("pe", lambda e, rhs=rhs, osl=osl, m_dte=m_dte: e.matmul(ps[2][:, osl], m_dte, rhs, start=True, stop=True),
                         reads=["MK", "sd_ATf"], writes=[("ps", 2)])
                    P.op("pe", lambda e, rhs=rhs, osl=osl: e.matmul(ps[3][:, osl], self.onesf[:], rhs, start=True, stop=True),
                         reads=["onesf", "sd_ATf"], writes=[("ps", 3)])
            flf = lambda T: T[:].rearrange("p c h -> p (c h)")
            P.op("act", lambda e: e.activation(out=flf(ECSf), in_=ps[1][:], func=AF.Exp), reads=[("ps", 1)], writes=["sd_ECSf"])
            P.op("act", lambda e: e.activation(out=flf(DDTf), in_=ps[2][:], func=AF.Exp), reads=[("ps", 2)], writes=["sd_DDTf"])
            P.op("act", lambda e: e.activation(out=flf(CDXf), in_=ps[3][:], func=AF.Exp), reads=[("ps", 3)], writes=["sd_CDXf"])
            P.op("dve", lambda e: e.tensor_mul(out=DDTf[:], in0=DDTf[:], in1=DTf[:]), reads=["sd_DDTf", "sd_DTf"], writes=["sd_DDTf"])
            for pi in range(NP):
                gi = (pi * NH) // 8
                h0 = pi * NH
                hsl = slice(h0, h0 + NH)
                ho = slice((1 - gi) * 64, (2 - gi) * 64)
                k0 = pi * NF
                if (pi * NH) % 8 == 0:
                    load_wt(8, 1)
                    self.conv_ft(e_, 8, Wt[1], "sd_Wt1", XC, lambda qc: (BTm[:, qc * 512:(qc + 1) * 512], "sd_BTm"))
                    P.op("dve", lambda e, ho=ho: e.memset(BTm[ho, :], 0.0), reads=[], writes=["sd_BTm"])
                for fl in range(NF):
                    ft = pi * NF + fl
                    load_wt(ft, fl % 2)

                    def dst_fn(qc):
                        return (xtf[qc % 2][:], f"sd_xtf{qc % 2}")

                    def post_fn(qc, fl=fl):
                        tb_ = 7 if qc % 2 == 0 else 4
                        pv = ps[tb_][:].bitcast(BF16)
                        for cl in range(4):
                            P.op("pe", lambda e, cl=cl, pv=pv: e.transpose(pv[:, cl * 128:(cl + 1) * 128], xtf[qc % 2][:, cl * 128:(cl + 1) * 128], idb[:]),
                                 reads=[f"sd_xtf{qc % 2}", "identb"], writes=[("ps", tb_)])
                        self.evac("dve", XS[:, qc * 4:(qc + 1) * 4, fl * 128:(fl + 1) * 128], pv[:, 0:512].rearrange("p (c f) -> p c f", c=4),
                                  [("ps", tb_)], ["sd_XS"])

                    self.conv_ft(e_, ft, Wt[fl % 2], f"sd_Wt{fl % 2}", XC, dst_fn, post_fn)
                for q in range(4):
                    P.dma("pool", lambda e, q=q, pi=pi: e.dma_start(
                        out=Wz[:, 2 * q:2 * q + 2, :], in_=self.w_in_ab[e_][:, pi * NW:(pi + 1) * NW].rearrange("(kt p) f -> p kt f", p=128)[:, 2 * q:2 * q + 2, :]),
                        writes=["sd_Wz"])
                v4 = lambda T: T[:].rearrange("p c (d h) -> p c d h", d=2)
                f4 = lambda T: T[:].rearrange("p c (d h) -> p c d h", d=2)[:, :, :, hsl]
                for (dst, src, dk, sk_) in ((DT, DTf, "sd_DT", "sd_DTf"), (AT, ATf, "sd_AT", "sd_ATf"), (ECS, ECSf, "sd_ECS", "sd_ECSf"),
                                            (DDT, DDTf, "sd_DDT", "sd_DDTf"), (CDX, CDXf, "sd_CDX", "sd_CDXf")):
                    P.op("dve", lambda e, dst=dst, src=src: e.tensor_copy(out=v4(dst), in_=f4(src)), reads=[sk_], writes=[dk])
                P.op("dve", lambda e: e.tensor_copy(out=AH[:], in_=AT[:]), reads=["sd_AT"], writes=["sd_AH"])
                P.op("dve", lambda e: e.tensor_copy(out=t16[:], in_=AH[:]), reads=["sd_AH"], writes=["sd_t16"])
                P.op("dve", lambda e: e.tensor_sub(out=t16[:], in0=AT[:], in1=t16[:]), reads=["sd_AT", "sd_t16"], writes=["sd_t16"])
                P.op("dve", lambda e: e.tensor_copy(out=AL[:], in_=t16[:]), reads=["sd_t16"], writes=["sd_AL"])

                def btok(ci):
                    pv = ps[7][:].bitcast(BF16)
                    Btk = Btk_l[ci % 2]
                    P.op("pe", lambda e: e.transpose(pv[:, 512:640], BTm[:, ci * 128:(ci + 1) * 128], idb[:]), reads=["sd_BTm", "identb"], writes=[("ps", 7)])
                    self.evac("act", Btk[:], pv[:, 512:640], [("ps", 7)], [f"sd_Btk{ci % 2}"])

                def xscale(dst, dkey, src_scale, ci, d):
                    P.op("dve", lambda e: e.tensor_tensor(out=dst[:], in0=h3(XS[:, ci, :]),
                                                           in1=src_scale[:, ci, d * NH:(d + 1) * NH].unsqueeze(2).to_broadcast([128, NH, 64]), op=ALU.mult),
                         reads=["sd_XS", "sd_DT", "sd_DDT"], writes=[dkey])

                def state_prep(ci, d, bank=5):
                    XDD = XDD_l[ci % 2]
                    Btk = Btk_l[ci % 2]
                    xscale(XDD, f"sd_XDD{ci % 2}", DDT, ci, d)
                    btok(ci)
                    P.op("pe", lambda e: e.matmul(ps[bank][:, 0:NW], Btk[:], XDD[:].rearrange("p h q -> p (h q)"), start=True, stop=True),
                         reads=[f"sd_Btk{ci % 2}", f"sd_XDD{ci % 2}"], writes=[("ps", bank)])

                def state_apply(ci, d, bank=5):
                    P.op("dve", lambda e: e.tensor_tensor(out=h3(Hst[:]), in0=h3(Hst[:]),
                                                          in1=CDX[:, ci, d * NH:(d + 1) * NH].unsqueeze(2).to_broadcast([128, NH, 64]), op=ALU.mult),
                         reads=["sd_Hst", "sd_CDX"], writes=["sd_Hst"])
                    P.op("dve", lambda e: e.tensor_tensor(out=Hst[:], in0=Hst[:], in1=ps[bank][:, 0:NW], op=ALU.add), reads=["sd_Hst", ("ps", bank)], writes=["sd_Hst"])

                hp_view = lambda ci: yT[:, k0:k0 + NF, ci * 128:(ci + 1) * 128]
                hkey = lambda ci: ("yTr", pi, ci)
                P.op("dve", lambda e: e.memset(Hst[:], 0.0), writes=["sd_Hst"])
                state_prep(15, 1, 5)
                for ci in range(15, -1, -1):
                    if ci - 1 > 0:
                        state_prep(ci - 1, 1, 5 + (ci % 2))
                    P.op("act", lambda e, ci=ci: e.copy(out=hp_view(ci), in_=Hst[:].rearrange("p (a b) -> p a b", a=NF)), reads=["sd_Hst"], writes=[hkey(ci)])
                    if ci > 0:
                        state_apply(ci, 1, 5 + ((ci + 1) % 2))
                P.op("dve", lambda e: e.memset(Hst[:], 0.0), writes=["sd_Hst"])
                dkeys = lambda d: (f"sd_RH{d}", f"sd_RL{d}", f"sd_E{d}", f"sd_CBM{d}", f"sd_XDT{d}", f"sd_tY{d}")

                def stage_A(ci):
                    for d in range(2):
                        RH, RL = RH_l[d], RL_l[d]
                        kRH, kRL = dkeys(d)[0:2]
                        mrow = MKb[:, 0 if d == 0 else 2, :]
                        for (R, A_, rk) in ((RH, AH, kRH), (RL, AL, kRL)):
                            P.op("dve", lambda e, R=R, A_=A_, d=d, mrow=mrow: e.tensor_tensor(
                                out=R[:], in0=A_[:, ci, d * NH:(d + 1) * NH].unsqueeze(2).to_broadcast([128, NH, 128]),
                                in1=mrow.unsqueeze(1).to_broadcast([128, NH, 128]), op=ALU.mult), reads=["sd_AH", "sd_AL", "MKb"], writes=[rk])

                def stage_Z(ci):
                    csl = slice(ci * 128, (ci + 1) * 128)
                    cp = ci % 2
                    P.op("pe", lambda e: e.matmul(ps[2][:, 0:128], BTm[:, csl], CTf[:, csl], start=True, stop=True), reads=["sd_BTm", "sd_CTf"], writes=[("ps", 2)])
                    if ci % 4 == 0:
                        for c4 in range(4):
                            csl4 = slice((ci + c4) * 128, (ci + c4 + 1) * 128)
                            zb = 6 if c4 % 2 == 0 else 7
                            for kt in range(8):
                                P.op("pe", lambda e, kt=kt, csl4=csl4, zb=zb: e.matmul(ps[zb][:, 0:NW], hT[:, kt, csl4], Wz[:, kt, :], start=(kt == 0), stop=(kt == 7)),
                                     reads=["hT", "sd_Wz"], writes=[("ps", zb)])
                            P.op("act", lambda e, c4=c4, zb=zb: e.activation(out=SZ4[:, c4, :], in_=ps[zb][:, 0:NW], func=AF.Silu), reads=[("ps", zb)], writes=[("sd_SZ", c4)])
                    P.op("act", lambda e: e.copy(out=Hbf_l[cp][:], in_=Hst[:]), reads=["sd_Hst"], writes=[f"sd_Hbf{cp}"])
                    P.op("act", lambda e: e.copy(out=HPt_l[cp][:].rearrange("p (a b) -> p a b", a=NF), in_=hp_view(ci)), reads=[hkey(ci)], writes=[f"sd_HPt{cp}"])

                def stage_B(ci):
                    for d in range(2):
                        RH, RL, Et = RH_l[d], RL_l[d], Et_l[d]
                        kRH, kRL, kE = dkeys(d)[0:3]
                        mlhs = MKb[:, 1 if d == 0 else 3, :]
                        P.op("pe", lambda e: e.matmul(ps[d][:, 0:NH * 128], mlhs, RH[:].rearrange("p h i -> p (h i)"), start=True, stop=False),
                             reads=["MKb", kRH], writes=[("ps", d)])
                        P.op("pe", lambda e: e.matmul(ps[d][:, 0:NH * 128], mlhs, RL[:].rearrange("p h i -> p (h i)"), start=False, stop=True),
                             reads=["MKb", kRL], writes=[("ps", d)])
                        P.op("act", lambda e: e.activation(out=Et[:].rearrange("p h i -> p (h i)"), in_=ps[d][:, 0:NH * 128], func=AF.Exp),
                             reads=[("ps", d)], writes=[kE])

                def stage_C(ci):
                    for d in range(2):
                        Et, CBM, XDT = Et_l[d], CBM_l[d], XDT_l[d]
                        kE, kCBM, kXDT = dkeys(d)[2:5]
                        xscale(XDT, kXDT, DT, ci, d)
                        P.op("dve", lambda e: e.tensor_tensor(out=CBM[:], in0=ps[2][:, 0:128], in1=MK[:, 4 + d, :], op=ALU.mult), reads=[("ps", 2), "MK"], writes=[kCBM])
                        P.op("dve", lambda e: e.tensor_tensor(out=Et[:], in0=Et[:], in1=CBM[:].unsqueeze(1).to_broadcast([128, NH, 128]), op=ALU.mult),
                             reads=[kE, kCBM], writes=[kE])

                def stage_D(ci):
                    csl = slice(ci * 128, (ci + 1) * 128)
                    cp = ci % 2
                    for d in range(2):
                        Et, XDT = Et_l[d], XDT_l[d]
                        kE, kXDT = dkeys(d)[2], dkeys(d)[4]
                        bk = 3 + d
                        for h in range(NH):
                            P.op("pe", lambda e, h=h: e.matmul(ps[bk][:, h * 64:(h + 1) * 64], Et[:, h, :], XDT[:, h, :], start=True, stop=True),
                                 reads=[kE, kXDT], writes=[("ps", bk)])
                        hsrc, hk = (Hbf_l[cp], f"sd_Hbf{cp}") if d == 0 else (HPt_l[cp], f"sd_HPt{cp}")
                        P.op("pe", lambda e: e.matmul(ps[bk][:, NW:2 * NW], CTf[:, csl], hsrc[:], start=True, stop=True), reads=["sd_CTf", hk], writes=[("ps", bk)])

                def stage_E(ci):
                    cp = ci % 2
                    Yacc, kYacc = Yacc_l[cp], f"sd_Yacc{cp}"
                    for d in range(2):
                        tY, ktY = tY_l[d], dkeys(d)[5]
                        bk = 3 + d
                        P.op("dve", lambda e: e.tensor_tensor(out=tY[:], in0=h3(ps[bk][:, NW:2 * NW]),
                                                              in1=ECS[:, ci, d * NH:(d + 1) * NH].unsqueeze(2).to_broadcast([128, NH, 64]), op=ALU.mult),
                             reads=[("ps", bk), "sd_ECS"], writes=[ktY])
                        if d == 0:
                            P.op("dve", lambda e: e.tensor_tensor(out=Yacc[:], in0=tY[:], in1=h3(ps[bk][:, 0:NW]), op=ALU.add),
                                 reads=[ktY, ("ps", bk)], writes=[kYacc])
                        else:
                            P.op("dve", lambda e: e.tensor_tensor(out=tY[:], in0=tY[:], in1=h3(ps[bk][:, 0:NW]), op=ALU.add),
                                 reads=[ktY, ("ps", bk)], writes=[ktY])
                            P.op("dve", lambda e: e.tensor_add(out=Yacc[:], in0=Yacc[:], in1=tY[:]), reads=[ktY, kYacc], writes=[kYacc])
                    tY = tY_l[0]
                    P.op("dve", lambda e: e.tensor_tensor(out=tY[:], in0=h3(XS[:, ci, :]),
                                                          in1=self.Dsk[:, e_, hsl].unsqueeze(2).to_broadcast([128, NH, 64]), op=ALU.mult),
                         reads=["sd_XS", "Dsk"], writes=["sd_tY0"])
                    P.op("dve", lambda e: e.tensor_add(out=Yacc[:], in0=Yacc[:], in1=tY[:]), reads=["sd_tY0", kYacc], writes=[kYacc])

                def stage_T(ci):
                    csl = slice(ci * 128, (ci + 1) * 128)
                    cp = ci % 2
                    Yacc, GT = Yacc_l[cp], GT_l[cp]
                    kYacc, kSZ, kGT, kHPt = f"sd_Yacc{cp}", ("sd_SZ", ci % 4), f"sd_GT{cp}", f"sd_HPt{cp}"
                    P.op("dve", lambda e: e.tensor_tensor(out=GT[:], in0=Yacc[:].rearrange("p h q -> p (h q)"), in1=SZ4[:, ci % 4, :], op=ALU.mult),
                         reads=[kYacc, kSZ], writes=[kGT])
                    pv = ps[7][:].bitcast(BF16)
                    for fl in range(NF):
                        P.op("pe", lambda e, fl=fl: e.transpose(pv[:, fl * 128:(fl + 1) * 128], GT[:, fl * 128:(fl + 1) * 128], idb[:]),
                             reads=[kGT, "identb"], writes=[("ps", 7)])
                    self.evac("act", yT[:, k0:k0 + NF, csl], pv[:, 0:NF * 128].rearrange("p (f t) -> p f t", f=NF), [("ps", 7), kHPt], [hkey(ci)])

                stage_A(0)
                for ci in range(16):
                    stage_Z(ci)
                    stage_B(ci)
                    if ci < 15:
                        state_prep(ci, 0, 5)
                    if ci + 1 < 16:
                        stage_A(ci + 1)
                    stage_C(ci)
                    stage_D(ci)
                    if ci < 15:
                        state_apply(ci, 0, 5)
                    stage_E(ci)
                    stage_T(ci)
            P.barrier()
        with contextlib.ExitStack() as st3:
            SQ = [self.sb(st3, f"sd_SQ{i}", [128, L], BF16) for i in range(2)]
            for kt in range(8):
                sq = SQ[kt % 2]
                P.op("dve", lambda e, kt=kt, sq=sq: e.tensor_tensor(out=sq[:], in0=yT[:, kt, :], in1=yT[:, kt, :], op=ALU.mult),
                     reads=["ab_yT"], writes=[f"sd_SQ{kt % 2}"])
                for j in range(16):
                    P.op("pe", lambda e, j=j, kt=kt, sq=sq: e.matmul(ps[0][:, j:j + 1], sq[:, j::16], self.onesb[:], start=(kt == 0 and j == 0), stop=(kt == 7)),
                         reads=[f"sd_SQ{kt % 2}", "onesb"], writes=[("ps", 0)])
            P.op("act", lambda e: e.activation(out=self.rs_ssd[:], in_=ps[0][:, 0:16], func=AF.Ln, bias=self.epsc[:, 0:1], scale=1.0 / 1024),
                 reads=[("ps", 0), "epsc"], writes=["rs_ssd"])
            P.op("act", lambda e: e.activation(out=self.rs_ssd[:], in_=self.rs_ssd[:], func=AF.Exp, scale=-0.5), reads=["rs_ssd"], writes=["rs_ssd"])
            P.barrier()


class Builder(S5Mixin, ABMixin, SSDMixin, BuilderBase):
    pass
```
